# Optimizing a Trainium2 kernel written in Bass

```python
import math
import jax
import jax.numpy as jnp
from jax import lax
import numpy as np

D_MODEL = 1024
BATCH = 8
SEQ = 2048
DEPTH = 4
DEC_BATCH = 8
DEC_SEQ = 8192
PAST_LEN = 128

N_MIXERS = 4
N_GROUPS = DEPTH // N_MIXERS
N_MOD = 6
NORM_EPS = 1e-6

HY_WIDTH = D_MODEL
HY_BANDS = 16
HY_EMB = 1 + 2 * HY_BANDS
HY_FILTER_ORDER = 64
HY_TARGET = 1e-2
HY_DECAY_SHORT = 0.3
HY_DECAY_LONG = 1.5

RET_HEADS = 4
RET_DK = D_MODEL // RET_HEADS
RET_DV = 2 * RET_DK
RET_CHUNK = 128
RET_SPLITS = [RET_HEADS * RET_DK, 2 * RET_HEADS * RET_DK, 2 * RET_HEADS * RET_DK + RET_HEADS * RET_DV]
RET_IN = 2 * RET_HEADS * RET_DK + 2 * RET_HEADS * RET_DV

SWA_HQ = 16
SWA_HKV = 4
SWA_GROUP = SWA_HQ // SWA_HKV
SWA_DH = 64
WINDOW = 128
ATTN_BLOCK = 128
ROPE_THETA = 10000.0
NEG_INF = -1e30
SWA_SPLITS = [SWA_HQ * SWA_DH, (SWA_HQ + SWA_HKV) * SWA_DH]
SWA_IN = (SWA_HQ + 2 * SWA_HKV) * SWA_DH

HG_HEADS = 8
HG_DK = 128
HG_DV = D_MODEL // HG_HEADS
HG_CHUNK = 64
HG_SPLITS = [HG_HEADS * HG_DK, HG_HEADS * (HG_DK + HG_DV), HG_HEADS * (2 * HG_DK + HG_DV), HG_HEADS * (3 * HG_DK + HG_DV)]
HG_IN = HG_HEADS * (3 * HG_DK + 2 * HG_DV)

D_FF = 2816

kernel_name = 'bidir_hybrid_encoder_two_groups'


def rms_norm(x, gain=None):
    xf = x.astype(jnp.float32)
    y = xf * lax.rsqrt(jnp.mean(xf * xf, axis=-1, keepdims=True) + NORM_EPS)
    if gain is not None:
        y = y * gain.astype(jnp.float32)
    return y.astype(x.dtype)


def dwconv3(x, w, b):
    xp = jnp.pad(x, ((0, 0), (1, 1), (0, 0)))
    return xp[:, :-2] * w[0] + xp[:, 1:-1] * w[1] + xp[:, 2:] * w[2] + b


def rope(x):
    L, dh = x.shape[1], x.shape[-1]
    inv = ROPE_THETA ** (-jnp.arange(0, dh, 2, dtype=jnp.float32) / dh)
    ang = jnp.arange(L, dtype=jnp.float32)[:, None] * inv[None, :]
    cos = jnp.cos(ang)[None, :, None, :]
    sin = jnp.sin(ang)[None, :, None, :]
    xf = x.astype(jnp.float32)
    x1, x2 = xf[..., : dh // 2], xf[..., dh // 2:]
    return jnp.concatenate([x1 * cos - x2 * sin, x1 * sin + x2 * cos], axis=-1).astype(x.dtype)


def chunk_recurrence(q, k, v, log_f, chunk):
    B, H, T, DK = q.shape
    DV = v.shape[-1]
    n = T // chunk
    q = q.astype(jnp.float32).reshape(B, H, n, chunk, DK)
    k = k.astype(jnp.float32).reshape(B, H, n, chunk, DK)
    v = v.astype(jnp.float32).reshape(B, H, n, chunk, DV)
    g = jnp.broadcast_to(log_f.astype(jnp.float32), (B, H, T, DK)).reshape(B, H, n, chunk, DK)
    b = jnp.cumsum(g, axis=3)
    b_last = b[:, :, :, -1:]
    q_dec = q * jnp.exp(b)
    k_inv = k * jnp.exp(-b)
    k_end = k * jnp.exp(b_last - b)
    causal = jnp.tril(jnp.ones((chunk, chunk), dtype=bool))
    scores = jnp.where(causal, jnp.einsum('bhntd,bhnsd->bhnts', q_dec, k_inv), 0.0)
    o_intra = jnp.einsum('bhnts,bhnse->bhnte', scores, v)

    def step(S, xs):
        qd, ke, vc, dl = xs
        o = jnp.einsum('bhtd,bhde->bhte', qd, S)
        S = dl[..., None] * S + jnp.einsum('bhtd,bhte->bhde', ke, vc)
        return S, o

    xs = (jnp.moveaxis(q_dec, 2, 0), jnp.moveaxis(k_end, 2, 0), jnp.moveaxis(v, 2, 0),
          jnp.moveaxis(jnp.exp(b_last[:, :, :, 0]), 2, 0))
    _, o_inter = lax.scan(step, jnp.zeros((B, H, DK, DV), jnp.float32), xs)
    return (o_intra + jnp.moveaxis(o_inter, 0, 2)).reshape(B, H, T, DV)


def bidir_recurrence(q, k_fwd, k_bwd, v, g_fwd, g_bwd, chunk):
    def flip(a):
        return jnp.flip(a, axis=2)
    fwd = chunk_recurrence(q, k_fwd, v, g_fwd, chunk)
    bwd = chunk_recurrence(flip(q), flip(k_bwd), flip(v), flip(g_bwd), chunk)
    return fwd + flip(bwd)


def hyena_filters(L, w1, b1, w2, b2, w3, freq, decay):
    t = jnp.linspace(0.0, 1.0, L, dtype=jnp.float32)[:, None]
    ang = (2.0 * math.pi / L) * jnp.arange(L, dtype=jnp.float32)[:, None]
    bands = jnp.linspace(1e-4, HY_BANDS - 1, HY_BANDS, dtype=jnp.float32)[None, :]
    z = jnp.concatenate([t, jnp.cos(bands * ang), -jnp.sin(bands * ang)], axis=-1)
    fr = freq.astype(jnp.float32)
    h = jnp.sin(fr * (z @ w1.astype(jnp.float32) + b1.astype(jnp.float32)))
    h = jnp.sin(fr * (h @ w2.astype(jnp.float32) + b2.astype(jnp.float32)))
    h = (h @ w3.astype(jnp.float32)).reshape(L, 2, HY_WIDTH)
    h = h * jnp.exp(-t[:, :, None] * jnp.abs(decay.astype(jnp.float32))[None])
    taps = jnp.concatenate([h[:, 0], jnp.zeros((1, HY_WIDTH), jnp.float32), h[:0:-1, 1]], axis=0)
    return taps / jnp.sum(jnp.abs(taps), axis=0, keepdims=True)


def long_conv(u, taps):
    L = u.shape[1]
    uf = jnp.fft.rfft(u.astype(jnp.float32), n=2 * L, axis=1)
    tf = jnp.fft.rfft(taps, axis=0)
    return jnp.fft.irfft(uf * tf[None], n=2 * L, axis=1)[:, :L]


def hyena_mixer(h, w_in, conv_w, conv_b, w1, b1, w2, b2, w3, freq, decay, skip, w_out):
    z = dwconv3(h @ w_in, conv_w, conv_b)
    x0, x1, v = jnp.split(z, 3, axis=-1)
    u = x1 * v
    taps = hyena_filters(h.shape[1], w1, b1, w2, b2, w3, freq, decay)
    y = long_conv(u, taps) + u.astype(jnp.float32) * skip.astype(jnp.float32)
    return (y.astype(h.dtype) * x0) @ w_out


def retention_mixer(h, w_in, decay_raw, w_out):
    B, L, _ = h.shape
    q, k, v, g = jnp.split(h @ w_in, RET_SPLITS, axis=-1)
    q = rope(q.reshape(B, L, RET_HEADS, RET_DK)).transpose(0, 2, 1, 3)
    k = (rope(k.reshape(B, L, RET_HEADS, RET_DK)) * (RET_DK ** -0.5)).transpose(0, 2, 1, 3)
    v = v.reshape(B, L, RET_HEADS, RET_DV).transpose(0, 2, 1, 3)
    log_gamma = -jnp.exp(decay_raw.astype(jnp.float32))
    o = bidir_recurrence(q, k, k, v, log_gamma[0].reshape(1, RET_HEADS, 1, 1),
                         log_gamma[1].reshape(1, RET_HEADS, 1, 1), RET_CHUNK)
    o = rms_norm(o.transpose(0, 2, 1, 3)).reshape(B, L, RET_HEADS * RET_DV).astype(h.dtype)
    return (o * jax.nn.silu(g)) @ w_out


def banded_window_attention(q, k, v, sink):
    B, L = q.shape[0], q.shape[1]
    nb = L // ATTN_BLOCK
    span = ATTN_BLOCK + 2 * WINDOW
    scale = SWA_DH ** -0.5
    kp = jnp.pad(k, ((0, 0), (WINDOW, WINDOW), (0, 0), (0, 0)))
    vp = jnp.pad(v, ((0, 0), (WINDOW, WINDOW), (0, 0), (0, 0)))
    qb = jnp.moveaxis(q.reshape(B, nb, ATTN_BLOCK, SWA_HKV, SWA_GROUP, SWA_DH), 1, 0)
    starts = jnp.arange(nb, dtype=jnp.int32) * ATTN_BLOCK
    rel = jnp.arange(ATTN_BLOCK)[:, None] - (jnp.arange(span)[None, :] - WINDOW)
    band = jnp.abs(rel) <= WINDOW
    sink_l = sink.astype(jnp.float32).reshape(1, SWA_HKV, SWA_GROUP, 1, 1)

    def block(args):
        qj, start = args
        kj = lax.dynamic_slice_in_dim(kp, start, span, axis=1)
        vj = lax.dynamic_slice_in_dim(vp, start, span, axis=1)
        kpos = start - WINDOW + jnp.arange(span)
        valid = band & ((kpos >= 0) & (kpos < L))[None, :]
        s = jnp.einsum('bqhgd,bkhd->bhgqk', qj.astype(jnp.float32), kj.astype(jnp.float32)) * scale
        s = jnp.where(valid, s, NEG_INF)
        sk = jnp.broadcast_to(sink_l, s.shape[:-1] + (1,))
        p = jax.nn.softmax(jnp.concatenate([s, sk], axis=-1), axis=-1)[..., :-1]
        return jnp.einsum('bhgqk,bkhd->bqhgd', p, vj.astype(jnp.float32))

    o = lax.map(block, (qb, starts))
    return jnp.moveaxis(o, 0, 1).reshape(B, L, SWA_HQ * SWA_DH)


def swa_mixer(h, w_in, q_gain, k_gain, sink, w_out):
    B, L, _ = h.shape
    q, k, v = jnp.split(h @ w_in, SWA_SPLITS, axis=-1)
    q = rope(rms_norm(q.reshape(B, L, SWA_HQ, SWA_DH), q_gain))
    k = rope(rms_norm(k.reshape(B, L, SWA_HKV, SWA_DH), k_gain))
    v = v.reshape(B, L, SWA_HKV, SWA_DH)
    o = banded_window_attention(q, k, v, sink).astype(h.dtype)
    return o @ w_out


def hgrn_mixer(h, w_in, lb_table, layer, gain, w_out):
    B, L, _ = h.shape
    q, i, ff, fb, gate = jnp.split(h @ w_in, HG_SPLITS, axis=-1)

    def heads(a):
        return a.reshape(B, L, HG_HEADS, -1).transpose(0, 2, 1, 3)

    sm = jax.nn.softmax(lb_table.astype(jnp.float32), axis=1)
    lb = (jnp.cumsum(sm, axis=1) - sm)[:, layer]
    f_f = lb[0] + (1.0 - lb[0]) * jax.nn.sigmoid(ff.astype(jnp.float32))
    f_b = lb[1] + (1.0 - lb[1]) * jax.nn.sigmoid(fb.astype(jnp.float32))
    o = bidir_recurrence(heads(jax.nn.silu(q)), heads(1.0 - f_f), heads(1.0 - f_b), heads(i),
                         heads(jnp.log(f_f)), heads(jnp.log(f_b)), HG_CHUNK)
    o = rms_norm(o.transpose(0, 2, 1, 3), gain).reshape(B, L, HG_HEADS * HG_DV).astype(h.dtype)
    return (o * jax.nn.silu(gate)) @ w_out


def conv_ffn(h, w_gate, w_val, conv_w, conv_b, w_down):
    a = dwconv3(h @ w_gate, conv_w, conv_b)
    return (jax.nn.silu(a) * (h @ w_val)) @ w_down


def setup_inputs(seed: int = 0) -> dict:
    key = jax.random.key(seed)
    keys = jax.random.split(key, 36)
    counter = iter(range(36))

    def nrm(shape, scale=1.0):
        return scale * jax.random.normal(keys[next(counter)], shape, jnp.float32)

    G, W, D = N_GROUPS, HY_WIDTH, D_MODEL
    hy_decay0 = jnp.linspace(math.log(HY_TARGET) / HY_DECAY_LONG, math.log(HY_TARGET) / HY_DECAY_SHORT, W, dtype=jnp.float32)
    ret_decay0 = jnp.log(-jnp.log(1.0 - 2.0 ** (-5.0 - jnp.arange(RET_HEADS, dtype=jnp.float32))))
    return {
        'x_prompt': nrm((BATCH, SEQ, D)),
        'x_sample': nrm((DEC_BATCH, DEC_SEQ, D)),
        'c_prompt': nrm((BATCH, D)),
        'c_sample': nrm((DEC_BATCH, D)),
        'ada_w': nrm((DEPTH, D, N_MOD * D), D ** -0.5),
        'ada_b': nrm((DEPTH, N_MOD * D), 0.02),
        'norm_g': 1.0 + nrm((DEPTH, 2, D), 0.02),
        'hy_w_in': nrm((G, D, 3 * W), D ** -0.5),
        'hy_conv_w': nrm((G, 3, 3 * W), 3 ** -0.5),
        'hy_conv_b': nrm((G, 3 * W), 0.02),
        'hy_w1': nrm((G, HY_EMB, HY_FILTER_ORDER), HY_EMB ** -0.5),
        'hy_b1': nrm((G, HY_FILTER_ORDER), 0.02),
        'hy_w2': nrm((G, HY_FILTER_ORDER, HY_FILTER_ORDER), HY_FILTER_ORDER ** -0.5),
        'hy_b2': nrm((G, HY_FILTER_ORDER), 0.02),
        'hy_w3': nrm((G, HY_FILTER_ORDER, 2 * W), HY_FILTER_ORDER ** -0.5),
        'hy_freq': 1.0 + nrm((G, HY_FILTER_ORDER), 0.01),
        'hy_decay': hy_decay0[None, None, :] + nrm((G, 2, W), 0.01),
        'hy_skip': nrm((G, W)),
        'hy_w_out': nrm((G, W, D), W ** -0.5),
        'ret_w_in': nrm((G, D, RET_IN), D ** -0.5),
        'ret_decay': ret_decay0[None, None, :] + nrm((G, 2, RET_HEADS), 0.01),
        'ret_w_out': nrm((G, RET_HEADS * RET_DV, D), (RET_HEADS * RET_DV) ** -0.5),
        'swa_w_in': nrm((G, D, SWA_IN), D ** -0.5),
        'swa_q_gain': 1.0 + nrm((G, SWA_DH), 0.02),
        'swa_k_gain': 1.0 + nrm((G, SWA_DH), 0.02),
        'swa_sink': nrm((G, SWA_HQ), 0.5),
        'swa_w_out': nrm((G, SWA_HQ * SWA_DH, D), (SWA_HQ * SWA_DH) ** -0.5),
        'hg_w_in': nrm((G, D, HG_IN), D ** -0.5),
        'hg_lb': nrm((2, DEPTH, HG_HEADS * HG_DK), 0.1),
        'hg_gain': 1.0 + nrm((G, HG_DV), 0.02),
        'hg_w_out': nrm((G, HG_HEADS * HG_DV, D), (HG_HEADS * HG_DV) ** -0.5),
        'ffn_w_gate': nrm((DEPTH, D, D_FF), D ** -0.5),
        'ffn_w_val': nrm((DEPTH, D, D_FF), D ** -0.5),
        'ffn_conv_w': nrm((DEPTH, 3, D_FF), 3 ** -0.5),
        'ffn_conv_b': nrm((DEPTH, D_FF), 0.02),
        'ffn_w_down': nrm((DEPTH, D_FF, D), D_FF ** -0.5),
    }


def reference(x_prompt, x_sample, c_prompt, c_sample, ada_w, ada_b, norm_g,
              hy_w_in, hy_conv_w, hy_conv_b, hy_w1, hy_b1, hy_w2, hy_b2, hy_w3, hy_freq, hy_decay, hy_skip, hy_w_out,
              ret_w_in, ret_decay, ret_w_out,
              swa_w_in, swa_q_gain, swa_k_gain, swa_sink, swa_w_out,
              hg_w_in, hg_lb, hg_gain, hg_w_out,
              ffn_w_gate, ffn_w_val, ffn_conv_w, ffn_conv_b, ffn_w_down):

    def trunk(x, c):
        cs = jax.nn.silu(c)
        for layer in range(DEPTH):
            kind, j = layer % N_MIXERS, layer // N_MIXERS
            mod = (cs @ ada_w[layer] + ada_b[layer])[:, None, :]
            sh1, sc1, g1, sh2, sc2, g2 = jnp.split(mod, N_MOD, axis=-1)
            h = rms_norm(x, norm_g[layer, 0]) * (1.0 + sc1) + sh1
            if kind == 0:
                m = hyena_mixer(h, hy_w_in[j], hy_conv_w[j], hy_conv_b[j], hy_w1[j], hy_b1[j], hy_w2[j], hy_b2[j],
                                hy_w3[j], hy_freq[j], hy_decay[j], hy_skip[j], hy_w_out[j])
            elif kind == 1:
                m = retention_mixer(h, ret_w_in[j], ret_decay[j], ret_w_out[j])
            elif kind == 2:
                m = swa_mixer(h, swa_w_in[j], swa_q_gain[j], swa_k_gain[j], swa_sink[j], swa_w_out[j])
            else:
                m = hgrn_mixer(h, hg_w_in[j], hg_lb, layer, hg_gain[j], hg_w_out[j])
            x = x + g1 * m
            h = rms_norm(x, norm_g[layer, 1]) * (1.0 + sc2) + sh2
            x = x + g2 * conv_ffn(h, ffn_w_gate[layer], ffn_w_val[layer], ffn_conv_w[layer], ffn_conv_b[layer], ffn_w_down[layer])
        return x

    y_prompt = trunk(x_prompt, c_prompt)
    y_sample = trunk(x_sample, c_sample)
    return (y_prompt, y_sample)
```

```python
import numpy as np
import ml_dtypes
import concourse.bass as bass
import concourse.mybir as mybir
from concourse.bass_utils import run_bass_kernel_spmd
from concourse.alu_op_type import AluOpType as ALU
from concourse.ap import AP

AF = mybir.ActivationFunctionType
F32 = mybir.dt.float32
BF16 = mybir.dt.bfloat16
I32 = mybir.dt.int32
AX = mybir.AxisListType

SEM_CAP = 30000


class Ev:
    __slots__ = ("sem", "val", "key", "eng")

    def __init__(self, sem, val, key, eng=""):
        self.sem, self.val, self.key, self.eng = sem, val, key, eng


class Buf:
    __slots__ = ("name", "w", "r", "dsem", "dkey", "dcnt")

    def __init__(self, name=""):
        self.name = name
        self.w = None
        self.r = {}
        self.dsem = None


class Eng:
    def __init__(self, P, name):
        self.P, self.name = P, name
        self.ops = []
        self.seen = {}
        self.sem = None
        self.cnt = 0
        self.last = None
        self.pending = False
        self.pend = []

    def flush_waits(self, keep_last=False):
        last = None
        if keep_last and self.pend:
            last = self.pend.pop()
        for sem, val in self.pend:
            self.ops.append(lambda be, sem=sem, val=val: be.wait_ge(sem, val))
        self.pend = []
        return last

    def _ensure(self):
        if self.sem is None or self.cnt >= SEM_CAP:
            assert not self.pending
            self.sem, self.key = self.P.new_sem(self.name)
            self.cnt = 0

    def need(self, ev):
        if ev is None:
            return
        if self.name == "pe" and ev.eng == "pe":
            return
        if self.seen.get(ev.key, 0) >= ev.val:
            return
        self.seen[ev.key] = ev.val
        self.pend.append((ev.sem, ev.val))


class _Rec:
    def __init__(self):
        self.call = None

    def __getattr__(self, name):
        def f(*a, **k):
            self.call = (name, a, k)
            return self
        return f


EMBED_WAITS = True


def _replay(call, wait=None):
    name, a, k = call
    if wait is None:
        return lambda be: getattr(be, name)(*a, **k)
    sem, val = wait
    return lambda be: getattr(be, name)(*a, **k)._wait_ge(sem, val)


class Prog:
    def __init__(self, nc):
        self.nc = nc
        self.eng = {n: Eng(self, n) for n in ("pe", "act", "dve", "pool", "sp")}
        self.nsem = 0
        self.dma_pool = []
        self.semval = {}
        self.live_dma = {}
        self.bufs = []

    def new_sem(self, tag):
        s = self.nc.alloc_semaphore(f"s{self.nsem}_{tag}")
        self.nsem += 1
        key = self.nsem
        self.semval[key] = 0
        return s, key

    def buf(self, name=""):
        b = Buf(name)
        self.bufs.append(b)
        return b

    def _dma_event(self, b):
        if b.dsem is None or self.semval[b.dkey] + 16 > SEM_CAP:
            got = None
            for i, (s, k) in enumerate(self.dma_pool):
                if self.semval[k] + 16 <= SEM_CAP:
                    got = self.dma_pool.pop(i)
                    break
            if got is None:
                got = self.new_sem("d")
            b.dsem, b.dkey = got
        self.semval[b.dkey] += 16
        ev = Ev(b.dsem, self.semval[b.dkey], b.dkey)
        self.live_dma[b.dkey] = ev
        return ev

    def _deps(self, E, reads, writes):
        for b in reads:
            E.need(b.w)
        for b in writes:
            E.need(b.w)
            for ev in b.r.values():
                E.need(ev)

    def op(self, en, fn, reads=(), writes=(), signal=True):
        E = self.eng[en]
        E._ensure()
        self._deps(E, reads, writes)
        w = E.flush_waits(keep_last=EMBED_WAITS)
        rec = _Rec()
        fn(rec)
        fn = _replay(rec.call, w)
        ev = Ev(E.sem, E.cnt + 1, E.key, en)
        if signal:
            sem = E.sem
            E.ops.append(lambda be: fn(be).then_inc(sem, 1))
            E.cnt += 1
            E.last = ev
            E.pending = False
        else:
            E.ops.append(fn)
            E.pending = True
        for b in reads:
            b.r[en] = ev
        for b in writes:
            b.w = ev
            b.r = {}
        return ev

    def dma(self, q, out, in_, reads=(), writes=(), **kw):
        E = self.eng[q]
        self._deps(E, reads, writes)
        E.flush_waits()
        b0 = (list(writes) + list(reads))[0]
        ev = self._dma_event(b0)
        sem = ev.sem
        E.ops.append(lambda be: be.dma_start(out=out, in_=in_, **kw).then_inc(sem, 16))
        for b in reads:
            b.r[("d", ev.key)] = ev
        for b in writes:
            b.w = ev
            b.r = {}
        return ev

    def barrier(self):
        evs = []
        for E in self.eng.values():
            assert not E.pending, E.name
            if E.last is not None:
                evs.append(E.last)
        evs.extend(self.live_dma.values())
        for E in self.eng.values():
            for ev in evs:
                E.need(ev)
            E.flush_waits()
        self.live_dma = {}
        for b in self.bufs:
            b.w = None
            b.r = {}
            if b.dsem is not None:
                self.dma_pool.append((b.dsem, b.dkey))
                b.dsem = None
        self.bufs = []

    def flush(self, final=False):
        nc = self.nc
        self.barrier()
        if not final and getattr(self, "defer_emit", False):
            return
        engs = self.eng
        with nc.Block() as block:
            @block.tensor
            def _(be):
                for f in self.eng["pe"].ops:
                    f(be)

            @block.scalar
            def _(be):
                for f in self.eng["act"].ops:
                    f(be)

            @block.vector
            def _(be):
                for f in self.eng["dve"].ops:
                    f(be)

            @block.gpsimd
            def _(be):
                for f in self.eng["pool"].ops:
                    f(be)

            @block.sync
            def _(be):
                for f in self.eng["sp"].ops:
                    f(be)

        for E in self.eng.values():
            E.ops = []
from contextlib import ExitStack
import math

D = 1024
DFF = 2816
EPS = 1e-6


class KB:
    def __init__(self, Ls, dbg=()):
        self.Ls = Ls
        self.dbg = set(dbg)
        nc = self.nc = bass.Bass("TRN2", target_bir_lowering=False)
        self.P = Prog(nc)
        self.din = {}
        self.outs = []
        ps = nc.alloc_psum_tensor("ps", [128, 4096], F32)
        psb = ps.bitcast(BF16)
        self.ps = [ps[:, b * 512:(b + 1) * 512] for b in range(8)]
        self.psb = [psb[:, b * 1024:(b + 1) * 1024] for b in range(8)]
        self.pbuf = [Buf(f"ps{b}") for b in range(8)]
        self.pb_i = 0
        self.uid = 0

    def inp(self, name, shape, dt=F32):
        t = self.nc.dram_tensor(name, list(shape), dt, kind="ExternalInput")
        self.din[name] = t
        return t.ap()

    def scratch(self, name, shape, dt):
        if not hasattr(self, "scr"):
            self.scr = {}
        if name in self.scr:
            return self.scr[name]
        self.scr[name] = self._scratch(name, shape, dt)
        return self.scr[name]

    def _scratch(self, name, shape, dt):
        kind = "ExternalOutput" if name in self.dbg else "Internal"
        t = self.nc.dram_tensor(name, list(shape), dt, kind=kind)
        if name in self.dbg:
            self.outs.append(name)
        return t.ap()

    def out(self, name, shape, dt=F32):
        t = self.nc.dram_tensor(name, list(shape), dt, kind="ExternalOutput")
        self.outs.append(name)
        return t.ap()

    def sb(self, es, shape, dt, name=None):
        self.uid += 1
        n = f"{name or 't'}_{self.uid}"
        if es is None:
            return self.nc.alloc_sbuf_tensor(n, list(shape), dt)
        return es.enter_context(self.nc.sbuf_tensor(n, list(shape), dt))

    def bank(self):
        res = getattr(self, "reserved", ())
        busy = getattr(self, "busy", ())
        for _ in range(9):
            b = self.pb_i
            self.pb_i = (b + 1) % 8
            if b not in res and b not in busy:
                return b
        raise RuntimeError("no free PSUM bank")

    def acq(self):
        if not hasattr(self, "busy"):
            self.busy = set()
        b = self.bank()
        self.busy.add(b)
        return b

    def rel(self, b):
        self.busy.discard(b)


def interleave(gens):
    gens = list(gens)
    while gens:
        nxt = []
        for g in gens:
            try:
                next(g)
                nxt.append(g)
            except StopIteration:
                pass
        gens = nxt


def load_col(P, dst, vec, b, np_=128):
    P.dma("sp", dst, vec.rearrange("(c p) -> p c", p=np_), writes=[b], allow_slow_non_contiguous=True)


def setup_consts(kb):
    nc, P = kb.nc, kb.P
    ident = kb.inp("ident", [128, 128])
    kb.identf = kb.sb(None, [128, 128], F32, "identf")
    kb.identb = kb.sb(None, [128, 128], BF16, "identb")
    kb.b_ident = Buf()
    P.dma("sp", kb.identf[:], ident, writes=[kb.b_ident])
    P.dma("pool", kb.identb[:], ident, writes=[kb.b_ident])
    kb.epsc = kb.sb(None, [128, 1], F32, "epsc")
    P.op("dve", lambda e: e.memset(kb.epsc[:], EPS), writes=[kb.b_ident])
    P.flush()


def phase_mod(kb, W):
    nc, P = kb.nc, kb.P
    nl = 4
    kb.modA = kb.sb(None, [128, nl, 2, 8, 2], F32, "modA")
    kb.modS = kb.sb(None, [128, nl, 2, 8, 2], F32, "modS")
    kb.gt = kb.scratch("gt", [nl, 2, 2, 128, 1024], F32)
    with ExitStack() as es:
        ccol = kb.sb(es, [128, 8, 2], F32)
        cs = kb.sb(es, [128, 8, 2], F32)
        csb = kb.sb(es, [128, 8, 2], BF16)
        csrep = kb.sb(es, [128, 8, 2, 128], BF16)
        bcol = kb.sb(es, [128, nl, 48], F32)
        gcol = kb.sb(es, [128, nl, 2, 8], F32)
        b_c = P.buf()
        for s in range(2):
            load_col(P, ccol[:, :, s], W["c"][s], b_c)
        for l in range(nl):
            load_col(P, bcol[:, l, :], W["ada_b"][l], b_c)
            for j in range(2):
                load_col(P, gcol[:, l, j, :], W["norm_g"][l, j], b_c)
        b_cs = P.buf()
        P.op("act", lambda e: e.activation(out=cs[:], in_=ccol[:], func=AF.Silu), reads=[b_c], writes=[b_cs])
        b_csb = P.buf()
        P.op("dve", lambda e: e.tensor_copy(out=csb[:], in_=cs[:]), reads=[b_cs], writes=[b_csb])
        P.op("dve", lambda e: e.tensor_copy(out=csrep[:], in_=AP(cs, 0, [[16, 128], [2, 8], [1, 2], [0, 128]])),
             reads=[b_cs], writes=[b_csb])
        wbs = [(kb.sb(es, [128, 8, 512], BF16), P.buf()) for _ in range(2)]
        bts = [(kb.sb(es, [128, 512], F32), P.buf()) for _ in range(2)]
        gts = [(kb.sb(es, [128, 512], F32), P.buf()) for _ in range(2)]
        b_mod = P.buf()
        it = 0
        for l in range(nl):
            for nb in range(12):
                wb, b_wb = wbs[it % 2]
                it += 1
                src = W["ada_w"][l, :, nb * 512:(nb + 1) * 512].rearrange("(k p) n -> p k n", p=128)
                P.dma("pool", wb[:], src, writes=[b_wb])
                blk = nb // 2
                if blk in (2, 5):
                    j = 0 if blk == 2 else 1
                    bt, b_bt = bts[nb % 2]
                    brow = W["ada_b"][l:l + 1, nb * 512:(nb + 1) * 512]
                    P.dma("sp", bt[:], AP(brow.tensor, brow.offset, [[0, 128], [1, 512]]), writes=[b_bt])
                    for s in range(2):
                        bk = kb.bank()
                        for k in range(8):
                            P.op("pe", lambda e, k=k, s=s, bk=bk, wb=wb: e.matmul(
                                kb.ps[bk], lhsT=csrep[:, k, s, :], rhs=wb[:, k, :], start=(k == 0), stop=(k == 7)),
                                reads=[b_csb, b_wb], writes=[kb.pbuf[bk]], signal=(k == 7))
                        g, b_g = gts[s]
                        P.op("dve", lambda e, bk=bk, g=g, bt=bt: e.tensor_tensor(out=g[:], in0=kb.ps[bk], in1=bt[:], op=ALU.add),
                             reads=[kb.pbuf[bk], b_bt], writes=[b_g])
                        c0 = (nb % 2) * 512
                        P.dma("sp", kb.gt[l, s, j, :, c0:c0 + 512], g[:], reads=[b_g])
                else:
                    j = 0 if blk < 2 else 1
                    is_sc = blk in (1, 4)
                    dst = kb.modA if is_sc else kb.modS
                    for q in range(4):
                        kc = (nb % 2) * 4 + q
                        bk = kb.bank()
                        for k in range(8):
                            P.op("pe", lambda e, k=k, q=q, bk=bk, wb=wb: e.matmul(
                                kb.ps[bk][:, 0:2], lhsT=wb[:, k, q * 128:(q + 1) * 128], rhs=csb[:, k, :],
                                start=(k == 0), stop=(k == 7)),
                                reads=[b_csb, b_wb], writes=[kb.pbuf[bk]], signal=(k == 7))
                        cidx = nb * 4 + q
                        P.op("dve", lambda e, bk=bk, dst=dst, l=l, j=j, kc=kc, cidx=cidx: e.tensor_scalar(
                            out=dst[:, l, j, kc, :], in0=kb.ps[bk][:, 0:2], scalar1=bcol[:, l, cidx:cidx + 1],
                            scalar2=None, op0=ALU.add), reads=[kb.pbuf[bk], b_c], writes=[b_mod])
        for l in range(nl):
            for j in range(2):
                for s in range(2):
                    P.op("dve", lambda e, l=l, j=j, s=s: e.scalar_tensor_tensor(
                        out=kb.modA[:, l, j, :, s], in0=kb.modA[:, l, j, :, s], scalar=1.0, in1=gcol[:, l, j, :],
                        op0=ALU.add, op1=ALU.mult), reads=[b_mod, b_c], writes=[b_mod])
        P.flush()


def phase_norm(kb, X, hT, L, l, j, s):
    nc, P = kb.nc, kb.P
    nt = L // 512
    with ExitStack() as es:
        NS = 3
        sets = []
        for i in range(NS):
            sets.append(dict(
                xt=kb.sb(es, [128, 4, 1024], F32), b_xt=P.buf(),
                xn=kb.sb(es, [128, 4, 1024], BF16), b_xn=P.buf(),
                ht=kb.sb(es, [128, 8, 512], BF16), b_ht=P.buf(),
                ss=kb.sb(es, [128, 4], F32), b_ss=P.buf(),
                rs=kb.sb(es, [128, 4], F32), b_rs=P.buf(),
            ))
        jk = kb.sb(es, [128, 1024], BF16)
        b_jk = P.buf()

        def stL(ti):
            S = sets[ti % NS]
            t0 = ti * 512
            P.dma("sp", S["xt"][:], X[t0:t0 + 512, :].rearrange("(a p) d -> p a d", p=128), writes=[S["b_xt"]])

        def stA(ti):
            S = sets[ti % NS]
            t0 = ti * 512
            xt, xn, ss, rs = S["xt"], S["xn"], S["ss"], S["rs"]
            for a in range(4):
                P.op("act", lambda e: e.activation(out=jk[:], in_=xt[:, a, :], func=AF.Square, accum_out=ss[:, a:a + 1]),
                     reads=[S["b_xt"]], writes=[b_jk, S["b_ss"]])
            P.op("act", lambda e: e.activation(out=rs[:], in_=ss[:], func=AF.Sqrt, scale=1.0 / D, bias=kb.epsc[:, 0:1]),
                 reads=[S["b_ss"]], writes=[S["b_rs"]])
            P.op("dve", lambda e: e.reciprocal(out=rs[:], in_=rs[:]), reads=[S["b_rs"]], writes=[S["b_rs"]])
            for a in range(4):
                eng = "dve" if a % 2 == 0 else "pool"
                P.op(eng, lambda e: e.tensor_scalar(out=xn[:, a, :], in0=xt[:, a, :], scalar1=rs[:, a:a + 1], scalar2=0.0,
                                                    op0=ALU.mult, op1=ALU.add), reads=[S["b_xt"], S["b_rs"]], writes=[S["b_xn"]])

        def stB(ti):
            S = sets[ti % NS]
            xn = S["xn"]
            bk0 = (ti % 2) * 4
            for a in range(4):
                bk = bk0 + a
                for k in range(8):
                    P.op("pe", lambda e: e.transpose(out=kb.psb[bk][:, k * 128:(k + 1) * 128], in_=xn[:, a, k * 128:(k + 1) * 128],
                                                     identity=kb.identb[:]),
                         reads=[S["b_xn"], kb.b_ident], writes=[kb.pbuf[bk]], signal=(k == 7))

        def stC(ti):
            S = sets[ti % NS]
            ht = S["ht"]
            t0 = ti * 512
            bk0 = (ti % 2) * 4
            pbs = [kb.pbuf[bk0 + a] for a in range(4)]
            for k in range(8):
                src = AP(kb.psb[bk0].tensor, kb.psb[bk0].offset + k * 128, [[8192, 128], [1024, 4], [1, 128]])
                dst = ht[:, k, :].rearrange("p (a t) -> p a t", a=4)
                Aap = kb.modA[:, l, j, k, s:s + 1]
                Sap = kb.modS[:, l, j, k, s:s + 1]
                if k % 2 == 0:
                    P.op("act", lambda e: e.activation(out=dst, in_=src, func=AF.Identity, scale=Aap, bias=Sap), reads=pbs, writes=[S["b_ht"]])
                else:
                    P.op("dve", lambda e: e.tensor_scalar(out=dst, in0=src, scalar1=Aap, scalar2=Sap, op0=ALU.mult, op1=ALU.add),
                         reads=pbs, writes=[S["b_ht"]])
            P.dma("sp", hT[:, t0:t0 + 512].rearrange("(k p) t -> p k t", p=128), ht[:], reads=[S["b_ht"]])

        stL(0)
        if nt > 1:
            stL(1)
        for step in range(nt + 2):
            if step + 2 < nt:
                stL(step + 2)
            if step < nt:
                stA(step)
            if 0 <= step - 1 < nt:
                stB(step - 1)
            if 0 <= step - 2 < nt:
                stC(step - 2)
        P.flush()


def phase_lf(kb, actT, K, L, groups, epi, halo=0, Lseg=2048, nrow=2, setup=None, depth=2, evac="act", cast="act"):
    nc, P = kb.nc, kb.P
    KC = K // 128
    Lseg = min(Lseg, L)
    with ExitStack() as es:
        W = Lseg + 2 * halo
        act = kb.sb(es, [128, KC, W], BF16)
        b_act = P.buf()
        NW = (depth + 1) * nrow
        wfs = [(kb.sb(es, [128, KC, 128], F32), P.buf()) for _ in range(NW)]
        wts = [(kb.sb(es, [128, KC, 128], BF16), P.buf()) for _ in range(NW)]
        rowsets = [[(kb.sb(es, [128, W], F32), P.buf()) for _ in range(nrow)] for _ in range(2)]
        ctx = setup(es) if setup else None
        work = [(t0, gi) for t0 in range(0, L, Lseg) for gi in range(len(groups))]
        wslot = {}
        wi = [0]

        def fetch(idx):
            if idx >= len(work):
                return
            _, gi = work[idx]
            slots = []
            for spec in groups[gi]:
                Wap, col0 = spec[0], spec[1]
                sl = wi[0] % NW
                wi[0] += 1
                wf, b_wf = wfs[sl]
                wt, b_wt = wts[sl]
                P.dma("sp", wf[:], Wap[:, col0:col0 + 128].rearrange("(k p) n -> p k n", p=128), writes=[b_wf])
                if cast == "act":
                    P.op("act", lambda e: e.activation(out=wt[:], in_=wf[:], func=AF.Copy), reads=[b_wf], writes=[b_wt])
                else:
                    P.op("pool", lambda e: e.tensor_copy(out=wt[:], in_=wf[:]), reads=[b_wf], writes=[b_wt])
                slots.append(sl)
            wslot[idx] = slots

        for dd in range(depth):
            fetch(dd)
        ev = 0
        for idx, (t0, gi) in enumerate(work):
            Ls = Lseg
            if gi == 0:
                lo = t0 - halo
                hi = t0 + Ls + halo
                clo, chi = max(lo, 0), min(hi, L)
                if halo:
                    if lo < 0:
                        P.op("dve", lambda e: e.memset(act[:, :, 0:1], 0.0), writes=[b_act])
                    if hi > L:
                        P.op("dve", lambda e: e.memset(act[:, :, W - 1:W], 0.0), writes=[b_act])
                P.dma("sp", act[:, :, clo - lo:chi - lo], actT[:, clo:chi].rearrange("(k p) t -> p k t", p=128),
                      writes=[b_act])
            fetch(idx + depth)
            grp = groups[gi]
            rows = rowsets[idx % 2][:len(grp)]
            for ci, spec in enumerate(grp):
                esc = spec[2] if len(spec) > 2 else 1.0
                wt, b_wt = wts[wslot[idx][ci]]
                row, b_row = rows[ci]
                for c0 in range(0, W, 512):
                    c1 = min(c0 + 512, W)
                    bk = kb.bank()
                    for k in range(KC):
                        P.op("pe", lambda e: e.matmul(kb.ps[bk][:, 0:c1 - c0], lhsT=wt[:, k, :], rhs=act[:, k, c0:c1],
                                                      start=(k == 0), stop=(k == KC - 1)),
                             reads=[b_wt, b_act], writes=[kb.pbuf[bk]], signal=(k == KC - 1))
                    ev += 1
                    if (evac == "act" or ev % 2 == 0) and esc == 1.0:
                        P.op("act", lambda e: e.activation(out=row[:, c0:c1], in_=kb.ps[bk][:, 0:c1 - c0], func=AF.Copy),
                             reads=[kb.pbuf[bk]], writes=[b_row])
                    else:
                        P.op("dve", lambda e: e.tensor_scalar(out=row[:, c0:c1], in0=kb.ps[bk][:, 0:c1 - c0], scalar1=esc, scalar2=None,
                                                              op0=ALU.mult), reads=[kb.pbuf[bk]], writes=[b_row])
            del wslot[idx]
            epi(kb, ctx, gi, rows, t0, Ls)
        P.flush()


def phase_lt(kb, actT, K, L, Wap, N, epi, setup=None, tile_pre=None, tile_post=None, epi_group=None):
    nc, P = kb.nc, kb.P
    KC = K // 128
    with ExitStack() as es:
        w = kb.sb(es, [128, KC, N], BF16)
        b_w = P.buf()
        wst = [(kb.sb(es, [128, 1024], F32), P.buf()) for _ in range(3)]
        ii = 0
        for k in range(KC):
            for n0 in range(0, N, 1024):
                n1 = min(N, n0 + 1024)
                wf, b_wf = wst[ii % 3]
                P.dma("sp", wf[:, 0:n1 - n0], Wap[k * 128:(k + 1) * 128, n0:n1], writes=[b_wf])
                eng = ("act", "dve")[ii % 2]
                if eng == "act":
                    P.op("act", lambda e: e.activation(out=w[:, k, n0:n1], in_=wf[:, 0:n1 - n0], func=AF.Copy), reads=[b_wf], writes=[b_w])
                else:
                    P.op(eng, lambda e: e.tensor_copy(out=w[:, k, n0:n1], in_=wf[:, 0:n1 - n0]), reads=[b_wf], writes=[b_w])
                ii += 1
        acts = [(kb.sb(es, [128, KC, 512], BF16), P.buf()) for _ in range(2)]
        ctx = setup(es) if setup else None
        def loads(ti):
            t0 = ti * 512
            act, b_act = acts[ti % 2]
            P.dma("sp", act[:], actT[:, t0:t0 + 512].rearrange("(k p) t -> p k t", p=128), writes=[b_act])
            if tile_pre:
                tile_pre(kb, ctx, ti)

        loads(0)
        for ti in range(L // 512):
            act, b_act = acts[ti % 2]
            if ti + 1 < L // 512:
                loads(ti + 1)
            for a in range(4):
                bks = []
                for nb in range(N // 512):
                    bk = kb.bank()
                    bks.append(bk)
                    for k in range(KC):
                        P.op("pe", lambda e, k=k, bk=bk, act=act, a=a, nb=nb: e.matmul(
                            kb.ps[bk], lhsT=act[:, k, a * 128:(a + 1) * 128], rhs=w[:, k, nb * 512:(nb + 1) * 512],
                            start=(k == 0), stop=(k == KC - 1)),
                            reads=[b_act, b_w], writes=[kb.pbuf[bk]], signal=(k == KC - 1))
                    if epi_group is None:
                        epi(kb, ctx, ti, a, nb, bk)
                if epi_group is not None:
                    epi_group(kb, ctx, ti, a, bks)
            if tile_post:
                tile_post(kb, ctx, ti)
        P.flush()


def lt_residual(kb, actT, K, L, Wap, Xsrc, Xdst, l, j, s):
    P = kb.P

    def setup(es):
        c = dict()
        c["g"] = kb.sb(es, [128, 1024], F32)
        c["b_g"] = P.buf()
        P.dma("sp", c["g"][:], kb.gt[l, s, j], writes=[c["b_g"]])
        c["x"] = [(kb.sb(es, [128, 4, 1024], F32), P.buf()) for _ in range(2)]
        c["tmp"] = [(kb.sb(es, [128, 512], F32), P.buf()) for _ in range(2)]
        c["n"] = 0
        return c

    def pre(kb, c, ti):
        x, b_x = c["x"][ti % 2]
        P.dma("sp", x[:], Xsrc[ti * 512:(ti + 1) * 512, :].rearrange("(a p) d -> p a d", p=128), writes=[b_x])

    def epi(kb, c, ti, a, nb, bk):
        x, b_x = c["x"][ti % 2]
        tmp, b_tmp = c["tmp"][c["n"] % 2]
        c["n"] += 1
        g = c["g"]
        P.op("dve", lambda e: e.tensor_tensor(out=tmp[:], in0=kb.ps[bk], in1=g[:, nb * 512:(nb + 1) * 512], op=ALU.mult),
             reads=[kb.pbuf[bk], c["b_g"]], writes=[b_tmp])
        P.op("pool", lambda e: e.tensor_tensor(out=x[:, a, nb * 512:(nb + 1) * 512], in0=x[:, a, nb * 512:(nb + 1) * 512],
                                               in1=tmp[:], op=ALU.add), reads=[b_tmp, b_x], writes=[b_x])

    def post(kb, c, ti):
        x, b_x = c["x"][ti % 2]
        P.dma("sp", Xdst[ti * 512:(ti + 1) * 512, :].rearrange("(a p) d -> p a d", p=128), x[:], reads=[b_x])

    phase_lt(kb, actT, K, L, Wap, 1024, epi, setup=setup, tile_pre=pre, tile_post=post)


def phase_ffn(kb, W, hT, uT, L, l):
    P = kb.P
    wg, wv = W["ffn_w_gate"][l], W["ffn_w_val"][l]
    groups = [[(wg, jj * 128), (wv, jj * 128)] for jj in range(22)]

    def setup(es):
        c = dict()
        c["cw"] = kb.sb(es, [128, 4, 22], F32)
        c["b_cw"] = P.buf()
        for jj in range(3):
            load_col(P, c["cw"][:, jj, :], W["ffn_conv_w"][l, jj], c["b_cw"])
        load_col(P, c["cw"][:, 3, :], W["ffn_conv_b"][l], c["b_cw"])
        c["r"] = [(kb.sb(es, [128, 2048], F32), P.buf()) for _ in range(2)]
        c["u"] = [(kb.sb(es, [128, 2048], BF16), P.buf()) for _ in range(2)]
        return c

    def epi(kb, c, gi, rows, t0, Ls):
        (g, b_g), (v, b_v) = rows
        r, b_r = c["r"][gi % 2]
        u, b_u = c["u"][gi % 2]
        cw = c["cw"]
        P.op("act", lambda e: e.activation(out=r[:, 0:Ls], in_=g[:, 1:Ls + 1], func=AF.Identity,
                                           scale=cw[:, 1, gi:gi + 1], bias=cw[:, 3, gi:gi + 1]),
             reads=[b_g, c["b_cw"]], writes=[b_r])
        P.op("dve", lambda e: e.scalar_tensor_tensor(out=r[:, 0:Ls], in0=g[:, 0:Ls], scalar=cw[:, 0, gi:gi + 1],
                                                     in1=r[:, 0:Ls], op0=ALU.mult, op1=ALU.add),
             reads=[b_g, c["b_cw"], b_r], writes=[b_r])
        P.op("dve", lambda e: e.scalar_tensor_tensor(out=r[:, 0:Ls], in0=g[:, 2:Ls + 2], scalar=cw[:, 2, gi:gi + 1],
                                                     in1=r[:, 0:Ls], op0=ALU.mult, op1=ALU.add),
             reads=[b_g, c["b_cw"], b_r], writes=[b_r])
        P.op("act", lambda e: e.activation(out=r[:, 0:Ls], in_=r[:, 0:Ls], func=AF.Silu), reads=[b_r], writes=[b_r])
        P.op("pool", lambda e: e.tensor_tensor(out=u[:, 0:Ls], in0=r[:, 0:Ls], in1=v[:, 1:Ls + 1], op=ALU.mult),
             reads=[b_r, b_v], writes=[b_u])
        P.dma("pool", uT[gi * 128:(gi + 1) * 128, t0:t0 + Ls], u[:, 0:Ls], reads=[b_u])

    phase_lf(kb, hT, 1024, L, groups, epi, halo=1, setup=setup)


WSPEC = [
    ("ada_w", [4, 1024, 6144]), ("ada_b", [4, 6144]), ("norm_g", [4, 2, 1024]),
    ("hy_w_in", [1, 1024, 3072]), ("hy_conv_w", [1, 3, 3072]), ("hy_conv_b", [1, 3072]),
    ("hy_w1", [1, 33, 64]), ("hy_b1", [1, 64]), ("hy_w2", [1, 64, 64]), ("hy_b2", [1, 64]),
    ("hy_w3", [1, 64, 2048]), ("hy_freq", [1, 64]), ("hy_decay", [1, 2, 1024]), ("hy_skip", [1, 1024]),
    ("hy_w_out", [1, 1024, 1024]),
    ("ret_w_in", [1, 1024, 6144]), ("ret_decay", [1, 2, 4]), ("ret_w_out", [1, 2048, 1024]),
    ("swa_w_in", [1, 1024, 1536]), ("swa_q_gain", [1, 64]), ("swa_k_gain", [1, 64]), ("swa_sink", [1, 16]),
    ("swa_w_out", [1, 1024, 1024]),
    ("hg_w_in", [1, 1024, 5120]), ("hg_lb", [2, 4, 1024]), ("hg_gain", [1, 128]), ("hg_w_out", [1, 1024, 1024]),
    ("ffn_w_gate", [4, 1024, 2816]), ("ffn_w_val", [4, 1024, 2816]), ("ffn_conv_w", [4, 3, 2816]),
    ("ffn_conv_b", [4, 2816]), ("ffn_w_down", [4, 2816, 1024]),
]


def build(Ls, mixers=(0, 1, 2, 3), layers=(0, 1, 2, 3), dbg=()):
    kb = KB(Ls, dbg)
    P = kb.P
    W = {n: kb.inp(n, shp) for n, shp in WSPEC}
    W["c"] = kb.inp("c", [2, 1024])
    X = [kb.inp(f"x{s}", [Ls[s], 1024]) for s in range(2)]
    Y = [kb.out(f"y{s}", [Ls[s], 1024]) for s in range(2)]
    Lmax = max(Ls)
    kb.C = {n: kb.inp(n, shp) for n, shp in CSPEC(Lmax)}
    if 0 in mixers and 0 in layers:
        for L in sorted(set(Ls)):
            for n, shp, dt in fft_cspec(L):
                kb.C[n] = kb.inp(n, shp, dt)
    kb.use_hy = (0 in mixers and 0 in layers)
    setup_consts(kb)
    P.defer_emit = DEFER_EMIT
    phase_mod(kb, W)
    hT = kb.scratch("hT", [1024, Lmax], BF16)
    uT = kb.scratch("uT", [DFF, Lmax], BF16)
    for s in range(2):
        L = Ls[s]
        cur = X[s]
        for l in layers:
            if l in mixers:
                phase_norm(kb, cur, hT[:, 0:L], L, l, 0, s)
                MIXERS[l](kb, W, hT[:, 0:L], cur, Y[s], L, l, s)
                cur = Y[s]
            phase_norm(kb, cur, hT[:, 0:L], L, l, 1, s)
            phase_ffn(kb, W, hT[:, 0:L], uT[:, 0:L], L, l)
            lt_residual(kb, uT[:, 0:L], DFF, L, W["ffn_w_down"][l], cur, Y[s], l, 1, s)
            cur = Y[s]
    P.flush(final=True)
    return kb


MIXERS = {}
DEFER_EMIT = True


def host_consts(Lmax):
    c = {"ident": np.eye(128, dtype=np.float32)}
    t = np.arange(Lmax, dtype=np.float32)[:, None]
    inv = (10000.0 ** (-np.arange(0, 64, 2, dtype=np.float32) / 64)).astype(np.float32)
    ang = (t * inv[None, :]).astype(np.float32)
    c["swa_cos"] = np.cos(ang).astype(np.float32)
    c["swa_sin"] = np.sin(ang).astype(np.float32)
    kk = np.arange(128)[:, None]
    qq = np.arange(128)[None, :]
    c["maskL"] = (kk >= qq).astype(np.float32)
    c["maskR"] = (kk <= qq).astype(np.float32)
    inv = (10000.0 ** (-np.arange(0, 256, 2, dtype=np.float32) / 256)).astype(np.float32)
    ang = (inv[:, None] * np.arange(Lmax, dtype=np.float32)[None, :]).astype(np.float32)
    c["ret_cos"] = np.cos(ang).astype(np.float32)
    c["ret_sin"] = np.sin(ang).astype(np.float32)
    tabs = np.zeros((6, 128, 128), np.float32)
    ss_ = np.arange(128)[:, None]
    tt_ = np.arange(128)[None, :]
    tabs[0] = np.maximum(tt_ - ss_, 0)
    tabs[1] = np.maximum(ss_ - tt_, 0)
    tabs[2] = 1.0 + np.eye(128)
    tabs[3] = np.broadcast_to(np.arange(128)[None, :] + 1.0, (128, 128))
    tabs[4] = np.broadcast_to(128.0 - np.arange(128)[None, :], (128, 128))
    tabs[5, :, 0] = 127.0 - np.arange(128)
    tabs[5, :, 1] = np.arange(128)
    tabs[5, :, 2] = 128.0
    c["ret_tabs"] = tabs
    hm = np.zeros((128, 256), np.float32)
    same = (ss_ // 64) == (tt_ // 64)
    hm[:, 0:128] = same & (ss_ <= tt_)
    hm[:, 128:256] = same & (ss_ >= tt_)
    c["hg_mask"] = hm
    return c


def fft_tables(L):
    import ml_dtypes
    bf = ml_dtypes.bfloat16
    N1 = 2 * L // 128
    N = 2 * L
    c = {}
    n1 = np.arange(N1 // 2)[:, None].astype(np.float64)
    k1 = np.arange(N1)[None, :].astype(np.float64)
    th = 2 * np.pi * n1 * k1 / N1
    c[f"F1_{N1}"] = np.stack([np.cos(th), -np.sin(th)], axis=1).astype(np.float32).astype(bf)
    k1v = np.arange(N1)[:, None, None].astype(np.float64)
    n2 = np.arange(128)[None, :, None].astype(np.float64)
    k2 = np.arange(128)[None, None, :].astype(np.float64)
    ph = 2 * np.pi * ((n2 * (k1v + N1 * k2)) % N) / N
    Gr, Gi = np.cos(ph), -np.sin(ph)
    c[f"G_{N1}"] = np.stack([Gr, Gi, -Gi, -Gr], axis=2).astype(np.float32).astype(bf)
    Gr_t, Gi_t = Gr.transpose(0, 2, 1), Gi.transpose(0, 2, 1)
    c[f"CT_{N1}"] = np.stack([Gr_t, Gi_t, -Gi_t], axis=2).astype(np.float32).astype(bf)
    k1c = np.arange(N1)[:, None].astype(np.float64)
    n1r = np.arange(N1 // 2)[None, :].astype(np.float64)
    th2 = 2 * np.pi * k1c * n1r / N1
    c[f"Finv_{N1}"] = (np.stack([np.cos(th2), -np.sin(th2)], axis=1) / N).astype(np.float32).astype(bf)
    t = np.linspace(0.0, 1.0, L, dtype=np.float32)[:, None]
    ang = (np.float32(2.0 * math.pi / L) * np.arange(L, dtype=np.float32))[:, None]
    bands = np.linspace(1e-4, 15, 16, dtype=np.float32)[None, :]
    z = np.concatenate([t, np.cos(bands * ang), -np.sin(bands * ang)], axis=-1).astype(np.float32)
    c[f"ZT_{L}"] = np.ascontiguousarray(z.T)
    c[f"negT_{L}"] = np.ascontiguousarray((-t[:, 0]).reshape(L // 128, 128).T)
    return c


def fft_cspec(L):
    N1 = 2 * L // 128
    return [(f"F1_{N1}", [N1 // 2, 2, N1], BF16), (f"G_{N1}", [N1, 128, 4, 128], BF16), (f"CT_{N1}", [N1, 128, 3, 128], BF16),
            (f"Finv_{N1}", [N1, 2, N1 // 2], BF16), (f"ZT_{L}", [33, L], F32), (f"negT_{L}", [128, L // 128], F32)]


CSPEC = lambda Lmax: [("swa_cos", [Lmax, 32]), ("swa_sin", [Lmax, 32]), ("maskL", [128, 128]), ("maskR", [128, 128]),
                      ("ret_cos", [128, Lmax]), ("ret_sin", [128, Lmax]), ("ret_tabs", [6, 128, 128]), ("hg_mask", [128, 256])]


def run(kb, inputs, ncores=8):
    consts = host_consts(max(kb.Ls))
    if kb.use_hy:
        for L in sorted(set(kb.Ls)):
            consts.update(fft_tables(L))
    in_maps = []
    for i in range(ncores):
        m = dict(consts)
        for n, _ in WSPEC:
            m[n] = np.ascontiguousarray(inputs[n], dtype=np.float32)
        m["c"] = np.ascontiguousarray(np.stack([inputs["c_prompt"][i], inputs["c_sample"][i]]), dtype=np.float32)
        m["x0"] = np.ascontiguousarray(inputs["x_prompt"][i], dtype=np.float32)
        m["x1"] = np.ascontiguousarray(inputs["x_sample"][i], dtype=np.float32)
        for k_ in list(m):
            if k_ in kb.din and m[k_].dtype == np.float32 and str(kb.din[k_].dtype).endswith("bfloat16"):
                m[k_] = m[k_].astype(ml_dtypes.bfloat16)
        in_maps.append({k: v for k, v in m.items() if k in kb.din})
    res = run_bass_kernel_spmd(kb.nc, in_maps, core_ids=list(range(ncores)))
    return res.results


def mixer_swa(kb, W, hT, Xsrc, Xdst, L, l, s):
    P = kb.P
    nblk = L // 128
    qT = kb.scratch("swa_qT", [16, 64, max(kb.Ls)], BF16)
    kT = kb.scratch("swa_kT", [4, 64, max(kb.Ls)], BF16)
    vD = kb.scratch("swa_v", [max(kb.Ls), 256], BF16)
    oT = kb.scratch("swa_oT", [1024, max(kb.Ls)], BF16)
    ropeC, ropeS = kb.C["swa_cos"], kb.C["swa_sin"]

    def setup(es):
        c = dict()
        c["gq"] = kb.sb(es, [128, 64], F32)
        c["gk"] = kb.sb(es, [128, 64], F32)
        c["b_g"] = P.buf()
        gq, gk = W["swa_q_gain"][0:1, :], W["swa_k_gain"][0:1, :]
        P.dma("sp", c["gq"][:], AP(gq.tensor, gq.offset, [[0, 128], [1, 64]]), writes=[c["b_g"]])
        P.dma("sp", c["gk"][:], AP(gk.tensor, gk.offset, [[0, 128], [1, 64]]), writes=[c["b_g"]])
        P.op("dve", lambda e: e.tensor_scalar(out=c["gq"][:], in0=c["gq"][:], scalar1=0.125, scalar2=None, op0=ALU.mult),
             reads=[c["b_g"]], writes=[c["b_g"]])
        c["cs"] = [(kb.sb(es, [128, 4, 2, 32], F32), P.buf()) for _ in range(2)]
        c["sq"] = [[(kb.sb(es, [128, 512], F32), P.buf()) for _ in range(2)] for _ in range(3)]
        c["xn"] = [[(kb.sb(es, [128, 512], F32), P.buf()) for _ in range(2)] for _ in range(3)]
        c["t"] = [[(kb.sb(es, [128, 4, 256], F32), P.buf()) for _ in range(2)] for _ in range(3)]
        c["xr"] = [[(kb.sb(es, [128, 512], BF16), P.buf()) for _ in range(2)] for _ in range(3)]
        c["ss"] = [[(kb.sb(es, [128, 8], F32), P.buf()) for _ in range(2)] for _ in range(3)]
        c["qt"] = [(kb.sb(es, [64, 16, 512], BF16), P.buf()) for _ in range(2)]
        c["kt"] = [(kb.sb(es, [64, 4, 512], BF16), P.buf()) for _ in range(2)]
        c["vt"] = [(kb.sb(es, [128, 4, 256], BF16), P.buf()) for _ in range(2)]
        c["n"] = 0
        return c

    def pre(kb, c, ti):
        cs, b_cs = c["cs"][ti % 2]
        t0 = ti * 512
        P.dma("sp", cs[:, :, 0, :], ropeC[t0:t0 + 512, :].rearrange("(a p) i -> p a i", p=128), writes=[b_cs])
        P.dma("sp", cs[:, :, 1, :], ropeS[t0:t0 + 512, :].rearrange("(a p) i -> p a i", p=128), writes=[b_cs])

    def chain(c, ti, a, nb, bk, n):
        cs, b_cs = c["cs"][ti % 2]
        H = 8 if nb < 2 else 4
        HW = H * 64
        psx = kb.ps[bk][:, 0:HW]
        b_ps = kb.pbuf[bk]
        sq, b_sq = c["sq"][nb][n % 2]
        xn, b_xn = c["xn"][nb][n % 2]
        tt, b_t = c["t"][nb][n % 2]
        xr, b_xr = c["xr"][nb][n % 2]
        ss, b_ss = c["ss"][nb][n % 2]
        gain = c["gq"] if nb < 2 else c["gk"]
        if nb == 2:
            vt, b_vt = c["vt"][ti % 2]
            P.op("act", lambda e: e.activation(out=vt[:, a, :], in_=kb.ps[bk][:, 256:512], func=AF.Copy),
                 reads=[b_ps], writes=[b_vt])
        P.op("act", lambda e: e.activation(out=sq[:, 0:HW], in_=psx, func=AF.Square), reads=[b_ps], writes=[b_sq])
        yield
        P.op("dve", lambda e: e.tensor_reduce(out=ss[:, 0:H], in_=sq[:, 0:HW].rearrange("p (h d) -> p h d", d=64),
                                              axis=AX.X, op=ALU.add), reads=[b_sq], writes=[b_ss])
        yield
        P.op("act", lambda e: e.activation(out=ss[:, 0:H], in_=ss[:, 0:H], func=AF.Sqrt, scale=1.0 / 64, bias=kb.epsc[:, 0:1]),
             reads=[b_ss], writes=[b_ss])
        yield
        P.op("dve", lambda e: e.reciprocal(out=ss[:, 0:H], in_=ss[:, 0:H]), reads=[b_ss], writes=[b_ss])
        yield
        ssb = AP(ss, 0, [[8, 128], [1, H], [0, 64]])
        P.op("dve", lambda e: e.tensor_tensor(out=xn[:, 0:HW].rearrange("p (h d) -> p h d", d=64),
                                              in0=psx.rearrange("p (h d) -> p h d", d=64), in1=ssb, op=ALU.mult),
             reads=[b_ps, b_ss], writes=[b_xn])
        yield
        gb = AP(gain, 0, [[64, 128], [0, H], [1, 64]])
        P.op("pool", lambda e: e.tensor_tensor(out=xn[:, 0:HW].rearrange("p (h d) -> p h d", d=64),
                                               in0=xn[:, 0:HW].rearrange("p (h d) -> p h d", d=64), in1=gb, op=ALU.mult),
             reads=[b_xn, c["b_g"]], writes=[b_xn])
        yield
        x3 = xn[:, 0:HW].rearrange("p (h d) -> p h d", d=64)
        x1, x2 = x3[:, :, 0:32], x3[:, :, 32:64]
        cosb = AP(cs, a * 64, [[256, 128], [0, H], [1, 32]])
        sinb = AP(cs, a * 64 + 32, [[256, 128], [0, H], [1, 32]])
        t4 = tt[:, :, 0:H * 32]
        tv = [t4[:, i, :].rearrange("p (h d) -> p h d", d=32) for i in range(4)]
        P.op("dve", lambda e: e.tensor_tensor(out=tv[0], in0=x1, in1=cosb, op=ALU.mult), reads=[b_xn, b_cs], writes=[b_t])
        P.op("pool", lambda e: e.tensor_tensor(out=tv[1], in0=x2, in1=sinb, op=ALU.mult), reads=[b_xn, b_cs], writes=[b_t])
        P.op("dve", lambda e: e.tensor_tensor(out=tv[2], in0=x1, in1=sinb, op=ALU.mult), reads=[b_xn, b_cs], writes=[b_t])
        P.op("pool", lambda e: e.tensor_tensor(out=tv[3], in0=x2, in1=cosb, op=ALU.mult), reads=[b_xn, b_cs], writes=[b_t])
        yield
        r3 = xr[:, 0:HW].rearrange("p (h d) -> p h d", d=64)
        P.op("dve", lambda e: e.tensor_tensor(out=r3[:, :, 0:32], in0=tv[0], in1=tv[1], op=ALU.subtract), reads=[b_t], writes=[b_xr])
        P.op("pool", lambda e: e.tensor_tensor(out=r3[:, :, 32:64], in0=tv[2], in1=tv[3], op=ALU.add), reads=[b_t], writes=[b_xr])
        yield
        bt = kb.bank()
        for h in range(H):
            P.op("pe", lambda e, h=h: e.transpose(out=kb.psb[bt][0:64, h * 128:(h + 1) * 128], in_=xr[:, h * 64:(h + 1) * 64],
                                                 identity=kb.identb[:]),
                 reads=[b_xr, kb.b_ident], writes=[kb.pbuf[bt]], signal=(h == H - 1))
        yield
        if nb < 2:
            dt_, b_dt = c["qt"][ti % 2]
            dst = dt_[:, nb * 8:(nb + 1) * 8, a * 128:(a + 1) * 128]
        else:
            dt_, b_dt = c["kt"][ti % 2]
            dst = dt_[:, :, a * 128:(a + 1) * 128]
        P.op("act", lambda e: e.activation(out=dst, in_=kb.psb[bt][0:64, 0:H * 128].rearrange("p (h t) -> p h t", t=128),
                                           func=AF.Copy), reads=[kb.pbuf[bt]], writes=[b_dt])

    def epi_group(kb, c, ti, a, bks):
        c["n"] += 1
        interleave([chain(c, ti, a, nb, bks[nb], c["n"]) for nb in range(3)])

    def post(kb, c, ti):
        t0 = ti * 512
        qt, b_qt = c["qt"][ti % 2]
        kt, b_kt = c["kt"][ti % 2]
        vt, b_vt = c["vt"][ti % 2]
        P.dma("sp", qT[:, :, t0:t0 + 512].rearrange("h d t -> d h t"), qt[:], reads=[b_qt])
        P.dma("sp", kT[:, :, t0:t0 + 512].rearrange("h d t -> d h t"), kt[:], reads=[b_kt])
        P.dma("sp", vD[t0:t0 + 512, :].rearrange("(a p) c -> p a c", p=128), vt[:], reads=[b_vt])

    phase_lt(kb, hT, 1024, L, W["swa_w_in"][0], 1536, None, setup=setup, tile_pre=pre, tile_post=post, epi_group=epi_group)

    with ExitStack() as es:
        mk = kb.sb(es, [128, 2, 128], BF16)
        b_mk = P.buf()
        esk = kb.sb(es, [128, 16], F32)
        b_esk = P.buf()
        sk = W["swa_sink"][0:1, :]
        P.dma("sp", esk[:], AP(sk.tensor, sk.offset, [[0, 128], [1, 16]]), writes=[b_esk])
        P.op("act", lambda e: e.activation(out=esk[:], in_=esk[:], func=AF.Exp), reads=[b_esk], writes=[b_esk])
        P.dma("pool", mk[:, 0, :], kb.C["maskL"], writes=[b_mk])
        P.dma("pool", mk[:, 1, :], kb.C["maskR"], writes=[b_mk])

        def grp_gen(g):
            ktg = kb.sb(es, [64, L], BF16)
            vp = kb.sb(es, [128, nblk, 65], BF16)
            b_kv = P.buf()
            qt = kb.sb(es, [64, 4, 512], BF16)
            b_qt = P.buf()
            Es = [(kb.sb(es, [128, 3, 512], BF16), P.buf()) for _ in range(2)]
            ots = [(kb.sb(es, [128, 256], BF16), P.buf()) for _ in range(2)]
            oTt = kb.sb(es, [128, 2, 512], BF16)
            b_oT = P.buf()
            dens = [(kb.sb(es, [128, 4], F32), P.buf()) for _ in range(2)]
            P.dma("sp", ktg[:], kT[g, :, 0:L], writes=[b_kv])
            P.op("dve", lambda e: e.memset(vp[:, :, 64:65], 1.0), writes=[b_kv])
            with kb.nc.allow_non_contiguous_dma(reason="v head slice"):
                pass
            P.dma("sp", vp[:, :, 0:64], vD[0:L, g * 64:(g + 1) * 64].rearrange("(b p) c -> p b c", p=128), writes=[b_kv])
            yield
            it = 0
            for qb in range(nblk):
                if qb % 4 == 0:
                    P.dma("sp", qt[:], qT[g * 4:(g + 1) * 4, :, qb * 128:qb * 128 + 512].rearrange("h d t -> d h t"),
                          writes=[b_qt])
                E, b_E = Es[it % 2]
                ot, b_ot = ots[it % 2]
                den, b_den = dens[it % 2]
                it += 1
                kbs = [x for x in (qb - 1, qb, qb + 1) if 0 <= x < nblk]
                qo = (qb % 4) * 128
                bks = []
                for i, kbi in enumerate(kbs):
                    bk = kb.acq()
                    bks.append(bk)
                    P.op("pe", lambda e: e.matmul(kb.ps[bk].rearrange("p (h t) -> p h t", t=128), lhsT=ktg[:, kbi * 128:(kbi + 1) * 128],
                                                  rhs=qt[:, :, qo:qo + 128], start=True, stop=True),
                         reads=[b_kv, b_qt], writes=[kb.pbuf[bk]])
                yield
                for i, kbi in enumerate(kbs):
                    bk = bks[i]
                    P.op("act", lambda e: e.activation(out=E[:, i, :], in_=kb.ps[bk], func=AF.Exp), reads=[kb.pbuf[bk]], writes=[b_E])
                    kb.rel(bk)
                yield
                for i, kbi in enumerate(kbs):
                    if kbi != qb:
                        mi = 0 if kbi < qb else 1
                        mb = AP(mk, mi * 128, [[256, 128], [0, 4], [1, 128]])
                        P.op("dve", lambda e: e.tensor_tensor(out=E[:, i, :].rearrange("p (h t) -> p h t", t=128),
                                                              in0=E[:, i, :].rearrange("p (h t) -> p h t", t=128), in1=mb, op=ALU.mult),
                             reads=[b_E, b_mk], writes=[b_E])
                yield
                bo = kb.acq()
                for hh in range(4):
                    for i, kbi in enumerate(kbs):
                        P.op("pe", lambda e: e.matmul(kb.ps[bo][:, hh * 65:(hh + 1) * 65], lhsT=E[:, i, hh * 128:(hh + 1) * 128], rhs=vp[:, kbi, :],
                                                      start=(i == 0), stop=(i == len(kbs) - 1)),
                             reads=[b_E, b_kv], writes=[kb.pbuf[bo]], signal=(hh == 3 and i == len(kbs) - 1))
                yield
                o3 = kb.ps[bo][:, 0:260].rearrange("p (h d) -> p h d", d=65)
                P.op("dve", lambda e: e.tensor_tensor(out=den[:], in0=o3[:, :, 64], in1=esk[:, g * 4:(g + 1) * 4], op=ALU.add),
                     reads=[kb.pbuf[bo], b_esk], writes=[b_den])
                P.op("dve", lambda e: e.reciprocal(out=den[:], in_=den[:]), reads=[b_den], writes=[b_den])
                yield
                db = AP(den, 0, [[4, 128], [1, 4], [0, 64]])
                P.op("dve", lambda e: e.tensor_tensor(out=ot[:].rearrange("p (h d) -> p h d", d=64), in0=o3[:, :, 0:64], in1=db, op=ALU.mult),
                     reads=[kb.pbuf[bo], b_den], writes=[b_ot])
                kb.rel(bo)
                yield
                bt = kb.acq()
                for hf in range(2):
                    P.op("pe", lambda e: e.transpose(out=kb.psb[bt][:, hf * 128:(hf + 1) * 128], in_=ot[:, hf * 128:(hf + 1) * 128],
                                                     identity=kb.identb[:]), reads=[b_ot, kb.b_ident], writes=[kb.pbuf[bt]], signal=(hf == 1))
                yield
                P.op("act", lambda e: e.activation(out=oTt[:, :, qo:qo + 128], in_=kb.psb[bt][:, 0:256].rearrange("p (h t) -> p h t", t=128),
                                                   func=AF.Copy), reads=[kb.pbuf[bt]], writes=[b_oT])
                kb.rel(bt)
                if qb % 4 == 3:
                    t0 = (qb - 3) * 128
                    P.dma("pool", oT[g * 256:(g + 1) * 256, t0:t0 + 512].rearrange("(h p) t -> p h t", p=128), oTt[:], reads=[b_oT])
                yield

        interleave([grp_gen(g) for g in range(2)])
        P.flush()
        interleave([grp_gen(g) for g in range(2, 4)])
        P.flush()

    lt_residual(kb, oT[:, 0:L], 1024, L, W["swa_w_out"][0], Xsrc, Xdst, l, 0, s)


MIXERS[2] = mixer_swa


def lt_plain(kb, actT, K, L, Wap, N, dst):
    P = kb.P

    def setup(es):
        return dict(o=[(kb.sb(es, [128, 4, N], BF16), P.buf()) for _ in range(2)], n=0)

    def epi(kb, c, ti, a, nb, bk):
        o, b_o = c["o"][ti % 2]
        c["n"] += 1
        if c["n"] % 2:
            P.op("act", lambda e: e.activation(out=o[:, a, nb * 512:(nb + 1) * 512], in_=kb.ps[bk], func=AF.Copy),
                 reads=[kb.pbuf[bk]], writes=[b_o])
        else:
            P.op("dve", lambda e: e.tensor_copy(out=o[:, a, nb * 512:(nb + 1) * 512], in_=kb.ps[bk]),
                 reads=[kb.pbuf[bk]], writes=[b_o])

    def post(kb, c, ti):
        o, b_o = c["o"][ti % 2]
        P.dma("sp", dst[ti * 512:(ti + 1) * 512, :].rearrange("(a p) n -> p a n", p=128), o[:], reads=[b_o])

    phase_lt(kb, actT, K, L, Wap, N, epi, setup=setup, tile_post=post)


def mixer_ret(kb, W, hT, Xsrc, Xdst, L, l, s):
    P = kb.P
    Lm = max(kb.Ls)
    nblk = L // 128
    qT = kb.scratch("ret_qT", [1024, Lm], BF16)
    kT = kb.scratch("ret_kT", [1024, Lm], BF16)
    gsT = kb.scratch("ret_gsT", [2048, Lm], BF16)
    vD = kb.scratch("ret_v", [Lm, 2048], BF16)
    ogT = kb.scratch("ret_ogT", [2048, Lm], BF16)
    SbD = kb.scratch("ret_Sb", [4, Lm // 128, 128, 2, 512], BF16)
    win = W["ret_w_in"][0]

    groups = []
    for h in range(4):
        groups.append([(win, h * 256), (win, h * 256 + 128)])
    for h in range(4):
        groups.append([(win, 1024 + h * 256, 1.0 / 16), (win, 1024 + h * 256 + 128, 1.0 / 16)])
    for jj in range(16):
        groups.append([(win, 4096 + jj * 128)])

    def setup(es):
        c = dict()
        c["cs"] = kb.sb(es, [128, 2, 2048], F32)
        c["b_cs"] = P.buf()
        c["t"] = [(kb.sb(es, [128, 2048], F32), P.buf()) for _ in range(4)]
        c["o"] = [(kb.sb(es, [128, 2, 2048], BF16), P.buf()) for _ in range(2)]
        return c

    def epi(kb, c, gi, rows, t0, Ls):
        o, b_o = c["o"][gi % 2]
        if gi == 0:
            P.dma("sp", c["cs"][:, 0, 0:Ls], kb.C["ret_cos"][:, t0:t0 + Ls], writes=[c["b_cs"]])
            P.dma("sp", c["cs"][:, 1, 0:Ls], kb.C["ret_sin"][:, t0:t0 + Ls], writes=[c["b_cs"]])
        if gi < 8:
            (x1, b1), (x2, b2) = rows
            cos, sin = c["cs"][:, 0, 0:Ls], c["cs"][:, 1, 0:Ls]
            (t1, bt1), (t2, bt2), (t3, bt3), (t4, bt4) = c["t"]
            P.op("dve", lambda e: e.tensor_tensor(out=t1[:, 0:Ls], in0=x1[:, 0:Ls], in1=cos, op=ALU.mult), reads=[b1, c["b_cs"]], writes=[bt1])
            P.op("pool", lambda e: e.tensor_tensor(out=t2[:, 0:Ls], in0=x2[:, 0:Ls], in1=sin, op=ALU.mult), reads=[b2, c["b_cs"]], writes=[bt2])
            P.op("pool", lambda e: e.tensor_tensor(out=t3[:, 0:Ls], in0=x1[:, 0:Ls], in1=sin, op=ALU.mult), reads=[b1, c["b_cs"]], writes=[bt3])
            P.op("dve", lambda e: e.tensor_tensor(out=t4[:, 0:Ls], in0=x2[:, 0:Ls], in1=cos, op=ALU.mult), reads=[b2, c["b_cs"]], writes=[bt4])
            P.op("dve", lambda e: e.tensor_tensor(out=o[:, 0, 0:Ls], in0=t1[:, 0:Ls], in1=t2[:, 0:Ls], op=ALU.subtract), reads=[bt1, bt2], writes=[b_o])
            P.op("pool", lambda e: e.tensor_tensor(out=o[:, 1, 0:Ls], in0=t3[:, 0:Ls], in1=t4[:, 0:Ls], op=ALU.add), reads=[bt3, bt4], writes=[b_o])
            dstT = qT if gi < 4 else kT
            h = gi % 4
            P.dma("pool", dstT[h * 256:(h + 1) * 256, t0:t0 + Ls].rearrange("(c p) t -> p c t", p=128), o[:, :, 0:Ls], reads=[b_o])
        else:
            (g, bg), = rows
            jj = gi - 8
            P.op("act", lambda e: e.activation(out=o[:, 0, 0:Ls], in_=g[:, 0:Ls], func=AF.Silu), reads=[bg], writes=[b_o])
            P.dma("pool", gsT[jj * 128:(jj + 1) * 128, t0:t0 + Ls], o[:, 0, 0:Ls], reads=[b_o])

    phase_lf(kb, hT, 1024, L, groups, epi, setup=setup, depth=1, evac="alt", cast="pool")
    lt_plain(kb, hT, 1024, L, win[:, 2048:4096], 2048, vD)

    with ExitStack() as es:
        lg = kb.sb(es, [128, 8], F32)
        b_lg = P.buf()
        rd = W["ret_decay"][0:1]
        P.dma("sp", lg[:], AP(rd.tensor, rd.offset, [[0, 128], [1, 8]]), writes=[b_lg])
        P.op("act", lambda e: e.activation(out=lg[:], in_=lg[:], func=AF.Exp), reads=[b_lg], writes=[b_lg])
        P.op("dve", lambda e: e.tensor_scalar(out=lg[:], in0=lg[:], scalar1=-1.0, scalar2=None, op0=ALU.mult), reads=[b_lg], writes=[b_lg])
        hc = kb.sb(es, [128, 6, 128], F32)
        b_hc = P.buf()
        P.dma("sp", hc[:], kb.C["ret_tabs"].rearrange("j p t -> p j t"), writes=[b_hc])
        DT = kb.sb(es, [128, 4, 128], F32)
        qrow = kb.sb(es, [128, 4, 2, 128], F32)
        kcol = kb.sb(es, [128, 4, 2], F32)
        g128 = kb.sb(es, [128, 4, 2], F32)
        b_tab = P.buf()
        tmp = kb.sb(es, [128, 128], F32)
        b_tmp = P.buf()
        for h in range(4):
            lf, lb = lg[:, h:h + 1], lg[:, 4 + h:5 + h]
            P.op("dve", lambda e, lf=lf: e.tensor_scalar(out=tmp[:], in0=hc[:, 0, :], scalar1=lf, scalar2=None, op0=ALU.mult),
                 reads=[b_hc, b_lg], writes=[b_tmp])
            P.op("dve", lambda e, lb=lb: e.scalar_tensor_tensor(out=tmp[:], in0=hc[:, 1, :], scalar=lb, in1=tmp[:], op0=ALU.mult, op1=ALU.add),
                 reads=[b_hc, b_lg, b_tmp], writes=[b_tmp])
            P.op("act", lambda e: e.activation(out=tmp[:], in_=tmp[:], func=AF.Exp), reads=[b_tmp], writes=[b_tmp])
            P.op("dve", lambda e, h=h: e.tensor_tensor(out=DT[:, h, :], in0=tmp[:], in1=hc[:, 2, :], op=ALU.mult), reads=[b_tmp, b_hc], writes=[b_tab])
            P.op("act", lambda e, h=h, lf=lf: e.activation(out=qrow[:, h, 0, :], in_=hc[:, 3, :], func=AF.Exp, scale=lf), reads=[b_hc, b_lg], writes=[b_tab])
            P.op("act", lambda e, h=h, lb=lb: e.activation(out=qrow[:, h, 1, :], in_=hc[:, 4, :], func=AF.Exp, scale=lb), reads=[b_hc, b_lg], writes=[b_tab])
            P.op("act", lambda e, h=h, lf=lf: e.activation(out=kcol[:, h, 0:1], in_=hc[:, 5, 0:1], func=AF.Exp, scale=lf), reads=[b_hc, b_lg], writes=[b_tab])
            P.op("act", lambda e, h=h, lb=lb: e.activation(out=kcol[:, h, 1:2], in_=hc[:, 5, 1:2], func=AF.Exp, scale=lb), reads=[b_hc, b_lg], writes=[b_tab])
            P.op("act", lambda e, h=h, lf=lf: e.activation(out=g128[:, h, 0:1], in_=hc[:, 5, 2:3], func=AF.Exp, scale=lf), reads=[b_hc, b_lg], writes=[b_tab])
            P.op("act", lambda e, h=h, lb=lb: e.activation(out=g128[:, h, 1:2], in_=hc[:, 5, 2:3], func=AF.Exp, scale=lb), reads=[b_hc, b_lg], writes=[b_tab])

        nb4 = nblk // 4

        def mkbufs():
            B = dict()
            tok = P.buf()
            B["St"] = (kb.sb(es, [128, 2, 512], F32), P.buf())
            B["Stb"] = [(kb.sb(es, [128, 2, 512], BF16), P.buf()) for _ in range(2)]
            B["kt"] = (kb.sb(es, [128, 2, 512], BF16), tok)
            B["qt"] = (kb.sb(es, [128, 2, 512], BF16), tok)
            B["vt"] = (kb.sb(es, [128, 4, 512], BF16), tok)
            B["gt"] = (kb.sb(es, [128, 4, 512], BF16), tok)
            B["og"] = (kb.sb(es, [128, 4, 512], BF16), P.buf())
            B["sbs"] = [(kb.sb(es, [128, 2, 512], BF16), P.buf()) for _ in range(2)]
            B["kes"] = [(kb.sb(es, [128, 256], BF16), P.buf()) for _ in range(2)]
            B["pts"] = [(kb.sb(es, [128, 128], BF16), P.buf()) for _ in range(2)]
            B["qds"] = [(kb.sb(es, [128, 2, 2, 128], BF16), P.buf()) for _ in range(2)]
            B["ons"] = [(kb.sb(es, [128, 512], BF16), P.buf()) for _ in range(2)]
            B["sss"] = [(kb.sb(es, [128, 2], F32), P.buf()) for _ in range(2)]
            B["it"] = 0
            return B

        BUFS = [mkbufs() for _ in range(4)]

        def load_kv(B, h, b4, need_q):
            kt, b_kt = B["kt"]
            vt, b_vt = B["vt"]
            t0 = b4 * 512
            P.dma("sp", kt[:], kT[h * 256:(h + 1) * 256, t0:t0 + 512].rearrange("(c p) t -> p c t", p=128), writes=[b_kt])
            P.dma("sp", vt[:], vD[t0:t0 + 512, h * 512:(h + 1) * 512].rearrange("(a p) n -> p a n", p=128), writes=[b_vt])
            if need_q:
                qt, b_qt = B["qt"]
                gt_, b_gt = B["gt"]
                P.dma("sp", qt[:], qT[h * 256:(h + 1) * 256, t0:t0 + 512].rearrange("(c p) t -> p c t", p=128), writes=[b_qt])
                P.dma("sp", gt_[:], gsT[h * 512:(h + 1) * 512, t0:t0 + 512].rearrange("(c p) t -> p c t", p=128), writes=[b_gt])

        def state_update(B, h, d, a):
            St, b_St = B["St"]
            kt, b_kt = B["kt"]
            vt, b_vt = B["vt"]
            B["it"] += 1
            ke, b_ke = B["kes"][B["it"] % 2]
            bt = kb.acq()
            for cc in range(2):
                P.op("pe", lambda e: e.transpose(out=kb.psb[bt][:, cc * 128:(cc + 1) * 128], in_=kt[:, cc, a * 128:(a + 1) * 128],
                                                 identity=kb.identb[:]), reads=[b_kt, kb.b_ident], writes=[kb.pbuf[bt]], signal=(cc == 1))
            yield
            P.op("dve", lambda e: e.tensor_scalar(out=ke[:], in0=kb.psb[bt][:, 0:256], scalar1=kcol[:, h, d:d + 1], scalar2=None, op0=ALU.mult),
                 reads=[kb.pbuf[bt], b_tab], writes=[b_ke])
            kb.rel(bt)
            yield
            bks = []
            for cc in range(2):
                bk = kb.acq()
                bks.append(bk)
                P.op("pe", lambda e: e.matmul(kb.ps[bk], lhsT=ke[:, cc * 128:(cc + 1) * 128], rhs=vt[:, a, :], start=True, stop=True),
                     reads=[b_ke, b_vt], writes=[kb.pbuf[bk]])
            yield
            for cc in range(2):
                bk = bks[cc]
                P.op("dve", lambda e: e.scalar_tensor_tensor(out=St[:, cc, :], in0=St[:, cc, :], scalar=g128[:, h, d:d + 1],
                                                             in1=kb.ps[bk], op0=ALU.mult, op1=ALU.add),
                     reads=[kb.pbuf[bk], b_St, b_tab], writes=[b_St])
                kb.rel(bk)
            yield

        def bwd_gen(h, B):
            St, b_St = B["St"]
            P.op("dve", lambda e: e.memset(St[:], 0.0), writes=[b_St])
            for b4 in range(nb4 - 1, -1, -1):
                load_kv(B, h, b4, False)
                yield
                for a in range(3, -1, -1):
                    J = b4 * 4 + a
                    B["it"] += 1
                    sb_, b_sb = B["Stb"][B["it"] % 2]
                    P.op("act", lambda e: e.activation(out=sb_[:], in_=St[:], func=AF.Copy), reads=[b_St], writes=[b_sb])
                    P.dma("pool", SbD[h, J], sb_[:], reads=[b_sb])
                    yield
                    if J > 0:
                        yield from state_update(B, h, 1, a)

        def fwd_gen(h, B):
            St, b_St = B["St"]
            kt, b_kt = B["kt"]
            vt, b_vt = B["vt"]
            qt, b_qt = B["qt"]
            gt_, b_gt = B["gt"]
            og, b_og = B["og"]
            P.op("dve", lambda e: e.memset(St[:], 0.0), writes=[b_St])
            sfb, b_sfb = B["Stb"][0]
            P.op("act", lambda e: e.activation(out=sfb[:], in_=St[:], func=AF.Copy), reads=[b_St], writes=[b_sfb])
            for b4 in range(nb4):
                load_kv(B, h, b4, True)
                yield
                for a in range(4):
                    J = b4 * 4 + a
                    B["it"] += 1
                    it = B["it"]
                    sbt, b_sbt = B["sbs"][it % 2]
                    P.dma("sp", sbt[:], SbD[h, J], writes=[b_sbt])
                    pt, b_pt = B["pts"][it % 2]
                    qd, b_qd = B["qds"][it % 2]
                    on, b_on = B["ons"][it % 2]
                    ss, b_ss = B["sss"][it % 2]
                    ts = slice(a * 128, (a + 1) * 128)
                    b1 = kb.acq()
                    for cc in range(2):
                        P.op("pe", lambda e: e.matmul(kb.ps[b1][:, 0:128], lhsT=kt[:, cc, ts], rhs=qt[:, cc, ts],
                                                      start=(cc == 0), stop=(cc == 1)),
                             reads=[b_kt, b_qt], writes=[kb.pbuf[b1]], signal=(cc == 1))
                    for d in range(2):
                        qb_ = AP(qrow, (h * 2 + d) * 128, [[1024, 128], [0, 2], [1, 128]])
                        P.op("pool", lambda e: e.tensor_tensor(out=qd[:, d, :, :], in0=qt[:, :, ts], in1=qb_, op=ALU.mult),
                             reads=[b_qt, b_tab], writes=[b_qd])
                    yield
                    P.op("dve", lambda e: e.tensor_tensor(out=pt[:], in0=kb.ps[b1][:, 0:128], in1=DT[:, h, :], op=ALU.mult),
                         reads=[kb.pbuf[b1], b_tab], writes=[b_pt])
                    kb.rel(b1)
                    yield
                    bo = kb.acq()
                    P.op("pe", lambda e: e.matmul(kb.ps[bo], lhsT=pt[:], rhs=vt[:, a, :], start=True, stop=False),
                         reads=[b_pt, b_vt], writes=[kb.pbuf[bo]], signal=False)
                    for cc in range(2):
                        P.op("pe", lambda e: e.matmul(kb.ps[bo], lhsT=qd[:, 0, cc, :], rhs=sfb[:, cc, :], start=False, stop=False),
                             reads=[b_qd, b_sfb], writes=[kb.pbuf[bo]], signal=False)
                    for cc in range(2):
                        P.op("pe", lambda e: e.matmul(kb.ps[bo], lhsT=qd[:, 1, cc, :], rhs=sbt[:, cc, :], start=False, stop=(cc == 1)),
                             reads=[b_qd, b_sbt], writes=[kb.pbuf[bo]], signal=(cc == 1))
                    yield
                    P.op("act", lambda e: e.activation(out=on[:], in_=kb.ps[bo], func=AF.Square, accum_out=ss[:, 0:1]),
                         reads=[kb.pbuf[bo]], writes=[b_on, b_ss])
                    P.op("act", lambda e: e.activation(out=ss[:, 1:2], in_=ss[:, 0:1], func=AF.Sqrt, scale=1.0 / 512, bias=kb.epsc[:, 0:1]),
                         reads=[b_ss], writes=[b_ss])
                    yield
                    P.op("dve", lambda e: e.reciprocal(out=ss[:, 1:2], in_=ss[:, 1:2]), reads=[b_ss], writes=[b_ss])
                    P.op("dve", lambda e: e.tensor_scalar(out=on[:], in0=kb.ps[bo], scalar1=ss[:, 1:2], scalar2=None, op0=ALU.mult),
                         reads=[kb.pbuf[bo], b_ss], writes=[b_on])
                    kb.rel(bo)
                    yield
                    bt = kb.acq()
                    for cc in range(4):
                        P.op("pe", lambda e: e.transpose(out=kb.psb[bt][:, cc * 128:(cc + 1) * 128], in_=on[:, cc * 128:(cc + 1) * 128],
                                                         identity=kb.identb[:]), reads=[b_on, kb.b_ident], writes=[kb.pbuf[bt]], signal=(cc == 3))
                    yield
                    P.op("dve", lambda e: e.tensor_tensor(out=og[:, :, ts], in0=kb.psb[bt][:, 0:512].rearrange("p (c t) -> p c t", t=128),
                                                          in1=gt_[:, :, ts], op=ALU.mult), reads=[kb.pbuf[bt], b_gt], writes=[b_og])
                    kb.rel(bt)
                    yield
                    if J < nblk - 1:
                        yield from state_update(B, h, 0, a)
                        P.op("act", lambda e: e.activation(out=sfb[:], in_=St[:], func=AF.Copy), reads=[b_St], writes=[b_sfb])
                        yield
                P.dma("pool", ogT[h * 512:(h + 1) * 512, b4 * 512:(b4 + 1) * 512].rearrange("(c p) t -> p c t", p=128), og[:], reads=[b_og])

        interleave([bwd_gen(h, BUFS[h]) for h in range(4)])
        P.flush()
        interleave([fwd_gen(h, BUFS[h]) for h in range(4)])
        P.flush()

    lt_residual(kb, ogT[:, 0:L], 2048, L, W["ret_w_out"][0], Xsrc, Xdst, l, 0, s)


MIXERS[1] = mixer_ret


def mixer_hgrn(kb, W, hT, Xsrc, Xdst, L, l, s):
    P = kb.P
    Lm = max(kb.Ls)
    win = W["hg_w_in"][0]
    hq = kb.scratch("hg_q", [2, 1024, Lm], BF16)
    hki = kb.scratch("hg_ki", [2, 1024, Lm], BF16)
    hke = kb.scratch("hg_ke", [2, 1024, Lm], BF16)
    hdl = kb.scratch("hg_dl", [2, 1024, Lm // 64], F32)
    gsT = kb.scratch("hg_gsT", [1024, Lm], BF16)
    vD = kb.scratch("hg_v", [Lm, 1024], BF16)
    ogT = kb.scratch("hg_ogT", [1024, Lm], BF16)
    SbD = kb.scratch("hg_Sb", [8, Lm // 64, 128, 128], BF16)
    LS = min(1024, L)
    nchs = LS // 64

    groups = [[(win, h * 128), (win, 2048 + h * 128), (win, 3072 + h * 128)] for h in range(8)]
    groups += [[(win, 4096 + j * 128)] for j in range(8)]

    def setup(es):
        c = dict()
        lbr = kb.sb(es, [128, 2, 4, 8], F32)
        c["b_lb"] = P.buf()
        for d in range(2):
            for ll in range(4):
                load_col(P, lbr[:, d, ll, :], W["hg_lb"][d, ll], c["b_lb"])
        P.op("act", lambda e: e.activation(out=lbr[:], in_=lbr[:], func=AF.Exp), reads=[c["b_lb"]], writes=[c["b_lb"]])
        c["lb"] = kb.sb(es, [128, 2, 8], F32)
        c["oml"] = kb.sb(es, [128, 2, 8], F32)
        tot = kb.sb(es, [128, 2, 8], F32)
        lb, oml = c["lb"], c["oml"]
        P.op("dve", lambda e: e.memset(lb[:], 0.0), writes=[c["b_lb"]])
        P.op("dve", lambda e: e.memset(tot[:], 0.0), writes=[c["b_lb"]])
        for ll in range(4):
            if ll < l:
                P.op("dve", lambda e, ll=ll: e.tensor_tensor(out=lb[:], in0=lb[:], in1=lbr[:, :, ll, :], op=ALU.add),
                     reads=[c["b_lb"]], writes=[c["b_lb"]])
            P.op("dve", lambda e, ll=ll: e.tensor_tensor(out=tot[:], in0=tot[:], in1=lbr[:, :, ll, :], op=ALU.add),
                 reads=[c["b_lb"]], writes=[c["b_lb"]])
        P.op("dve", lambda e: e.reciprocal(out=tot[:], in_=tot[:]), reads=[c["b_lb"]], writes=[c["b_lb"]])
        P.op("dve", lambda e: e.tensor_tensor(out=lb[:], in0=lb[:], in1=tot[:], op=ALU.mult), reads=[c["b_lb"]], writes=[c["b_lb"]])
        P.op("dve", lambda e: e.tensor_scalar(out=oml[:], in0=lb[:], scalar1=-1.0, scalar2=1.0, op0=ALU.mult, op1=ALU.add),
             reads=[c["b_lb"]], writes=[c["b_lb"]])
        c["mask"] = kb.sb(es, [128, LS], F32)
        c["b_mask"] = P.buf()
        P.op("dve", lambda e: e.memset(c["mask"][:], 1.0), writes=[c["b_mask"]])
        P.op("dve", lambda e: e.memset(c["mask"][:].rearrange("p (c t) -> p c t", t=64)[:, :, 0:1], 0.0), writes=[c["b_mask"]])
        c["qs"] = (kb.sb(es, [128, LS], F32), P.buf())
        for nm in ("sg", "sn", "g", "k", "cb", "bb", "ep", "em", "kf"):
            c[nm] = [(kb.sb(es, [128, LS], F32), P.buf()) for _ in range(2)]
        for nm in ("oq", "oki", "oke"):
            c[nm] = [(kb.sb(es, [128, LS], BF16), P.buf()) for _ in range(2)]
        c["dl"] = [(kb.sb(es, [128, nchs], F32), P.buf()) for _ in range(2)]
        c["n"] = 0
        return c

    def epi(kb, c, gi, rows, t0, Ls):
        if gi >= 8:
            (g, bg), = rows
            o, b_o = c["oq"][gi % 2]
            P.op("act", lambda e: e.activation(out=o[:, 0:Ls], in_=g[:, 0:Ls], func=AF.Silu), reads=[bg], writes=[b_o])
            P.dma("pool", gsT[(gi - 8) * 128:(gi - 7) * 128, t0:t0 + Ls], o[:, 0:Ls], reads=[b_o])
            return
        h = gi
        (q, bq), (ff, bff), (fb, bfb) = rows
        qs, b_qs = c["qs"]
        P.op("act", lambda e: e.activation(out=qs[:], in_=q[:, 0:Ls], func=AF.Silu), reads=[bq], writes=[b_qs])
        nch = Ls // 64
        def dir_gen(d, r, br):
            sg, b_sg = c["sg"][d]
            sn, b_sn = c["sn"][d]
            g, b_g = c["g"][d]
            k, b_k = c["k"][d]
            cb, b_cb = c["cb"][d]
            bb, b_bb = c["bb"][d]
            ep, b_ep = c["ep"][d]
            em, b_em = c["em"][d]
            kf, b_kf = c["kf"][d]
            oq, b_oq = c["oq"][d]
            oki, b_oki = c["oki"][d]
            oke, b_oke = c["oke"][d]
            dl, b_dl = c["dl"][d]
            lbc, omc = c["lb"][:, d, h:h + 1], c["oml"][:, d, h:h + 1]
            P.op("act", lambda e: e.activation(out=sg[:], in_=r[:, 0:Ls], func=AF.Sigmoid), reads=[br], writes=[b_sg])
            P.op("act", lambda e: e.activation(out=sn[:], in_=r[:, 0:Ls], func=AF.Sigmoid, scale=-1.0), reads=[br], writes=[b_sn])
            yield
            P.op("dve", lambda e: e.tensor_scalar(out=g[:], in0=sg[:], scalar1=omc, scalar2=lbc, op0=ALU.mult, op1=ALU.add),
                 reads=[b_sg, c["b_lb"]], writes=[b_g])
            P.op("pool", lambda e: e.tensor_scalar(out=k[:], in0=sn[:], scalar1=omc, scalar2=0.0, op0=ALU.mult, op1=ALU.add),
                 reads=[b_sn, c["b_lb"]], writes=[b_k])
            yield
            P.op("act", lambda e: e.activation(out=g[:], in_=g[:], func=AF.Ln), reads=[b_g], writes=[b_g])
            yield
            P.op("dve", lambda e: e.tensor_tensor_scan(out=cb[:], data0=c["mask"][:], data1=g[:], initial=0.0, op0=ALU.mult, op1=ALU.add),
                 reads=[c["b_mask"], b_g], writes=[b_cb])
            yield
            if d == 0:
                bsrc, b_bsrc = cb, b_cb
                dlv = AP(ep, 63, [[LS, 128], [64, nch], [0, 64]])
                dls = AP(ep, 63, [[LS, 128], [64, nch]])
            else:
                P.op("pool", lambda e: e.tensor_tensor(out=bb[:], in0=g[:], in1=cb[:], op=ALU.subtract), reads=[b_g, b_cb], writes=[b_bb])
                yield
                totb = AP(cb, 63, [[LS, 128], [64, nch], [0, 64]])
                P.op("dve", lambda e: e.tensor_tensor(out=bb[:].rearrange("p (c t) -> p c t", t=64),
                                                      in0=bb[:].rearrange("p (c t) -> p c t", t=64), in1=totb, op=ALU.add),
                     reads=[b_bb, b_cb], writes=[b_bb])
                yield
                bsrc, b_bsrc = bb, b_bb
                dlv = AP(ep, 0, [[LS, 128], [64, nch], [0, 64]])
                dls = AP(ep, 0, [[LS, 128], [64, nch]])
            P.op("act", lambda e: e.activation(out=ep[:], in_=bsrc[:], func=AF.Exp), reads=[b_bsrc], writes=[b_ep])
            P.op("act", lambda e: e.activation(out=em[:], in_=bsrc[:], func=AF.Exp, scale=-1.0), reads=[b_bsrc], writes=[b_em])
            yield
            P.op("pool", lambda e: e.tensor_tensor(out=oq[:], in0=qs[:], in1=ep[:], op=ALU.mult), reads=[b_qs, b_ep], writes=[b_oq])
            P.op("dve", lambda e: e.tensor_tensor(out=kf[:], in0=k[:], in1=em[:], op=ALU.mult), reads=[b_k, b_em], writes=[b_kf])
            yield
            P.op("act", lambda e: e.activation(out=oki[:], in_=kf[:], func=AF.Copy), reads=[b_kf], writes=[b_oki])
            P.op("pool", lambda e: e.tensor_tensor(out=oke[:].rearrange("p (c t) -> p c t", t=64),
                                                   in0=kf[:].rearrange("p (c t) -> p c t", t=64), in1=dlv, op=ALU.mult),
                 reads=[b_kf, b_ep], writes=[b_oke])
            P.op("dve", lambda e: e.tensor_copy(out=dl[:, 0:nch], in_=dls), reads=[b_ep], writes=[b_dl])
            yield
            rs = slice(h * 128, (h + 1) * 128)
            P.dma("pool", hq[d, rs, t0:t0 + Ls], oq[:, 0:Ls], reads=[b_oq])
            P.dma("pool", hki[d, rs, t0:t0 + Ls], oki[:, 0:Ls], reads=[b_oki])
            P.dma("pool", hke[d, rs, t0:t0 + Ls], oke[:, 0:Ls], reads=[b_oke])
            P.dma("pool", hdl[d, rs, t0 // 64:t0 // 64 + nch], dl[:, 0:nch], reads=[b_dl])

        interleave([dir_gen(0, ff, bff), dir_gen(1, fb, bfb)])

    phase_lf(kb, hT, 1024, L, groups, epi, Lseg=LS, nrow=3, setup=setup, depth=1, evac="alt", cast="pool")
    lt_plain(kb, hT, 1024, L, win[:, 1024:2048], 1024, vD)

    nb4 = L // 512
    NWAY = 4
    kb.reserved = set(range(8 - NWAY, 8))
    with ExitStack() as es:
        mk = kb.sb(es, [128, 256], BF16)
        b_mk = P.buf()
        P.dma("pool", mk[:], kb.C["hg_mask"], writes=[b_mk])
        gcol = kb.sb(es, [128, 1], F32)
        b_gc = P.buf()
        load_col(P, gcol[:], W["hg_gain"][0], b_gc)

        def mkbufs():
            B = dict()
            tok = [P.buf(), P.buf()]
            B["St"] = (kb.sb(es, [128, 128], F32), P.buf())
            B["Sfb"] = [(kb.sb(es, [128, 128], BF16), P.buf()) for _ in range(2)]
            B["stg"] = [(kb.sb(es, [128, 8, 128], BF16), P.buf()) for _ in range(2)]
            B["kes"] = [(kb.sb(es, [128, 512], BF16), tok[i]) for i in range(2)]
            B["kis"] = [(kb.sb(es, [128, 2, 512], BF16), tok[i]) for i in range(2)]
            B["qds"] = [(kb.sb(es, [128, 2, 512], BF16), tok[i]) for i in range(2)]
            B["vts"] = [(kb.sb(es, [128, 4, 128], BF16), tok[i]) for i in range(2)]
            B["gts"] = [(kb.sb(es, [128, 512], BF16), tok[i]) for i in range(2)]
            B["ogs"] = [(kb.sb(es, [128, 512], BF16), P.buf()) for _ in range(2)]
            B["dls"] = [(kb.sb(es, [128, 8], F32), tok[i]) for i in range(2)]
            B["sbl"] = [(kb.sb(es, [128, 8, 128], BF16), tok[i]) for i in range(2)]
            B["ktm"] = [(kb.sb(es, [128, 128], BF16), P.buf()) for _ in range(2)]
            B["pts"] = [(kb.sb(es, [128, 256], BF16), P.buf()) for _ in range(2)]
            B["ons"] = [(kb.sb(es, [128, 128], BF16), P.buf()) for _ in range(2)]
            B["sss"] = [(kb.sb(es, [128, 2], F32), P.buf()) for _ in range(2)]
            B["cnt"] = 0
            return B

        BUFS = [mkbufs() for _ in range(NWAY)]

        def ktrans(B, ke, b_ke, a):
            B["cnt"] += 1
            kt, b_kt = B["ktm"][B["cnt"] % 2]
            bt = kb.bank()
            P.op("pe", lambda e: e.transpose(out=kb.psb[bt][:, 0:128], in_=ke[:, a * 128:(a + 1) * 128], identity=kb.identb[:]),
                 reads=[b_ke, kb.b_ident], writes=[kb.pbuf[bt]])
            P.op("act", lambda e: e.activation(out=kt[:], in_=kb.psb[bt][:, 0:128], func=AF.Copy), reads=[kb.pbuf[bt]], writes=[b_kt])
            return kt, b_kt

        def upd_chunk(B, kt, b_kt, vt, b_vt, dl, b_dl, a, hf):
            St, b_St = B["St"]
            bk = kb.bank()
            lo = hf * 64
            P.op("pe", lambda e: e.matmul(kb.ps[bk][:, 0:128], lhsT=kt[lo:lo + 64, :], rhs=vt[lo:lo + 64, a, :], start=True, stop=True),
                 reads=[b_kt, b_vt], writes=[kb.pbuf[bk]])
            ci = a * 2 + hf
            P.op("dve", lambda e: e.scalar_tensor_tensor(out=St[:], in0=St[:], scalar=dl[:, ci:ci + 1], in1=kb.ps[bk][:, 0:128],
                                                         op0=ALU.mult, op1=ALU.add), reads=[kb.pbuf[bk], b_St, b_dl], writes=[b_St])

        def bwd_gen(h, B):
            rs = slice(h * 128, (h + 1) * 128)
            St, b_St = B["St"]
            P.op("dve", lambda e: e.memset(St[:], 0.0), writes=[b_St])
            for b4 in range(nb4 - 1, -1, -1):
                t0 = b4 * 512
                ke, b_ke = B["kes"][b4 % 2]
                vt, b_vt = B["vts"][b4 % 2]
                dl, b_dl = B["dls"][b4 % 2]
                sg_, b_sg = B["stg"][b4 % 2]
                P.dma("sp", ke[:], hke[1, rs, t0:t0 + 512], writes=[b_ke])
                P.dma("sp", vt[:], vD[t0:t0 + 512, rs].rearrange("(a p) n -> p a n", p=128), writes=[b_vt])
                P.dma("sp", dl[:], hdl[1, rs, t0 // 64:t0 // 64 + 8], writes=[b_dl])
                yield
                for a in range(3, -1, -1):
                    kt, b_kt = ktrans(B, ke, b_ke, a)
                    yield
                    for hf in (1, 0):
                        ci = a * 2 + hf
                        P.op("act", lambda e: e.activation(out=sg_[:, ci, :], in_=St[:], func=AF.Copy), reads=[b_St], writes=[b_sg])
                        upd_chunk(B, kt, b_kt, vt, b_vt, dl, b_dl, a, hf)
                        yield
                P.dma("pool", SbD[h, t0 // 64:t0 // 64 + 8].rearrange("c p n -> p c n"), sg_[:], reads=[b_sg])

        def fwd_gen(h, B, bo):
            rs = slice(h * 128, (h + 1) * 128)
            St, b_St = B["St"]
            P.op("dve", lambda e: e.memset(St[:], 0.0), writes=[b_St])
            sfi = 0
            sf, b_sf = B["Sfb"][0]
            P.op("act", lambda e: e.activation(out=sf[:], in_=St[:], func=AF.Copy), reads=[b_St], writes=[b_sf])
            for b4 in range(nb4):
                t0 = b4 * 512
                ke, b_ke = B["kes"][b4 % 2]
                ki, b_ki = B["kis"][b4 % 2]
                qd, b_qd = B["qds"][b4 % 2]
                vt, b_vt = B["vts"][b4 % 2]
                gt_, b_gt = B["gts"][b4 % 2]
                og, b_og = B["ogs"][b4 % 2]
                dl, b_dl = B["dls"][b4 % 2]
                sb_, b_sb = B["sbl"][b4 % 2]
                P.dma("sp", ke[:], hke[0, rs, t0:t0 + 512], writes=[b_ke])
                for d in range(2):
                    P.dma("sp", ki[:, d, :], hki[d, rs, t0:t0 + 512], writes=[b_ki])
                    P.dma("sp", qd[:, d, :], hq[d, rs, t0:t0 + 512], writes=[b_qd])
                P.dma("sp", vt[:], vD[t0:t0 + 512, rs].rearrange("(a p) n -> p a n", p=128), writes=[b_vt])
                P.dma("sp", gt_[:], gsT[rs, t0:t0 + 512], writes=[b_gt])
                P.dma("sp", dl[:], hdl[0, rs, t0 // 64:t0 // 64 + 8], writes=[b_dl])
                P.dma("sp", sb_[:], SbD[h, t0 // 64:t0 // 64 + 8].rearrange("c p n -> p c n"), writes=[b_sb])
                yield
                for a in range(4):
                    B["cnt"] += 1
                    n = B["cnt"]
                    ts = slice(a * 128, (a + 1) * 128)
                    pt, b_pt = B["pts"][n % 2]
                    on, b_on = B["ons"][n % 2]
                    ss, b_ss = B["sss"][n % 2]
                    b1 = bo
                    for d in range(2):
                        P.op("pe", lambda e: e.matmul(kb.ps[b1][:, 128 + d * 128:128 + (d + 1) * 128], lhsT=ki[:, d, ts], rhs=qd[:, d, ts],
                                                      start=True, stop=True), reads=[b_ki, b_qd], writes=[kb.pbuf[b1]], signal=(d == 1))
                    kt, b_kt = ktrans(B, ke, b_ke, a)
                    yield
                    P.op("dve", lambda e: e.tensor_tensor(out=pt[:], in0=kb.ps[b1][:, 128:384], in1=mk[:], op=ALU.mult),
                         reads=[kb.pbuf[b1], b_mk], writes=[b_pt])
                    yield
                    P.op("pe", lambda e: e.matmul(kb.ps[bo][:, 0:128], lhsT=pt[:, 0:128], rhs=vt[:, a, :], start=True, stop=False),
                         reads=[b_pt, b_vt], writes=[kb.pbuf[bo]], signal=False)
                    P.op("pe", lambda e: e.matmul(kb.ps[bo][:, 0:128], lhsT=pt[:, 128:256], rhs=vt[:, a, :], start=False, stop=False),
                         reads=[b_pt, b_vt], writes=[kb.pbuf[bo]], signal=False)
                    for hf in range(2):
                        ci = a * 2 + hf
                        lo = hf * 64
                        P.op("pe", lambda e: e.matmul(kb.ps[bo][lo:lo + 64, 0:128], lhsT=qd[:, 1, a * 128 + lo:a * 128 + lo + 64],
                                                      rhs=sb_[:, ci, :], start=False, stop=False),
                             reads=[b_qd, b_sb], writes=[kb.pbuf[bo]], signal=False)
                    for hf in range(2):
                        ci = a * 2 + hf
                        lo = hf * 64
                        sf, b_sf = B["Sfb"][sfi % 2]
                        P.op("pe", lambda e: e.matmul(kb.ps[bo][lo:lo + 64, 0:128], lhsT=qd[:, 0, a * 128 + lo:a * 128 + lo + 64],
                                                      rhs=sf[:], start=False, stop=(hf == 1)),
                             reads=[b_qd, b_sf], writes=[kb.pbuf[bo]], signal=(hf == 1))
                        upd_chunk(B, kt, b_kt, vt, b_vt, dl, b_dl, a, hf)
                        sfi += 1
                        sf2, b_sf2 = B["Sfb"][sfi % 2]
                        P.op("act", lambda e: e.activation(out=sf2[:], in_=St[:], func=AF.Copy), reads=[b_St], writes=[b_sf2])
                        yield
                    P.op("act", lambda e: e.activation(out=on[:], in_=kb.ps[bo][:, 0:128], func=AF.Square, accum_out=ss[:, 0:1]),
                         reads=[kb.pbuf[bo]], writes=[b_on, b_ss])
                    P.op("act", lambda e: e.activation(out=ss[:, 1:2], in_=ss[:, 0:1], func=AF.Sqrt, scale=1.0 / 128, bias=kb.epsc[:, 0:1]),
                         reads=[b_ss], writes=[b_ss])
                    yield
                    P.op("dve", lambda e: e.reciprocal(out=ss[:, 1:2], in_=ss[:, 1:2]), reads=[b_ss], writes=[b_ss])
                    P.op("dve", lambda e: e.tensor_scalar(out=on[:], in0=kb.ps[bo][:, 0:128], scalar1=ss[:, 1:2], scalar2=None, op0=ALU.mult),
                         reads=[kb.pbuf[bo], b_ss], writes=[b_on])
                    yield
                    bt = bo
                    P.op("pe", lambda e: e.transpose(out=kb.psb[bt][:, 768:896], in_=on[:], identity=kb.identb[:]),
                         reads=[b_on, kb.b_ident], writes=[kb.pbuf[bt]])
                    yield
                    P.op("dve", lambda e: e.scalar_tensor_tensor(out=og[:, ts], in0=kb.psb[bt][:, 768:896], scalar=gcol[:, 0:1], in1=gt_[:, ts],
                                                                 op0=ALU.mult, op1=ALU.mult), reads=[kb.pbuf[bt], b_gc, b_gt], writes=[b_og])
                P.dma("pool", ogT[rs, t0:t0 + 512], og[:], reads=[b_og])

        for h0 in range(0, 8, NWAY):
            interleave([bwd_gen(h0 + i, BUFS[i]) for i in range(NWAY)])
            P.flush()
            interleave([fwd_gen(h0 + i, BUFS[i], 8 - NWAY + i) for i in range(NWAY)])
            P.flush()
        kb.reserved = set()

    lt_residual(kb, ogT[:, 0:L], 1024, L, W["hg_w_out"][0], Xsrc, Xdst, l, 0, s)


MIXERS[3] = mixer_hgrn


def fft_f1(kb, src, C, L, A):
    P = kb.P
    N1 = 2 * L // 128
    K = N1 // 2
    NB = 4096 // C
    F1t = kb.C[f"F1_{N1}"]
    with ExitStack() as es:
        f1 = kb.sb(es, [K, 2, N1], BF16)
        b_f1 = P.buf()
        P.dma("sp", f1[:], F1t, writes=[b_f1])
        xs = [(kb.sb(es, [K, NB * C], BF16), P.buf()) for _ in range(2)]
        As = [(kb.sb(es, [N1, 2, NB * C], BF16), P.buf()) for _ in range(2)]
        srcv = src.rearrange("(a b) c -> a b c", b=128)
        n = 0
        for i, n2c in enumerate(range(0, 128, NB)):
            x, b_x = xs[i % 2]
            At, b_A = As[i % 2]
            P.dma("sp", x[:].rearrange("p (b c) -> p b c", c=C), srcv[:, n2c:n2c + NB, :], writes=[b_x])
            for c0 in range(0, NB * C, 512):
                for ri in range(2):
                    bk = kb.bank()
                    P.op("pe", lambda e: e.matmul(kb.ps[bk][0:N1, :], lhsT=f1[:, ri, :], rhs=x[:, c0:c0 + 512], start=True, stop=True),
                         reads=[b_f1, b_x], writes=[kb.pbuf[bk]])
                    n += 1
                    if n % 2:
                        P.op("act", lambda e: e.activation(out=At[:, ri, c0:c0 + 512], in_=kb.ps[bk][0:N1, :], func=AF.Copy),
                             reads=[kb.pbuf[bk]], writes=[b_A])
                    else:
                        P.op("dve", lambda e: e.tensor_copy(out=At[:, ri, c0:c0 + 512], in_=kb.ps[bk][0:N1, :]),
                             reads=[kb.pbuf[bk]], writes=[b_A])
            for ri in range(2):
                P.dma("pool", A[ri, 0:N1, n2c:n2c + NB, 0:C], At[:, ri, :].rearrange("p (b c) -> p b c", c=C), reads=[b_A])
        P.flush()


def fft_f2(kb, A, C, L, mode, Hd, rn=None, Bd=None):
    P = kb.P
    N1 = 2 * L // 128
    Gt, CTt = kb.C[f"G_{N1}"], kb.C[f"CT_{N1}"]
    with ExitStack() as es:
        NB_ = 3
        Ats = [(kb.sb(es, [128, 2, C], BF16), P.buf()) for _ in range(NB_)]
        Gs = [(kb.sb(es, [128, 4, 128], BF16), P.buf()) for _ in range(NB_)]
        Hs = [(kb.sb(es, [128, 2, 1024], BF16), P.buf()) for _ in range(NB_)]
        if mode == "conv":
            Cs = [(kb.sb(es, [128, 3, 128], BF16), P.buf()) for _ in range(NB_)]
            Ts = [(kb.sb(es, [128, 4, 512], F32), P.buf()) for _ in range(2)]
            Ys = [(kb.sb(es, [128, 2, 512], BF16), P.buf()) for _ in range(2)]
            Bs = [(kb.sb(es, [128, 2, C], BF16), P.buf()) for _ in range(NB_)]
        ncb = (C if mode == "conv" else 1024) // 512
        items = [(k1, cb) for k1 in range(N1) for cb in range(ncb)]
        st = {}

        def loads(k1):
            At, b_A = Ats[k1 % NB_]
            G, b_G = Gs[k1 % NB_]
            for ri in range(2):
                P.dma("sp", At[:, ri, :], A[ri, k1, :, 0:C], writes=[b_A])
            P.dma("sp", G[:], Gt[k1], writes=[b_G])
            if mode == "conv":
                Ct, b_C = Cs[k1 % NB_]
                Ht, b_H = Hs[k1 % NB_]
                P.dma("sp", Ct[:], CTt[k1], writes=[b_C])
                for ri in range(2):
                    P.dma("sp", Ht[:, ri, :], Hd[ri, k1], writes=[b_H])

        def S1(n):
            k1, cb = items[n]
            c0 = cb * 512
            At, b_A = Ats[k1 % NB_]
            G, b_G = Gs[k1 % NB_]
            br, bi = kb.bank(), kb.bank()
            st[n] = (br, bi)
            if mode == "conv":
                seq_r = [(0, 0, c0), (2, 1, c0)]
                seq_i = [(1, 0, c0), (0, 1, c0)]
            else:
                seq_r = [(0, 0, c0), (2, 1, c0), (0, 0, 1024 + c0), (2, 1, 1024 + c0)]
                seq_i = [(1, 0, c0), (0, 1, c0), (2, 0, 1024 + c0), (3, 1, 1024 + c0)]
            for bk, seq in ((br, seq_r), (bi, seq_i)):
                for i, (gi_, ri, cc) in enumerate(seq):
                    last = (i == len(seq) - 1)
                    P.op("pe", lambda e: e.matmul(kb.ps[bk], lhsT=G[:, gi_, :], rhs=At[:, ri, cc:cc + 512], start=(i == 0), stop=last),
                         reads=[b_G, b_A], writes=[kb.pbuf[bk]], signal=last)

        def S2(n):
            k1, cb = items[n]
            c0 = cb * 512
            br, bi = st[n]
            Ht, b_H = Hs[k1 % NB_]
            if mode == "conv":
                T, b_T = Ts[n % 2]
                Y, b_Y = Ys[n % 2]
                hr, hi = Ht[:, 0, c0:c0 + 512], Ht[:, 1, c0:c0 + 512]
                P.op("dve", lambda e: e.tensor_tensor(out=T[:, 0, :], in0=kb.ps[br], in1=hr, op=ALU.mult), reads=[kb.pbuf[br], b_H], writes=[b_T])
                P.op("dve", lambda e: e.tensor_tensor(out=T[:, 1, :], in0=kb.ps[bi], in1=hi, op=ALU.mult), reads=[kb.pbuf[bi], b_H], writes=[b_T])
                P.op("dve", lambda e: e.tensor_tensor(out=T[:, 2, :], in0=kb.ps[br], in1=hi, op=ALU.mult), reads=[kb.pbuf[br], b_H], writes=[b_T])
                P.op("dve", lambda e: e.tensor_tensor(out=T[:, 3, :], in0=kb.ps[bi], in1=hr, op=ALU.mult), reads=[kb.pbuf[bi], b_H], writes=[b_T])
                P.op("pool", lambda e: e.tensor_tensor(out=Y[:, 0, :], in0=T[:, 0, :], in1=T[:, 1, :], op=ALU.subtract), reads=[b_T], writes=[b_Y])
                P.op("pool", lambda e: e.tensor_tensor(out=Y[:, 1, :], in0=T[:, 2, :], in1=T[:, 3, :], op=ALU.add), reads=[b_T], writes=[b_Y])
            else:
                for ri, bk in ((0, br), (1, bi)):
                    P.op("dve", lambda e: e.tensor_tensor(out=Ht[:, ri, c0:c0 + 512], in0=kb.ps[bk], in1=rn[0][:, c0:c0 + 512], op=ALU.mult),
                         reads=[kb.pbuf[bk], rn[1]], writes=[b_H])
                if cb == ncb - 1:
                    for ri in range(2):
                        P.dma("pool", Hd[ri, k1], Ht[:, ri, :], reads=[b_H])

        def S3(n):
            k1, cb = items[n]
            c0 = cb * 512
            Ct, b_C = Cs[k1 % NB_]
            Bt, b_B = Bs[k1 % NB_]
            Y, b_Y = Ys[n % 2]
            b2r, b2i = kb.bank(), kb.bank()
            P.op("pe", lambda e: e.matmul(kb.ps[b2r], lhsT=Ct[:, 0, :], rhs=Y[:, 0, :], start=True, stop=False),
                 reads=[b_C, b_Y], writes=[kb.pbuf[b2r]], signal=False)
            P.op("pe", lambda e: e.matmul(kb.ps[b2r], lhsT=Ct[:, 1, :], rhs=Y[:, 1, :], start=False, stop=True),
                 reads=[b_C, b_Y], writes=[kb.pbuf[b2r]])
            P.op("pe", lambda e: e.matmul(kb.ps[b2i], lhsT=Ct[:, 0, :], rhs=Y[:, 1, :], start=True, stop=False),
                 reads=[b_C, b_Y], writes=[kb.pbuf[b2i]], signal=False)
            P.op("pe", lambda e: e.matmul(kb.ps[b2i], lhsT=Ct[:, 2, :], rhs=Y[:, 0, :], start=False, stop=True),
                 reads=[b_C, b_Y], writes=[kb.pbuf[b2i]])
            P.op("act", lambda e: e.activation(out=Bt[:, 0, c0:c0 + 512], in_=kb.ps[b2r], func=AF.Copy), reads=[kb.pbuf[b2r]], writes=[b_B])
            P.op("act", lambda e: e.activation(out=Bt[:, 1, c0:c0 + 512], in_=kb.ps[b2i], func=AF.Copy), reads=[kb.pbuf[b2i]], writes=[b_B])
            if cb == ncb - 1:
                for ri in range(2):
                    P.dma("pool", Bd[ri, k1, :, 0:C], Bt[:, ri, :], reads=[b_B])

        loads(0)
        if N1 > 1:
            loads(1)
        S1(0)
        for n in range(len(items)):
            k1, cb = items[n]
            if cb == 0 and k1 + 2 < N1:
                loads(k1 + 2)
            if n + 1 < len(items):
                S1(n + 1)
            S2(n)
            if mode == "conv":
                S3(n)
            del st[n]
        P.flush()


def fft_f3(kb, Bd, C, L, ytm):
    P = kb.P
    N1 = 2 * L // 128
    M = N1 // 2
    NB = 4096 // C
    with ExitStack() as es:
        fi = kb.sb(es, [N1, 2, M], BF16)
        b_fi = P.buf()
        P.dma("sp", fi[:], kb.C[f"Finv_{N1}"], writes=[b_fi])
        Bs = [(kb.sb(es, [N1, 2, NB * C], BF16), P.buf()) for _ in range(2)]
        ys = [(kb.sb(es, [M, NB * C], F32), P.buf()) for _ in range(2)]
        yv = ytm.rearrange("(a b) c -> a b c", b=128)
        n = 0
        for i, n2c in enumerate(range(0, 128, NB)):
            Bt, b_B = Bs[i % 2]
            y, b_y = ys[i % 2]
            for ri in range(2):
                P.dma("sp", Bt[:, ri, :].rearrange("p (b c) -> p b c", c=C), Bd[ri, 0:N1, n2c:n2c + NB, 0:C], writes=[b_B])
            for c0 in range(0, NB * C, 512):
                bk = kb.bank()
                P.op("pe", lambda e: e.matmul(kb.ps[bk][0:M, :], lhsT=fi[:, 0, :], rhs=Bt[:, 0, c0:c0 + 512], start=True, stop=False),
                     reads=[b_fi, b_B], writes=[kb.pbuf[bk]], signal=False)
                P.op("pe", lambda e: e.matmul(kb.ps[bk][0:M, :], lhsT=fi[:, 1, :], rhs=Bt[:, 1, c0:c0 + 512], start=False, stop=True),
                     reads=[b_fi, b_B], writes=[kb.pbuf[bk]])
                n += 1
                if n % 2:
                    P.op("act", lambda e: e.activation(out=y[:, c0:c0 + 512], in_=kb.ps[bk][0:M, :], func=AF.Copy), reads=[kb.pbuf[bk]], writes=[b_y])
                else:
                    P.op("dve", lambda e: e.tensor_copy(out=y[:, c0:c0 + 512], in_=kb.ps[bk][0:M, :]), reads=[kb.pbuf[bk]], writes=[b_y])
            P.dma("pool", yv[:, n2c:n2c + NB, :], y[:].rearrange("p (b c) -> p b c", c=C), reads=[b_y])
        P.flush()


def hyena_taps(kb, W, L, s):
    P = kb.P
    Lm = max(kb.Ls)
    N1 = 2 * L // 128
    tD = kb.scratch("hy_taps", [Lm, 2048], BF16)
    A = kb.scratch("hy_A", [2, 2 * Lm // 128, 128, 2048], BF16)
    Hd = kb.scratch(f"hy_H{s}", [2, N1, 128, 1024], BF16)
    kb.Hd[s] = Hd
    ZT = kb.C[f"ZT_{L}"]
    negT = kb.C[f"negT_{L}"]
    rn = kb.sb(None, [128, 1024], F32, "hy_rn")
    b_rn = Buf()
    with ExitStack() as es:
        zt = kb.sb(es, [33, L], F32)
        b_zt = P.buf()
        P.dma("sp", zt[:], ZT, writes=[b_zt])
        w1 = kb.sb(es, [33, 64], F32)
        w2 = kb.sb(es, [64, 64], F32)
        w3 = kb.sb(es, [64, 2048], F32)
        cols = kb.sb(es, [64, 3], F32)
        b_w = P.buf()
        P.dma("sp", w1[:], W["hy_w1"][0], writes=[b_w])
        P.dma("sp", w2[:], W["hy_w2"][0], writes=[b_w])
        P.dma("sp", w3[:], W["hy_w3"][0], writes=[b_w])
        load_col(P, cols[:, 0:1], W["hy_b1"][0], b_w, 64)
        load_col(P, cols[:, 1:2], W["hy_b2"][0], b_w, 64)
        load_col(P, cols[:, 2:3], W["hy_freq"][0], b_w, 64)
        nt = kb.sb(es, [128, L // 128], F32)
        P.dma("sp", nt[:], negT, writes=[b_w])
        adec = kb.sb(es, [128, 2048], F32)
        dd = W["hy_decay"][0:1]
        P.dma("sp", adec[:], AP(dd.tensor, dd.offset, [[0, 128], [1, 2048]]), writes=[b_w])
        P.op("act", lambda e: e.activation(out=adec[:], in_=adec[:], func=AF.Abs), reads=[b_w], writes=[b_w])
        onesf = kb.sb(es, [128, 128], BF16)
        P.op("dve", lambda e: e.memset(onesf[:], 1.0), writes=[b_w])
        tabs_ = [(kb.sb(es, [128, 2048], BF16), P.buf()) for _ in range(2)]
        h1 = kb.sb(es, [64, L], F32)
        h2 = kb.sb(es, [64, L], F32)
        b_h1, b_h2 = P.buf(), P.buf()
        arg = [(kb.sb(es, [64, 512], F32), P.buf()) for _ in range(2)]
        ki = [(kb.sb(es, [64, 512], I32), P.buf()) for _ in range(2)]
        kf = [(kb.sb(es, [64, 512], F32), P.buf()) for _ in range(2)]
        TWO_PI = 2.0 * math.pi

        def sin_layer(wt, src, b_src, bcol, dst, b_dst, Kdim):
            for i, c0 in enumerate(range(0, L, 512)):
                a_, b_a = arg[i % 2]
                k_, b_k = ki[i % 2]
                f_, b_f = kf[i % 2]
                bk = i % 4
                P.op("pe", lambda e: e.matmul(kb.ps[bk][0:64, :], lhsT=wt[:], rhs=src[0:Kdim, c0:c0 + 512], start=True, stop=True),
                     reads=[b_w, b_src], writes=[kb.pbuf[bk]])
                P.op("dve", lambda e: e.tensor_scalar(out=a_[:], in0=kb.ps[bk][0:64, :], scalar1=cols[:, bcol:bcol + 1], scalar2=cols[:, 2:3],
                                                      op0=ALU.add, op1=ALU.mult), reads=[kb.pbuf[bk], b_w], writes=[b_a])
                P.op("dve", lambda e: e.tensor_scalar(out=k_[:], in0=a_[:], scalar1=1.0 / TWO_PI, scalar2=None, op0=ALU.mult), reads=[b_a], writes=[b_k])
                P.op("dve", lambda e: e.tensor_copy(out=f_[:], in_=k_[:]), reads=[b_k], writes=[b_f])
                P.op("dve", lambda e: e.scalar_tensor_tensor(out=a_[:], in0=f_[:], scalar=-TWO_PI, in1=a_[:], op0=ALU.mult, op1=ALU.add),
                     reads=[b_f, b_a], writes=[b_a])
                P.op("dve", lambda e: e.tensor_scalar(out=f_[:], in0=a_[:], scalar1=math.pi, scalar2=-TWO_PI, op0=ALU.is_gt, op1=ALU.mult), reads=[b_a], writes=[b_f])
                P.op("dve", lambda e: e.tensor_tensor(out=a_[:], in0=a_[:], in1=f_[:], op=ALU.add), reads=[b_a, b_f], writes=[b_a])
                P.op("dve", lambda e: e.tensor_scalar(out=f_[:], in0=a_[:], scalar1=-math.pi, scalar2=TWO_PI, op0=ALU.is_lt, op1=ALU.mult), reads=[b_a], writes=[b_f])
                P.op("dve", lambda e: e.tensor_tensor(out=a_[:], in0=a_[:], in1=f_[:], op=ALU.add), reads=[b_a, b_f], writes=[b_a])
                P.op("dve", lambda e: e.tensor_scalar(out=a_[:], in0=a_[:], scalar1=3.14159, scalar2=-3.14159, op0=ALU.min, op1=ALU.max), reads=[b_a], writes=[b_a])
                P.op("act", lambda e: e.activation(out=dst[:, c0:c0 + 512], in_=a_[:], func=AF.Sin), reads=[b_a], writes=[b_dst])

        sin_layer(w1, zt, b_zt, 0, h1, b_h1, 33)
        sin_layer(w2, h1, b_h1, 1, h2, b_h2, 64)
        wins = [(kb.sb(es, [128, 2048], F32), P.buf()) for _ in range(2)]
        tps = [(kb.sb(es, [128, 2048], F32), P.buf()) for _ in range(2)]
        tbs = [(kb.sb(es, [128, 2048], BF16), P.buf()) for _ in range(2)]
        nti = L // 128
        for ti in range(nti):
            wn, b_wn = wins[ti % 2]
            tp, b_tp = tps[ti % 2]
            tb, b_tb = tbs[ti % 2]
            P.op("act", lambda e: e.activation(out=wn[:], in_=adec[:], func=AF.Exp, scale=nt[:, ti:ti + 1]), reads=[b_w], writes=[b_wn])
            for nb in range(4):
                bk = nb
                P.op("pe", lambda e: e.matmul(kb.ps[bk], lhsT=h2[:, ti * 128:(ti + 1) * 128], rhs=w3[:, nb * 512:(nb + 1) * 512], start=True, stop=True),
                     reads=[b_h2, b_w], writes=[kb.pbuf[bk]])
                P.op("dve", lambda e: e.tensor_tensor(out=tp[:, nb * 512:(nb + 1) * 512], in0=kb.ps[bk], in1=wn[:, nb * 512:(nb + 1) * 512], op=ALU.mult),
                     reads=[kb.pbuf[bk], b_wn], writes=[b_tp])
            if ti == 0:
                P.op("dve", lambda e: e.memset(tp[0:1, 1024:2048], 0.0), writes=[b_tp])
            P.op("act", lambda e: e.activation(out=tb[:], in_=tp[:], func=AF.Copy), reads=[b_tp], writes=[b_tb])
            P.dma("sp", tD[ti * 128:(ti + 1) * 128, :], tb[:], reads=[b_tb])
            ta, b_ta = tabs_[ti % 2]
            P.op("act", lambda e: e.activation(out=ta[:], in_=tp[:], func=AF.Abs), reads=[b_tp], writes=[b_ta])
            for nb in range(4):
                bk = 4 + nb
                P.op("pe", lambda e: e.matmul(kb.ps[bk], lhsT=onesf[:], rhs=ta[:, nb * 512:(nb + 1) * 512], start=(ti == 0), stop=(ti == nti - 1)),
                     reads=[b_w, b_ta], writes=[kb.pbuf[bk]], signal=(ti == nti - 1 or nb == 3))
        for hf in range(2):
            P.op("act", lambda e: e.activation(out=rn[:, hf * 512:(hf + 1) * 512], in_=kb.ps[6 + hf], func=AF.Copy), reads=[kb.pbuf[6 + hf]], writes=[b_rn])
            P.op("dve", lambda e: e.tensor_tensor(out=rn[:, hf * 512:(hf + 1) * 512], in0=kb.ps[4 + hf], in1=rn[:, hf * 512:(hf + 1) * 512], op=ALU.add),
                 reads=[kb.pbuf[4 + hf], b_rn], writes=[b_rn])
        P.op("dve", lambda e: e.reciprocal(out=rn[:], in_=rn[:]), reads=[b_rn], writes=[b_rn])
        P.flush()
    fft_f1(kb, tD[0:L, :], 2048, L, A)
    fft_f2(kb, A, 2048, L, "H", Hd, rn=(rn, b_rn))


def mixer_hyena(kb, W, hT, Xsrc, Xdst, L, l, s):
    P = kb.P
    Lm = max(kb.Ls)
    win = W["hy_w_in"][0]
    if not hasattr(kb, "Hd"):
        kb.Hd = {}
    if s not in kb.Hd:
        hyena_taps(kb, W, L, s)
    x0T = kb.scratch("hy_x0T", [1024, Lm], F32)
    uT = kb.scratch("hy_uT", [1024, Lm], F32)
    utm = kb.scratch("hy_utm", [Lm, 1024], BF16)
    A = kb.scratch("hy_A", [2, 2 * Lm // 128, 128, 2048], BF16)
    Bd = kb.scratch("hy_B", [2, 2 * Lm // 128, 128, 1024], BF16)
    ytm = kb.scratch("hy_ytm", [Lm, 1024], F32)
    mT = kb.scratch("hy_mT", [1024, Lm], BF16)

    groups = [[(win, j * 128)] for j in range(8)]
    groups += [[(win, 1024 + j * 128), (win, 2048 + j * 128)] for j in range(8)]

    def setup(es):
        c = dict()
        c["cw"] = kb.sb(es, [128, 4, 24], F32)
        c["b_cw"] = P.buf()
        for jj in range(3):
            load_col(P, c["cw"][:, jj, :], W["hy_conv_w"][0, jj], c["b_cw"])
        load_col(P, c["cw"][:, 3, :], W["hy_conv_b"][0], c["b_cw"])
        c["r"] = [(kb.sb(es, [128, 2048], F32), P.buf()) for _ in range(5)]
        c["ub"] = [(kb.sb(es, [128, 2048], BF16), P.buf()) for _ in range(2)]
        c["ut"] = [(kb.sb(es, [128, 16, 128], BF16), P.buf()) for _ in range(2)]
        c["n"] = 0
        return c

    def conv(c, g, b_g, ch, r, b_r, Ls):
        cw = c["cw"]
        P.op("act", lambda e: e.activation(out=r[:, 0:Ls], in_=g[:, 1:Ls + 1], func=AF.Identity,
                                           scale=cw[:, 1, ch:ch + 1], bias=cw[:, 3, ch:ch + 1]), reads=[b_g, c["b_cw"]], writes=[b_r])
        P.op("dve", lambda e: e.scalar_tensor_tensor(out=r[:, 0:Ls], in0=g[:, 0:Ls], scalar=cw[:, 0, ch:ch + 1], in1=r[:, 0:Ls],
                                                     op0=ALU.mult, op1=ALU.add), reads=[b_g, c["b_cw"], b_r], writes=[b_r])
        P.op("dve", lambda e: e.scalar_tensor_tensor(out=r[:, 0:Ls], in0=g[:, 2:Ls + 2], scalar=cw[:, 2, ch:ch + 1], in1=r[:, 0:Ls],
                                                     op0=ALU.mult, op1=ALU.add), reads=[b_g, c["b_cw"], b_r], writes=[b_r])

    def epi(kb, c, gi, rows, t0, Ls):
        c["n"] += 1
        n = c["n"]
        if gi < 8:
            (g, b_g), = rows
            r, b_r = c["r"][0]
            conv(c, g, b_g, gi, r, b_r, Ls)
            P.dma("pool", x0T[gi * 128:(gi + 1) * 128, t0:t0 + Ls], r[:, 0:Ls], reads=[b_r])
            return
        j = gi - 8
        (g1, b_g1), (g2, b_g2) = rows
        r1, b_r1 = c["r"][1 + 2 * (gi % 2)]
        r2, b_r2 = c["r"][2 + 2 * (gi % 2)]
        conv(c, g1, b_g1, 8 + j, r1, b_r1, Ls)
        conv(c, g2, b_g2, 16 + j, r2, b_r2, Ls)
        ub, b_ub = c["ub"][n % 2]
        ut, b_ut = c["ut"][n % 2]
        P.op("pool", lambda e: e.tensor_tensor(out=r1[:, 0:Ls], in0=r1[:, 0:Ls], in1=r2[:, 0:Ls], op=ALU.mult), reads=[b_r1, b_r2], writes=[b_r1])
        P.dma("pool", uT[j * 128:(j + 1) * 128, t0:t0 + Ls], r1[:, 0:Ls], reads=[b_r1])
        P.op("act", lambda e: e.activation(out=ub[:, 0:Ls], in_=r1[:, 0:Ls], func=AF.Copy), reads=[b_r1], writes=[b_ub])
        nbk = Ls // 128
        for q0 in range(0, nbk, 8):
            bt = kb.bank()
            for q in range(q0, min(q0 + 8, nbk)):
                P.op("pe", lambda e: e.transpose(out=kb.psb[bt][:, (q - q0) * 128:(q - q0 + 1) * 128], in_=ub[:, q * 128:(q + 1) * 128],
                                                 identity=kb.identb[:]), reads=[b_ub, kb.b_ident], writes=[kb.pbuf[bt]], signal=(q == min(q0 + 8, nbk) - 1))
            nq = min(q0 + 8, nbk) - q0
            P.op("act", lambda e: e.activation(out=ut[:, q0:q0 + nq, :], in_=kb.psb[bt][:, 0:nq * 128].rearrange("p (q c) -> p q c", c=128),
                                               func=AF.Copy), reads=[kb.pbuf[bt]], writes=[b_ut])
        P.dma("sp", utm[t0:t0 + Ls, j * 128:(j + 1) * 128].rearrange("(q p) c -> p q c", p=128), ut[:, 0:nbk, :], reads=[b_ut])

    phase_lf(kb, hT, 1024, L, groups, epi, halo=1, setup=setup)
    fft_f1(kb, utm[0:L, :], 1024, L, A)
    fft_f2(kb, A, 1024, L, "conv", kb.Hd[s], Bd=Bd)
    fft_f3(kb, Bd, 1024, L, ytm[0:L, :])

    with ExitStack() as es:
        skc = kb.sb(es, [128, 8], F32)
        b_sk = P.buf()
        load_col(P, skc[:], W["hy_skip"][0], b_sk)
        ys = [(kb.sb(es, [128, 4, 1024], F32), P.buf()) for _ in range(2)]
        us = [(kb.sb(es, [128, 8, 512], F32), P.buf()) for _ in range(2)]
        xs = [(kb.sb(es, [128, 8, 512], F32), P.buf()) for _ in range(2)]
        ms = [(kb.sb(es, [128, 8, 512], BF16), P.buf()) for _ in range(2)]
        tmps = [(kb.sb(es, [128, 512], F32), P.buf()) for _ in range(2)]
        for ti in range(L // 512):
            t0 = ti * 512
            y, b_y = ys[ti % 2]
            u, b_u = us[ti % 2]
            x, b_x = xs[ti % 2]
            m, b_m = ms[ti % 2]
            P.dma("sp", y[:], ytm[t0:t0 + 512, :].rearrange("(a p) c -> p a c", p=128), writes=[b_y])
            P.dma("sp", u[:], uT[:, t0:t0 + 512].rearrange("(k p) t -> p k t", p=128), writes=[b_u])
            P.dma("sp", x[:], x0T[:, t0:t0 + 512].rearrange("(k p) t -> p k t", p=128), writes=[b_x])
            for k in range(8):
                bk = kb.bank()
                for a in range(4):
                    P.op("pe", lambda e: e.transpose(out=kb.ps[bk][:, a * 128:(a + 1) * 128], in_=y[:, a, k * 128:(k + 1) * 128], identity=kb.identf[:]),
                         reads=[b_y, kb.b_ident], writes=[kb.pbuf[bk]], signal=(a == 3))
                tmp, b_tmp = tmps[k % 2]
                P.op("dve", lambda e: e.scalar_tensor_tensor(out=tmp[:], in0=u[:, k, :], scalar=skc[:, k:k + 1], in1=kb.ps[bk], op0=ALU.mult, op1=ALU.add),
                     reads=[b_u, b_sk, kb.pbuf[bk]], writes=[b_tmp])
                P.op("pool", lambda e: e.tensor_tensor(out=m[:, k, :], in0=tmp[:], in1=x[:, k, :], op=ALU.mult), reads=[b_tmp, b_x], writes=[b_m])
            P.dma("sp", mT[:, t0:t0 + 512].rearrange("(k p) t -> p k t", p=128), m[:], reads=[b_m])
        P.flush()

    lt_residual(kb, mT[:, 0:L], 1024, L, W["hy_w_out"][0], Xsrc, Xdst, l, 0, s)


MIXERS[0] = mixer_hyena


def kernel(**inputs):
    Ls = (2048, 8192)
    inputs = {k: np.asarray(v) for k, v in inputs.items()}
    kb = build(Ls)
    res = run(kb, inputs)
    y0 = np.stack([np.asarray(res[i]["y0"], dtype=np.float32) for i in range(8)])
    y1 = np.stack([np.asarray(res[i]["y1"], dtype=np.float32) for i in range(8)])
    return (y0, y1)
```

```python
import numpy as np
import ml_dtypes
import concourse.bass as bass
import concourse.mybir as mybir
from concourse.bass_utils import run_bass_kernel_spmd
from concourse.alu_op_type import AluOpType as ALU
from concourse.ap import AP

AF = mybir.ActivationFunctionType
F32 = mybir.dt.float32
BF16 = mybir.dt.bfloat16
I32 = mybir.dt.int32
AX = mybir.AxisListType

SEM_CAP = 30000


class Ev:
    __slots__ = ("sem", "val", "key", "eng")

    def __init__(self, sem, val, key, eng=""):
        self.sem, self.val, self.key, self.eng = sem, val, key, eng


class Buf:
    __slots__ = ("name", "w", "r", "dsem", "dkey", "dcnt")

    def __init__(self, name=""):
        self.name = name
        self.w = None
        self.r = {}
        self.dsem = None


class Eng:
    def __init__(self, P, name):
        self.P, self.name = P, name
        self.ops = []
        self.seen = {}
        self.sem = None
        self.cnt = 0
        self.last = None
        self.pending = False
        self.pend = []

    def flush_waits(self, keep_last=False):
        last = None
        if keep_last and self.pend:
            last = self.pend.pop()
        for sem, val in self.pend:
            self.ops.append(lambda be, sem=sem, val=val: be.wait_ge(sem, val))
        self.pend = []
        return last

    def _ensure(self):
        if self.sem is None or self.cnt >= SEM_CAP:
            assert not self.pending
            self.sem, self.key = self.P.new_sem(self.name)
            self.cnt = 0

    def need(self, ev):
        if ev is None:
            return
        if self.name == "pe" and ev.eng == "pe":
            return
        if self.seen.get(ev.key, 0) >= ev.val:
            return
        self.seen[ev.key] = ev.val
        self.pend.append((ev.sem, ev.val))


class _Rec:
    def __init__(self):
        self.call = None

    def __getattr__(self, name):
        def f(*a, **k):
            self.call = (name, a, k)
            return self
        return f


EMBED_WAITS = True


def _replay(call, wait=None):
    name, a, k = call
    if wait is None:
        return lambda be: getattr(be, name)(*a, **k)
    sem, val = wait
    return lambda be: getattr(be, name)(*a, **k)._wait_ge(sem, val)


class Prog:
    def __init__(self, nc):
        self.nc = nc
        self.eng = {n: Eng(self, n) for n in ("pe", "act", "dve", "pool", "sp")}
        self.nsem = 0
        self.dma_pool = []
        self.semval = {}
        self.live_dma = {}
        self.bufs = []

    def new_sem(self, tag):
        s = self.nc.alloc_semaphore(f"s{self.nsem}_{tag}")
        self.nsem += 1
        key = self.nsem
        self.semval[key] = 0
        return s, key

    def buf(self, name=""):
        b = Buf(name)
        self.bufs.append(b)
        return b

    def _dma_event(self, b):
        if b.dsem is None or self.semval[b.dkey] + 16 > SEM_CAP:
            got = None
            for i, (s, k) in enumerate(self.dma_pool):
                if self.semval[k] + 16 <= SEM_CAP:
                    got = self.dma_pool.pop(i)
                    break
            if got is None:
                got = self.new_sem("d")
            b.dsem, b.dkey = got
        self.semval[b.dkey] += 16
        ev = Ev(b.dsem, self.semval[b.dkey], b.dkey)
        self.live_dma[b.dkey] = ev
        return ev

    def _deps(self, E, reads, writes):
        for b in reads:
            E.need(b.w)
        for b in writes:
            E.need(b.w)
            for ev in b.r.values():
                E.need(ev)

    def op(self, en, fn, reads=(), writes=(), signal=True):
        E = self.eng[en]
        E._ensure()
        self._deps(E, reads, writes)
        w = E.flush_waits(keep_last=EMBED_WAITS)
        rec = _Rec()
        fn(rec)
        fn = _replay(rec.call, w)
        ev = Ev(E.sem, E.cnt + 1, E.key, en)
        if signal:
            sem = E.sem
            E.ops.append(lambda be: fn(be).then_inc(sem, 1))
            E.cnt += 1
            E.last = ev
            E.pending = False
        else:
            E.ops.append(fn)
            E.pending = True
        for b in reads:
            b.r[en] = ev
        for b in writes:
            b.w = ev
            b.r = {}
        return ev

    def dma(self, q, out, in_, reads=(), writes=(), **kw):
        E = self.eng[q]
        self._deps(E, reads, writes)
        E.flush_waits()
        b0 = (list(writes) + list(reads))[0]
        ev = self._dma_event(b0)
        sem = ev.sem
        E.ops.append(lambda be: be.dma_start(out=out, in_=in_, **kw).then_inc(sem, 16))
        for b in reads:
            b.r[("d", ev.key)] = ev
        for b in writes:
            b.w = ev
            b.r = {}
        return ev

    def barrier(self):
        evs = []
        for E in self.eng.values():
            assert not E.pending, E.name
            if E.last is not None:
                evs.append(E.last)
        evs.extend(self.live_dma.values())
        for E in self.eng.values():
            for ev in evs:
                E.need(ev)
            E.flush_waits()
        self.live_dma = {}
        for b in self.bufs:
            b.w = None
            b.r = {}
            if b.dsem is not None:
                self.dma_pool.append((b.dsem, b.dkey))
                b.dsem = None
        self.bufs = []

    def flush(self, final=False):
        nc = self.nc
        self.barrier()
        if not final and getattr(self, "defer_emit", False):
            return
        engs = self.eng
        with nc.Block() as block:
            @block.tensor
            def _(be):
                for f in self.eng["pe"].ops:
                    f(be)

            @block.scalar
            def _(be):
                for f in self.eng["act"].ops:
                    f(be)

            @block.vector
            def _(be):
                for f in self.eng["dve"].ops:
                    f(be)

            @block.gpsimd
            def _(be):
                for f in self.eng["pool"].ops:
                    f(be)

            @block.sync
            def _(be):
                for f in self.eng["sp"].ops:
                    f(be)

        for E in self.eng.values():
            E.ops = []
from contextlib import ExitStack
import math

D = 1024
DFF = 2816
EPS = 1e-6


class KB:
    def __init__(self, Ls, dbg=()):
        self.Ls = Ls
        self.dbg = set(dbg)
        nc = self.nc = bass.Bass("TRN2", target_bir_lowering=False)
        self.P = Prog(nc)
        self.din = {}
        self.outs = []
        ps = nc.alloc_psum_tensor("ps", [128, 4096], F32)
        psb = ps.bitcast(BF16)
        self.ps = [ps[:, b * 512:(b + 1) * 512] for b in range(8)]
        self.psb = [psb[:, b * 1024:(b + 1) * 1024] for b in range(8)]
        self.pbuf = [Buf(f"ps{b}") for b in range(8)]
        self.pb_i = 0
        self.uid = 0

    def inp(self, name, shape, dt=F32):
        t = self.nc.dram_tensor(name, list(shape), dt, kind="ExternalInput")
        self.din[name] = t
        return t.ap()

    def scratch(self, name, shape, dt):
        if not hasattr(self, "scr"):
            self.scr = {}
        if name in self.scr:
            return self.scr[name]
        self.scr[name] = self._scratch(name, shape, dt)
        return self.scr[name]

    def _scratch(self, name, shape, dt):
        kind = "ExternalOutput" if name in self.dbg else "Internal"
        t = self.nc.dram_tensor(name, list(shape), dt, kind=kind)
        if name in self.dbg:
            self.outs.append(name)
        return t.ap()

    def out(self, name, shape, dt=F32):
        t = self.nc.dram_tensor(name, list(shape), dt, kind="ExternalOutput")
        self.outs.append(name)
        return t.ap()

    def sb(self, es, shape, dt, name=None):
        self.uid += 1
        n = f"{name or 't'}_{self.uid}"
        if es is None:
            return self.nc.alloc_sbuf_tensor(n, list(shape), dt)
        return es.enter_context(self.nc.sbuf_tensor(n, list(shape), dt))

    def bank(self):
        res = getattr(self, "reserved", ())
        busy = getattr(self, "busy", ())
        for _ in range(9):
            b = self.pb_i
            self.pb_i = (b + 1) % 8
            if b not in res and b not in busy:
                return b
        raise RuntimeError("no free PSUM bank")

    def acq(self):
        if not hasattr(self, "busy"):
            self.busy = set()
        b = self.bank()
        self.busy.add(b)
        return b

    def rel(self, b):
        self.busy.discard(b)


def interleave(gens):
    gens = list(gens)
    while gens:
        nxt = []
        for g in gens:
            try:
                next(g)
                nxt.append(g)
            except StopIteration:
                pass
        gens = nxt


def load_col(P, dst, vec, b, np_=128):
    P.dma("sp", dst, vec.rearrange("(c p) -> p c", p=np_), writes=[b], allow_slow_non_contiguous=True)


def setup_consts(kb):
    nc, P = kb.nc, kb.P
    ident = kb.inp("ident", [128, 128])
    kb.identf = kb.sb(None, [128, 128], F32, "identf")
    kb.identb = kb.sb(None, [128, 128], BF16, "identb")
    kb.b_ident = Buf()
    P.dma("sp", kb.identf[:], ident, writes=[kb.b_ident])
    P.dma("pool", kb.identb[:], ident, writes=[kb.b_ident])
    kb.epsc = kb.sb(None, [128, 1], F32, "epsc")
    P.op("dve", lambda e: e.memset(kb.epsc[:], EPS), writes=[kb.b_ident])
    P.flush()


def phase_mod(kb, W):
    nc, P = kb.nc, kb.P
    nl = 4
    kb.modA = kb.sb(None, [128, nl, 2, 8, 2], F32, "modA")
    kb.modS = kb.sb(None, [128, nl, 2, 8, 2], F32, "modS")
    kb.gt = kb.scratch("gt", [nl, 2, 2, 128, 1024], F32)
    with ExitStack() as es:
        ccol = kb.sb(es, [128, 8, 2], F32)
        cs = kb.sb(es, [128, 8, 2], F32)
        csb = kb.sb(es, [128, 8, 2], BF16)
        csrep = kb.sb(es, [128, 8, 2, 128], BF16)
        bcol = kb.sb(es, [128, nl, 48], F32)
        gcol = kb.sb(es, [128, nl, 2, 8], F32)
        b_c = P.buf()
        for s in range(2):
            load_col(P, ccol[:, :, s], W["c"][s], b_c)
        for l in range(nl):
            load_col(P, bcol[:, l, :], W["ada_b"][l], b_c)
            for j in range(2):
                load_col(P, gcol[:, l, j, :], W["norm_g"][l, j], b_c)
        b_cs = P.buf()
        P.op("act", lambda e: e.activation(out=cs[:], in_=ccol[:], func=AF.Silu), reads=[b_c], writes=[b_cs])
        b_csb = P.buf()
        P.op("dve", lambda e: e.tensor_copy(out=csb[:], in_=cs[:]), reads=[b_cs], writes=[b_csb])
        P.op("dve", lambda e: e.tensor_copy(out=csrep[:], in_=AP(cs, 0, [[16, 128], [2, 8], [1, 2], [0, 128]])),
             reads=[b_cs], writes=[b_csb])
        wbs = [(kb.sb(es, [128, 8, 512], BF16), P.buf()) for _ in range(2)]
        bts = [(kb.sb(es, [128, 512], F32), P.buf()) for _ in range(2)]
        gts = [(kb.sb(es, [128, 512], F32), P.buf()) for _ in range(2)]
        b_mod = P.buf()
        it = 0
        for l in range(nl):
            for nb in range(12):
                wb, b_wb = wbs[it % 2]
                it += 1
                src = W["ada_w"][l, :, nb * 512:(nb + 1) * 512].rearrange("(k p) n -> p k n", p=128)
                P.dma("pool", wb[:], src, writes=[b_wb])
                blk = nb // 2
                if blk in (2, 5):
                    j = 0 if blk == 2 else 1
                    bt, b_bt = bts[nb % 2]
                    brow = W["ada_b"][l:l + 1, nb * 512:(nb + 1) * 512]
                    P.dma("sp", bt[:], AP(brow.tensor, brow.offset, [[0, 128], [1, 512]]), writes=[b_bt])
                    for s in range(2):
                        bk = kb.bank()
                        for k in range(8):
                            P.op("pe", lambda e, k=k, s=s, bk=bk, wb=wb: e.matmul(
                                kb.ps[bk], lhsT=csrep[:, k, s, :], rhs=wb[:, k, :], start=(k == 0), stop=(k == 7)),
                                reads=[b_csb, b_wb], writes=[kb.pbuf[bk]], signal=(k == 7))
                        g, b_g = gts[s]
                        P.op("dve", lambda e, bk=bk, g=g, bt=bt: e.tensor_tensor(out=g[:], in0=kb.ps[bk], in1=bt[:], op=ALU.add),
                             reads=[kb.pbuf[bk], b_bt], writes=[b_g])
                        c0 = (nb % 2) * 512
                        P.dma("sp", kb.gt[l, s, j, :, c0:c0 + 512], g[:], reads=[b_g])
                else:
                    j = 0 if blk < 2 else 1
                    is_sc = blk in (1, 4)
                    dst = kb.modA if is_sc else kb.modS
                    for q in range(4):
                        kc = (nb % 2) * 4 + q
                        bk = kb.bank()
                        for k in range(8):
                            P.op("pe", lambda e, k=k, q=q, bk=bk, wb=wb: e.matmul(
                                kb.ps[bk][:, 0:2], lhsT=wb[:, k, q * 128:(q + 1) * 128], rhs=csb[:, k, :],
                                start=(k == 0), stop=(k == 7)),
                                reads=[b_csb, b_wb], writes=[kb.pbuf[bk]], signal=(k == 7))
                        cidx = nb * 4 + q
                        P.op("dve", lambda e, bk=bk, dst=dst, l=l, j=j, kc=kc, cidx=cidx: e.tensor_scalar(
                            out=dst[:, l, j, kc, :], in0=kb.ps[bk][:, 0:2], scalar1=bcol[:, l, cidx:cidx + 1],
                            scalar2=None, op0=ALU.add), reads=[kb.pbuf[bk], b_c], writes=[b_mod])
        for l in range(nl):
            for j in range(2):
                for s in range(2):
                    P.op("dve", lambda e, l=l, j=j, s=s: e.scalar_tensor_tensor(
                        out=kb.modA[:, l, j, :, s], in0=kb.modA[:, l, j, :, s], scalar=1.0, in1=gcol[:, l, j, :],
                        op0=ALU.add, op1=ALU.mult), reads=[b_mod, b_c], writes=[b_mod])
        P.flush()


def phase_norm(kb, X, hT, L, l, j, s):
    nc, P = kb.nc, kb.P
    nt = L // 512
    with ExitStack() as es:
        NS = 3
        sets = []
        for i in range(NS):
            sets.append(dict(
                xt=kb.sb(es, [128, 4, 1024], F32), b_xt=P.buf(),
                xn=kb.sb(es, [128, 4, 1024], BF16), b_xn=P.buf(),
                ht=kb.sb(es, [128, 8, 512], BF16), b_ht=P.buf(),
                ss=kb.sb(es, [128, 4], F32), b_ss=P.buf(),
                rs=kb.sb(es, [128, 4], F32), b_rs=P.buf(),
            ))
        jk = kb.sb(es, [128, 1024], BF16)
        b_jk = P.buf()

        def stL(ti):
            S = sets[ti % NS]
            t0 = ti * 512
            P.dma("sp", S["xt"][:], X[t0:t0 + 512, :].rearrange("(a p) d -> p a d", p=128), writes=[S["b_xt"]])

        def stA(ti):
            S = sets[ti % NS]
            t0 = ti * 512
            xt, xn, ss, rs = S["xt"], S["xn"], S["ss"], S["rs"]
            for a in range(4):
                P.op("act", lambda e: e.activation(out=jk[:], in_=xt[:, a, :], func=AF.Square, accum_out=ss[:, a:a + 1]),
                     reads=[S["b_xt"]], writes=[b_jk, S["b_ss"]])
            P.op("act", lambda e: e.activation(out=rs[:], in_=ss[:], func=AF.Sqrt, scale=1.0 / D, bias=kb.epsc[:, 0:1]),
                 reads=[S["b_ss"]], writes=[S["b_rs"]])
            P.op("dve", lambda e: e.reciprocal(out=rs[:], in_=rs[:]), reads=[S["b_rs"]], writes=[S["b_rs"]])
            for a in range(4):
                eng = "dve" if a % 2 == 0 else "pool"
                P.op(eng, lambda e: e.tensor_scalar(out=xn[:, a, :], in0=xt[:, a, :], scalar1=rs[:, a:a + 1], scalar2=0.0,
                                                    op0=ALU.mult, op1=ALU.add), reads=[S["b_xt"], S["b_rs"]], writes=[S["b_xn"]])

        def stB(ti):
            S = sets[ti % NS]
            xn = S["xn"]
            bk0 = (ti % 2) * 4
            for a in range(4):
                bk = bk0 + a
                for k in range(8):
                    P.op("pe", lambda e: e.transpose(out=kb.psb[bk][:, k * 128:(k + 1) * 128], in_=xn[:, a, k * 128:(k + 1) * 128],
                                                     identity=kb.identb[:]),
                         reads=[S["b_xn"], kb.b_ident], writes=[kb.pbuf[bk]], signal=(k == 7))

        def stC(ti):
            S = sets[ti % NS]
            ht = S["ht"]
            t0 = ti * 512
            bk0 = (ti % 2) * 4
            pbs = [kb.pbuf[bk0 + a] for a in range(4)]
            for k in range(8):
                src = AP(kb.psb[bk0].tensor, kb.psb[bk0].offset + k * 128, [[8192, 128], [1024, 4], [1, 128]])
                dst = ht[:, k, :].rearrange("p (a t) -> p a t", a=4)
                Aap = kb.modA[:, l, j, k, s:s + 1]
                Sap = kb.modS[:, l, j, k, s:s + 1]
                if k % 2 == 0:
                    P.op("act", lambda e: e.activation(out=dst, in_=src, func=AF.Identity, scale=Aap, bias=Sap), reads=pbs, writes=[S["b_ht"]])
                else:
                    P.op("dve", lambda e: e.tensor_scalar(out=dst, in0=src, scalar1=Aap, scalar2=Sap, op0=ALU.mult, op1=ALU.add),
                         reads=pbs, writes=[S["b_ht"]])
            P.dma("sp", hT[:, t0:t0 + 512].rearrange("(k p) t -> p k t", p=128), ht[:], reads=[S["b_ht"]])

        stL(0)
        if nt > 1:
            stL(1)
        for step in range(nt + 2):
            if step + 2 < nt:
                stL(step + 2)
            if step < nt:
                stA(step)
            if 0 <= step - 1 < nt:
                stB(step - 1)
            if 0 <= step - 2 < nt:
                stC(step - 2)
        P.flush()


def phase_lf(kb, actT, K, L, groups, epi, halo=0, Lseg=2048, nrow=2, setup=None, depth=2, evac="act", cast="act"):
    nc, P = kb.nc, kb.P
    KC = K // 128
    Lseg = min(Lseg, L)
    with ExitStack() as es:
        W = Lseg + 2 * halo
        act = kb.sb(es, [128, KC, W], BF16)
        b_act = P.buf()
        NW = (depth + 1) * nrow
        wfs = [(kb.sb(es, [128, KC, 128], F32), P.buf()) for _ in range(NW)]
        wts = [(kb.sb(es, [128, KC, 128], BF16), P.buf()) for _ in range(NW)]
        rowsets = [[(kb.sb(es, [128, W], F32), P.buf()) for _ in range(nrow)] for _ in range(2)]
        ctx = setup(es) if setup else None
        work = [(t0, gi) for t0 in range(0, L, Lseg) for gi in range(len(groups))]
        wslot = {}
        wi = [0]

        def fetch(idx):
            if idx >= len(work):
                return
            _, gi = work[idx]
            slots = []
            for spec in groups[gi]:
                Wap, col0 = spec[0], spec[1]
                sl = wi[0] % NW
                wi[0] += 1
                wf, b_wf = wfs[sl]
                wt, b_wt = wts[sl]
                P.dma("sp", wf[:], Wap[:, col0:col0 + 128].rearrange("(k p) n -> p k n", p=128), writes=[b_wf])
                if cast == "act":
                    P.op("act", lambda e: e.activation(out=wt[:], in_=wf[:], func=AF.Copy), reads=[b_wf], writes=[b_wt])
                else:
                    P.op("pool", lambda e: e.tensor_copy(out=wt[:], in_=wf[:]), reads=[b_wf], writes=[b_wt])
                slots.append(sl)
            wslot[idx] = slots

        for dd in range(depth):
            fetch(dd)
        ev = 0
        for idx, (t0, gi) in enumerate(work):
            Ls = Lseg
            if gi == 0:
                lo = t0 - halo
                hi = t0 + Ls + halo
                clo, chi = max(lo, 0), min(hi, L)
                if halo:
                    if lo < 0:
                        P.op("dve", lambda e: e.memset(act[:, :, 0:1], 0.0), writes=[b_act])
                    if hi > L:
                        P.op("dve", lambda e: e.memset(act[:, :, W - 1:W], 0.0), writes=[b_act])
                P.dma("sp", act[:, :, clo - lo:chi - lo], actT[:, clo:chi].rearrange("(k p) t -> p k t", p=128),
                      writes=[b_act])
            fetch(idx + depth)
            grp = groups[gi]
            rows = rowsets[idx % 2][:len(grp)]
            for ci, spec in enumerate(grp):
                esc = spec[2] if len(spec) > 2 else 1.0
                wt, b_wt = wts[wslot[idx][ci]]
                row, b_row = rows[ci]
                for c0 in range(0, W, 512):
                    c1 = min(c0 + 512, W)
                    bk = kb.bank()
                    for k in range(KC):
                        P.op("pe", lambda e: e.matmul(kb.ps[bk][:, 0:c1 - c0], lhsT=wt[:, k, :], rhs=act[:, k, c0:c1],
                                                      start=(k == 0), stop=(k == KC - 1)),
                             reads=[b_wt, b_act], writes=[kb.pbuf[bk]], signal=(k == KC - 1))
                    ev += 1
                    if (evac == "act" or ev % 2 == 0) and esc == 1.0:
                        P.op("act", lambda e: e.activation(out=row[:, c0:c1], in_=kb.ps[bk][:, 0:c1 - c0], func=AF.Copy),
                             reads=[kb.pbuf[bk]], writes=[b_row])
                    else:
                        P.op("dve", lambda e: e.tensor_scalar(out=row[:, c0:c1], in0=kb.ps[bk][:, 0:c1 - c0], scalar1=esc, scalar2=None,
                                                              op0=ALU.mult), reads=[kb.pbuf[bk]], writes=[b_row])
            del wslot[idx]
            epi(kb, ctx, gi, rows, t0, Ls)
        P.flush()


def phase_lt(kb, actT, K, L, Wap, N, epi, setup=None, tile_pre=None, tile_post=None, epi_group=None):
    nc, P = kb.nc, kb.P
    KC = K // 128
    with ExitStack() as es:
        w = kb.sb(es, [128, KC, N], BF16)
        b_w = P.buf()
        wst = [(kb.sb(es, [128, 1024], F32), P.buf()) for _ in range(3)]
        ii = 0
        for k in range(KC):
            for n0 in range(0, N, 1024):
                n1 = min(N, n0 + 1024)
                wf, b_wf = wst[ii % 3]
                P.dma("sp", wf[:, 0:n1 - n0], Wap[k * 128:(k + 1) * 128, n0:n1], writes=[b_wf])
                eng = ("act", "dve")[ii % 2]
                if eng == "act":
                    P.op("act", lambda e: e.activation(out=w[:, k, n0:n1], in_=wf[:, 0:n1 - n0], func=AF.Copy), reads=[b_wf], writes=[b_w])
                else:
                    P.op(eng, lambda e: e.tensor_copy(out=w[:, k, n0:n1], in_=wf[:, 0:n1 - n0]), reads=[b_wf], writes=[b_w])
                ii += 1
        acts = [(kb.sb(es, [128, KC, 512], BF16), P.buf()) for _ in range(2)]
        ctx = setup(es) if setup else None
        def loads(ti):
            t0 = ti * 512
            act, b_act = acts[ti % 2]
            P.dma("sp", act[:], actT[:, t0:t0 + 512].rearrange("(k p) t -> p k t", p=128), writes=[b_act])
            if tile_pre:
                tile_pre(kb, ctx, ti)

        loads(0)
        for ti in range(L // 512):
            act, b_act = acts[ti % 2]
            if ti + 1 < L // 512:
                loads(ti + 1)
            for a in range(4):
                bks = []
                for nb in range(N // 512):
                    bk = kb.bank()
                    bks.append(bk)
                    for k in range(KC):
                        P.op("pe", lambda e, k=k, bk=bk, act=act, a=a, nb=nb: e.matmul(
                            kb.ps[bk], lhsT=act[:, k, a * 128:(a + 1) * 128], rhs=w[:, k, nb * 512:(nb + 1) * 512],
                            start=(k == 0), stop=(k == KC - 1)),
                            reads=[b_act, b_w], writes=[kb.pbuf[bk]], signal=(k == KC - 1))
                    if epi_group is None:
                        epi(kb, ctx, ti, a, nb, bk)
                if epi_group is not None:
                    epi_group(kb, ctx, ti, a, bks)
            if tile_post:
                tile_post(kb, ctx, ti)
        P.flush()


def lt_residual(kb, actT, K, L, Wap, Xsrc, Xdst, l, j, s):
    P = kb.P

    def setup(es):
        c = dict()
        c["g"] = kb.sb(es, [128, 1024], F32)
        c["b_g"] = P.buf()
        P.dma("sp", c["g"][:], kb.gt[l, s, j], writes=[c["b_g"]])
        c["x"] = [(kb.sb(es, [128, 4, 1024], F32), P.buf()) for _ in range(2)]
        c["tmp"] = [(kb.sb(es, [128, 512], F32), P.buf()) for _ in range(2)]
        c["n"] = 0
        return c

    def pre(kb, c, ti):
        x, b_x = c["x"][ti % 2]
        P.dma("sp", x[:], Xsrc[ti * 512:(ti + 1) * 512, :].rearrange("(a p) d -> p a d", p=128), writes=[b_x])

    def epi(kb, c, ti, a, nb, bk):
        x, b_x = c["x"][ti % 2]
        tmp, b_tmp = c["tmp"][c["n"] % 2]
        c["n"] += 1
        g = c["g"]
        P.op("dve", lambda e: e.tensor_tensor(out=tmp[:], in0=kb.ps[bk], in1=g[:, nb * 512:(nb + 1) * 512], op=ALU.mult),
             reads=[kb.pbuf[bk], c["b_g"]], writes=[b_tmp])
        P.op("pool", lambda e: e.tensor_tensor(out=x[:, a, nb * 512:(nb + 1) * 512], in0=x[:, a, nb * 512:(nb + 1) * 512],
                                               in1=tmp[:], op=ALU.add), reads=[b_tmp, b_x], writes=[b_x])

    def post(kb, c, ti):
        x, b_x = c["x"][ti % 2]
        P.dma("sp", Xdst[ti * 512:(ti + 1) * 512, :].rearrange("(a p) d -> p a d", p=128), x[:], reads=[b_x])

    phase_lt(kb, actT, K, L, Wap, 1024, epi, setup=setup, tile_pre=pre, tile_post=post)


def phase_ffn(kb, W, hT, uT, L, l):
    P = kb.P
    wg, wv = W["ffn_w_gate"][l], W["ffn_w_val"][l]
    groups = [[(wg, jj * 128), (wv, jj * 128)] for jj in range(22)]

    def setup(es):
        c = dict()
        c["cw"] = kb.sb(es, [128, 4, 22], F32)
        c["b_cw"] = P.buf()
        for jj in range(3):
            load_col(P, c["cw"][:, jj, :], W["ffn_conv_w"][l, jj], c["b_cw"])
        load_col(P, c["cw"][:, 3, :], W["ffn_conv_b"][l], c["b_cw"])
        c["r"] = [(kb.sb(es, [128, 2048], F32), P.buf()) for _ in range(2)]
        c["u"] = [(kb.sb(es, [128, 2048], BF16), P.buf()) for _ in range(2)]
        return c

    def epi(kb, c, gi, rows, t0, Ls):
        (g, b_g), (v, b_v) = rows
        r, b_r = c["r"][gi % 2]
        u, b_u = c["u"][gi % 2]
        cw = c["cw"]
        P.op("act", lambda e: e.activation(out=r[:, 0:Ls], in_=g[:, 1:Ls + 1], func=AF.Identity,
                                           scale=cw[:, 1, gi:gi + 1], bias=cw[:, 3, gi:gi + 1]),
             reads=[b_g, c["b_cw"]], writes=[b_r])
        P.op("dve", lambda e: e.scalar_tensor_tensor(out=r[:, 0:Ls], in0=g[:, 0:Ls], scalar=cw[:, 0, gi:gi + 1],
                                                     in1=r[:, 0:Ls], op0=ALU.mult, op1=ALU.add),
             reads=[b_g, c["b_cw"], b_r], writes=[b_r])
        P.op("dve", lambda e: e.scalar_tensor_tensor(out=r[:, 0:Ls], in0=g[:, 2:Ls + 2], scalar=cw[:, 2, gi:gi + 1],
                                                     in1=r[:, 0:Ls], op0=ALU.mult, op1=ALU.add),
             reads=[b_g, c["b_cw"], b_r], writes=[b_r])
        P.op("act", lambda e: e.activation(out=r[:, 0:Ls], in_=r[:, 0:Ls], func=AF.Silu), reads=[b_r], writes=[b_r])
        P.op("pool", lambda e: e.tensor_tensor(out=u[:, 0:Ls], in0=r[:, 0:Ls], in1=v[:, 1:Ls + 1], op=ALU.mult),
             reads=[b_r, b_v], writes=[b_u])
        P.dma("pool", uT[gi * 128:(gi + 1) * 128, t0:t0 + Ls], u[:, 0:Ls], reads=[b_u])

    phase_lf(kb, hT, 1024, L, groups, epi, halo=1, setup=setup)


WSPEC = [
    ("ada_w", [4, 1024, 6144]), ("ada_b", [4, 6144]), ("norm_g", [4, 2, 1024]),
    ("hy_w_in", [1, 1024, 3072]), ("hy_conv_w", [1, 3, 3072]), ("hy_conv_b", [1, 3072]),
    ("hy_w1", [1, 33, 64]), ("hy_b1", [1, 64]), ("hy_w2", [1, 64, 64]), ("hy_b2", [1, 64]),
    ("hy_w3", [1, 64, 2048]), ("hy_freq", [1, 64]), ("hy_decay", [1, 2, 1024]), ("hy_skip", [1, 1024]),
    ("hy_w_out", [1, 1024, 1024]),
    ("ret_w_in", [1, 1024, 6144]), ("ret_decay", [1, 2, 4]), ("ret_w_out", [1, 2048, 1024]),
    ("swa_w_in", [1, 1024, 1536]), ("swa_q_gain", [1, 64]), ("swa_k_gain", [1, 64]), ("swa_sink", [1, 16]),
    ("swa_w_out", [1, 1024, 1024]),
    ("hg_w_in", [1, 1024, 5120]), ("hg_lb", [2, 4, 1024]), ("hg_gain", [1, 128]), ("hg_w_out", [1, 1024, 1024]),
    ("ffn_w_gate", [4, 1024, 2816]), ("ffn_w_val", [4, 1024, 2816]), ("ffn_conv_w", [4, 3, 2816]),
    ("ffn_conv_b", [4, 2816]), ("ffn_w_down", [4, 2816, 1024]),
]


def build(Ls, mixers=(0, 1, 2, 3), layers=(0, 1, 2, 3), dbg=()):
    kb = KB(Ls, dbg)
    P = kb.P
    W = {n: kb.inp(n, shp) for n, shp in WSPEC}
    W["c"] = kb.inp("c", [2, 1024])
    X = [kb.inp(f"x{s}", [Ls[s], 1024]) for s in range(2)]
    Y = [kb.out(f"y{s}", [Ls[s], 1024]) for s in range(2)]
    Lmax = max(Ls)
    kb.C = {n: kb.inp(n, shp) for n, shp in CSPEC(Lmax)}
    if 0 in mixers and 0 in layers:
        for L in sorted(set(Ls)):
            for n, shp, dt in fft_cspec(L):
                kb.C[n] = kb.inp(n, shp, dt)
    kb.use_hy = (0 in mixers and 0 in layers)
    setup_consts(kb)
    P.defer_emit = DEFER_EMIT
    phase_mod(kb, W)
    hT = kb.scratch("hT", [1024, Lmax], BF16)
    uT = kb.scratch("uT", [DFF, Lmax], BF16)
    for s in range(2):
        L = Ls[s]
        cur = X[s]
        for l in layers:
            if l in mixers:
                phase_norm(kb, cur, hT[:, 0:L], L, l, 0, s)
                MIXERS[l](kb, W, hT[:, 0:L], cur, Y[s], L, l, s)
                cur = Y[s]
            phase_norm(kb, cur, hT[:, 0:L], L, l, 1, s)
            phase_ffn(kb, W, hT[:, 0:L], uT[:, 0:L], L, l)
            lt_residual(kb, uT[:, 0:L], DFF, L, W["ffn_w_down"][l], cur, Y[s], l, 1, s)
            cur = Y[s]
    P.flush(final=True)
    return kb


MIXERS = {}
DEFER_EMIT = True


def host_consts(Lmax):
    c = {"ident": np.eye(128, dtype=np.float32)}
    t = np.arange(Lmax, dtype=np.float32)[:, None]
    inv = (10000.0 ** (-np.arange(0, 64, 2, dtype=np.float32) / 64)).astype(np.float32)
    ang = (t * inv[None, :]).astype(np.float32)
    c["swa_cos"] = np.cos(ang).astype(np.float32)
    c["swa_sin"] = np.sin(ang).astype(np.float32)
    kk = np.arange(128)[:, None]
    qq = np.arange(128)[None, :]
    c["maskL"] = (kk >= qq).astype(np.float32)
    c["maskR"] = (kk <= qq).astype(np.float32)
    inv = (10000.0 ** (-np.arange(0, 256, 2, dtype=np.float32) / 256)).astype(np.float32)
    ang = (inv[:, None] * np.arange(Lmax, dtype=np.float32)[None, :]).astype(np.float32)
    c["ret_cos"] = np.cos(ang).astype(np.float32)
    c["ret_sin"] = np.sin(ang).astype(np.float32)
    tabs = np.zeros((6, 128, 128), np.float32)
    ss_ = np.arange(128)[:, None]
    tt_ = np.arange(128)[None, :]
    tabs[0] = np.maximum(tt_ - ss_, 0)
    tabs[1] = np.maximum(ss_ - tt_, 0)
    tabs[2] = 1.0 + np.eye(128)
    tabs[3] = np.broadcast_to(np.arange(128)[None, :] + 1.0, (128, 128))
    tabs[4] = np.broadcast_to(128.0 - np.arange(128)[None, :], (128, 128))
    tabs[5, :, 0] = 127.0 - np.arange(128)
    tabs[5, :, 1] = np.arange(128)
    tabs[5, :, 2] = 128.0
    c["ret_tabs"] = tabs
    hm = np.zeros((128, 256), np.float32)
    same = (ss_ // 64) == (tt_ // 64)
    hm[:, 0:128] = same & (ss_ <= tt_)
    hm[:, 128:256] = same & (ss_ >= tt_)
    c["hg_mask"] = hm
    return c


def fft_tables(L):
    import ml_dtypes
    bf = ml_dtypes.bfloat16
    N1 = 2 * L // 128
    N = 2 * L
    c = {}
    n1 = np.arange(N1 // 2)[:, None].astype(np.float64)
    k1 = np.arange(N1)[None, :].astype(np.float64)
    th = 2 * np.pi * n1 * k1 / N1
    c[f"F1_{N1}"] = np.stack([np.cos(th), -np.sin(th)], axis=1).astype(np.float32).astype(bf)
    k1v = np.arange(N1)[:, None, None].astype(np.float64)
    n2 = np.arange(128)[None, :, None].astype(np.float64)
    k2 = np.arange(128)[None, None, :].astype(np.float64)
    ph = 2 * np.pi * ((n2 * (k1v + N1 * k2)) % N) / N
    Gr, Gi = np.cos(ph), -np.sin(ph)
    c[f"G_{N1}"] = np.stack([Gr, Gi, -Gi, -Gr], axis=2).astype(np.float32).astype(bf)
    Gr_t, Gi_t = Gr.transpose(0, 2, 1), Gi.transpose(0, 2, 1)
    c[f"CT_{N1}"] = np.stack([Gr_t, Gi_t, -Gi_t], axis=2).astype(np.float32).astype(bf)
    k1c = np.arange(N1)[:, None].astype(np.float64)
    n1r = np.arange(N1 // 2)[None, :].astype(np.float64)
    th2 = 2 * np.pi * k1c * n1r / N1
    c[f"Finv_{N1}"] = (np.stack([np.cos(th2), -np.sin(th2)], axis=1) / N).astype(np.float32).astype(bf)
    t = np.linspace(0.0, 1.0, L, dtype=np.float32)[:, None]
    ang = (np.float32(2.0 * math.pi / L) * np.arange(L, dtype=np.float32))[:, None]
    bands = np.linspace(1e-4, 15, 16, dtype=np.float32)[None, :]
    z = np.concatenate([t, np.cos(bands * ang), -np.sin(bands * ang)], axis=-1).astype(np.float32)
    c[f"ZT_{L}"] = np.ascontiguousarray(z.T)
    c[f"negT_{L}"] = np.ascontiguousarray((-t[:, 0]).reshape(L // 128, 128).T)
    return c


def fft_cspec(L):
    N1 = 2 * L // 128
    return [(f"F1_{N1}", [N1 // 2, 2, N1], BF16), (f"G_{N1}", [N1, 128, 4, 128], BF16), (f"CT_{N1}", [N1, 128, 3, 128], BF16),
            (f"Finv_{N1}", [N1, 2, N1 // 2], BF16), (f"ZT_{L}", [33, L], F32), (f"negT_{L}", [128, L // 128], F32)]


CSPEC = lambda Lmax: [("swa_cos", [Lmax, 32]), ("swa_sin", [Lmax, 32]), ("maskL", [128, 128]), ("maskR", [128, 128]),
                      ("ret_cos", [128, Lmax]), ("ret_sin", [128, Lmax]), ("ret_tabs", [6, 128, 128]), ("hg_mask", [128, 256])]


def run(kb, inputs, ncores=8):
    consts = host_consts(max(kb.Ls))
    if kb.use_hy:
        for L in sorted(set(kb.Ls)):
            consts.update(fft_tables(L))
    in_maps = []
    for i in range(ncores):
        m = dict(consts)
        for n, _ in WSPEC:
            m[n] = np.ascontiguousarray(inputs[n], dtype=np.float32)
        m["c"] = np.ascontiguousarray(np.stack([inputs["c_prompt"][i], inputs["c_sample"][i]]), dtype=np.float32)
        m["x0"] = np.ascontiguousarray(inputs["x_prompt"][i], dtype=np.float32)
        m["x1"] = np.ascontiguousarray(inputs["x_sample"][i], dtype=np.float32)
        for k_ in list(m):
            if k_ in kb.din and m[k_].dtype == np.float32 and str(kb.din[k_].dtype).endswith("bfloat16"):
                m[k_] = m[k_].astype(ml_dtypes.bfloat16)
        in_maps.append({k: v for k, v in m.items() if k in kb.din})
    res = run_bass_kernel_spmd(kb.nc, in_maps, core_ids=list(range(ncores)))
    return res.results


def mixer_swa(kb, W, hT, Xsrc, Xdst, L, l, s):
    P = kb.P
    nblk = L // 128
    qT = kb.scratch("swa_qT", [16, 64, max(kb.Ls)], BF16)
    kT = kb.scratch("swa_kT", [4, 64, max(kb.Ls)], BF16)
    vD = kb.scratch("swa_v", [max(kb.Ls), 256], BF16)
    oT = kb.scratch("swa_oT", [1024, max(kb.Ls)], BF16)
    ropeC, ropeS = kb.C["swa_cos"], kb.C["swa_sin"]

    def setup(es):
        c = dict()
        c["gq"] = kb.sb(es, [128, 64], F32)
        c["gk"] = kb.sb(es, [128, 64], F32)
        c["b_g"] = P.buf()
        gq, gk = W["swa_q_gain"][0:1, :], W["swa_k_gain"][0:1, :]
        P.dma("sp", c["gq"][:], AP(gq.tensor, gq.offset, [[0, 128], [1, 64]]), writes=[c["b_g"]])
        P.dma("sp", c["gk"][:], AP(gk.tensor, gk.offset, [[0, 128], [1, 64]]), writes=[c["b_g"]])
        P.op("dve", lambda e: e.tensor_scalar(out=c["gq"][:], in0=c["gq"][:], scalar1=0.125, scalar2=None, op0=ALU.mult),
             reads=[c["b_g"]], writes=[c["b_g"]])
        c["cs"] = [(kb.sb(es, [128, 4, 2, 32], F32), P.buf()) for _ in range(2)]
        c["sq"] = [[(kb.sb(es, [128, 512], F32), P.buf()) for _ in range(2)] for _ in range(3)]
        c["xn"] = [[(kb.sb(es, [128, 512], F32), P.buf()) for _ in range(2)] for _ in range(3)]
        c["t"] = [[(kb.sb(es, [128, 4, 256], F32), P.buf()) for _ in range(2)] for _ in range(3)]
        c["xr"] = [[(kb.sb(es, [128, 512], BF16), P.buf()) for _ in range(2)] for _ in range(3)]
        c["ss"] = [[(kb.sb(es, [128, 8], F32), P.buf()) for _ in range(2)] for _ in range(3)]
        c["qt"] = [(kb.sb(es, [64, 16, 512], BF16), P.buf()) for _ in range(2)]
        c["kt"] = [(kb.sb(es, [64, 4, 512], BF16), P.buf()) for _ in range(2)]
        c["vt"] = [(kb.sb(es, [128, 4, 256], BF16), P.buf()) for _ in range(2)]
        c["n"] = 0
        return c

    def pre(kb, c, ti):
        cs, b_cs = c["cs"][ti % 2]
        t0 = ti * 512
        P.dma("sp", cs[:, :, 0, :], ropeC[t0:t0 + 512, :].rearrange("(a p) i -> p a i", p=128), writes=[b_cs])
        P.dma("sp", cs[:, :, 1, :], ropeS[t0:t0 + 512, :].rearrange("(a p) i -> p a i", p=128), writes=[b_cs])

    def chain(c, ti, a, nb, bk, n):
        cs, b_cs = c["cs"][ti % 2]
        H = 8 if nb < 2 else 4
        HW = H * 64
        psx = kb.ps[bk][:, 0:HW]
        b_ps = kb.pbuf[bk]
        sq, b_sq = c["sq"][nb][n % 2]
        xn, b_xn = c["xn"][nb][n % 2]
        tt, b_t = c["t"][nb][n % 2]
        xr, b_xr = c["xr"][nb][n % 2]
        ss, b_ss = c["ss"][nb][n % 2]
        gain = c["gq"] if nb < 2 else c["gk"]
        if nb == 2:
            vt, b_vt = c["vt"][ti % 2]
            P.op("act", lambda e: e.activation(out=vt[:, a, :], in_=kb.ps[bk][:, 256:512], func=AF.Copy),
                 reads=[b_ps], writes=[b_vt])
        P.op("act", lambda e: e.activation(out=sq[:, 0:HW], in_=psx, func=AF.Square), reads=[b_ps], writes=[b_sq])
        yield
        P.op("dve", lambda e: e.tensor_reduce(out=ss[:, 0:H], in_=sq[:, 0:HW].rearrange("p (h d) -> p h d", d=64),
                                              axis=AX.X, op=ALU.add), reads=[b_sq], writes=[b_ss])
        yield
        P.op("act", lambda e: e.activation(out=ss[:, 0:H], in_=ss[:, 0:H], func=AF.Sqrt, scale=1.0 / 64, bias=kb.epsc[:, 0:1]),
             reads=[b_ss], writes=[b_ss])
        yield
        P.op("dve", lambda e: e.reciprocal(out=ss[:, 0:H], in_=ss[:, 0:H]), reads=[b_ss], writes=[b_ss])
        yield
        ssb = AP(ss, 0, [[8, 128], [1, H], [0, 64]])
        P.op("dve", lambda e: e.tensor_tensor(out=xn[:, 0:HW].rearrange("p (h d) -> p h d", d=64),
                                              in0=psx.rearrange("p (h d) -> p h d", d=64), in1=ssb, op=ALU.mult),
             reads=[b_ps, b_ss], writes=[b_xn])
        yield
        gb = AP(gain, 0, [[64, 128], [0, H], [1, 64]])
        P.op("pool", lambda e: e.tensor_tensor(out=xn[:, 0:HW].rearrange("p (h d) -> p h d", d=64),
                                               in0=xn[:, 0:HW].rearrange("p (h d) -> p h d", d=64), in1=gb, op=ALU.mult),
             reads=[b_xn, c["b_g"]], writes=[b_xn])
        yield
        x3 = xn[:, 0:HW].rearrange("p (h d) -> p h d", d=64)
        x1, x2 = x3[:, :, 0:32], x3[:, :, 32:64]
        cosb = AP(cs, a * 64, [[256, 128], [0, H], [1, 32]])
        sinb = AP(cs, a * 64 + 32, [[256, 128], [0, H], [1, 32]])
        t4 = tt[:, :, 0:H * 32]
        tv = [t4[:, i, :].rearrange("p (h d) -> p h d", d=32) for i in range(4)]
        P.op("dve", lambda e: e.tensor_tensor(out=tv[0], in0=x1, in1=cosb, op=ALU.mult), reads=[b_xn, b_cs], writes=[b_t])
        P.op("pool", lambda e: e.tensor_tensor(out=tv[1], in0=x2, in1=sinb, op=ALU.mult), reads=[b_xn, b_cs], writes=[b_t])
        P.op("dve", lambda e: e.tensor_tensor(out=tv[2], in0=x1, in1=sinb, op=ALU.mult), reads=[b_xn, b_cs], writes=[b_t])
        P.op("pool", lambda e: e.tensor_tensor(out=tv[3], in0=x2, in1=cosb, op=ALU.mult), reads=[b_xn, b_cs], writes=[b_t])
        yield
        r3 = xr[:, 0:HW].rearrange("p (h d) -> p h d", d=64)
        P.op("dve", lambda e: e.tensor_tensor(out=r3[:, :, 0:32], in0=tv[0], in1=tv[1], op=ALU.subtract), reads=[b_t], writes=[b_xr])
        P.op("pool", lambda e: e.tensor_tensor(out=r3[:, :, 32:64], in0=tv[2], in1=tv[3], op=ALU.add), reads=[b_t], writes=[b_xr])
        yield
        bt = kb.bank()
        for h in range(H):
            P.op("pe", lambda e, h=h: e.transpose(out=kb.psb[bt][0:64, h * 128:(h + 1) * 128], in_=xr[:, h * 64:(h + 1) * 64],
                                                 identity=kb.identb[:]),
                 reads=[b_xr, kb.b_ident], writes=[kb.pbuf[bt]], signal=(h == H - 1))
        yield
        if nb < 2:
            dt_, b_dt = c["qt"][ti % 2]
            dst = dt_[:, nb * 8:(nb + 1) * 8, a * 128:(a + 1) * 128]
        else:
            dt_, b_dt = c["kt"][ti % 2]
            dst = dt_[:, :, a * 128:(a + 1) * 128]
        P.op("act", lambda e: e.activation(out=dst, in_=kb.psb[bt][0:64, 0:H * 128].rearrange("p (h t) -> p h t", t=128),
                                           func=AF.Copy), reads=[kb.pbuf[bt]], writes=[b_dt])

    def epi_group(kb, c, ti, a, bks):
        c["n"] += 1
        interleave([chain(c, ti, a, nb, bks[nb], c["n"]) for nb in range(3)])

    def post(kb, c, ti):
        t0 = ti * 512
        qt, b_qt = c["qt"][ti % 2]
        kt, b_kt = c["kt"][ti % 2]
        vt, b_vt = c["vt"][ti % 2]
        P.dma("sp", qT[:, :, t0:t0 + 512].rearrange("h d t -> d h t"), qt[:], reads=[b_qt])
        P.dma("sp", kT[:, :, t0:t0 + 512].rearrange("h d t -> d h t"), kt[:], reads=[b_kt])
        P.dma("sp", vD[t0:t0 + 512, :].rearrange("(a p) c -> p a c", p=128), vt[:], reads=[b_vt])

    phase_lt(kb, hT, 1024, L, W["swa_w_in"][0], 1536, None, setup=setup, tile_pre=pre, tile_post=post, epi_group=epi_group)

    with ExitStack() as es:
        mk = kb.sb(es, [128, 2, 128], BF16)
        b_mk = P.buf()
        esk = kb.sb(es, [128, 16], F32)
        b_esk = P.buf()
        sk = W["swa_sink"][0:1, :]
        P.dma("sp", esk[:], AP(sk.tensor, sk.offset, [[0, 128], [1, 16]]), writes=[b_esk])
        P.op("act", lambda e: e.activation(out=esk[:], in_=esk[:], func=AF.Exp), reads=[b_esk], writes=[b_esk])
        P.dma("pool", mk[:, 0, :], kb.C["maskL"], writes=[b_mk])
        P.dma("pool", mk[:, 1, :], kb.C["maskR"], writes=[b_mk])

        def grp_gen(g):
            ktg = kb.sb(es, [64, L], BF16)
            vp = kb.sb(es, [128, nblk, 65], BF16)
            b_kv = P.buf()
            qt = kb.sb(es, [64, 4, 512], BF16)
            b_qt = P.buf()
            Es = [(kb.sb(es, [128, 3, 512], BF16), P.buf()) for _ in range(2)]
            ots = [(kb.sb(es, [128, 256], BF16), P.buf()) for _ in range(2)]
            oTt = kb.sb(es, [128, 2, 512], BF16)
            b_oT = P.buf()
            dens = [(kb.sb(es, [128, 4], F32), P.buf()) for _ in range(2)]
            P.dma("sp", ktg[:], kT[g, :, 0:L], writes=[b_kv])
            P.op("dve", lambda e: e.memset(vp[:, :, 64:65], 1.0), writes=[b_kv])
            with kb.nc.allow_non_contiguous_dma(reason="v head slice"):
                pass
            P.dma("sp", vp[:, :, 0:64], vD[0:L, g * 64:(g + 1) * 64].rearrange("(b p) c -> p b c", p=128), writes=[b_kv])
            yield
            it = 0
            for qb in range(nblk):
                if qb % 4 == 0:
                    P.dma("sp", qt[:], qT[g * 4:(g + 1) * 4, :, qb * 128:qb * 128 + 512].rearrange("h d t -> d h t"),
                          writes=[b_qt])
                E, b_E = Es[it % 2]
                ot, b_ot = ots[it % 2]
                den, b_den = dens[it % 2]
                it += 1
                kbs = [x for x in (qb - 1, qb, qb + 1) if 0 <= x < nblk]
                qo = (qb % 4) * 128
                bks = []
                for i, kbi in enumerate(kbs):
                    bk = kb.acq()
                    bks.append(bk)
                    P.op("pe", lambda e: e.matmul(kb.ps[bk].rearrange("p (h t) -> p h t", t=128), lhsT=ktg[:, kbi * 128:(kbi + 1) * 128],
                                                  rhs=qt[:, :, qo:qo + 128], start=True, stop=True),
                         reads=[b_kv, b_qt], writes=[kb.pbuf[bk]])
                yield
                for i, kbi in enumerate(kbs):
                    bk = bks[i]
                    P.op("act", lambda e: e.activation(out=E[:, i, :], in_=kb.ps[bk], func=AF.Exp), reads=[kb.pbuf[bk]], writes=[b_E])
                    kb.rel(bk)
                yield
                for i, kbi in enumerate(kbs):
                    if kbi != qb:
                        mi = 0 if kbi < qb else 1
                        mb = AP(mk, mi * 128, [[256, 128], [0, 4], [1, 128]])
                        P.op("dve", lambda e: e.tensor_tensor(out=E[:, i, :].rearrange("p (h t) -> p h t", t=128),
                                                              in0=E[:, i, :].rearrange("p (h t) -> p h t", t=128), in1=mb, op=ALU.mult),
                             reads=[b_E, b_mk], writes=[b_E])
                yield
                bo = kb.acq()
                for hh in range(4):
                    for i, kbi in enumerate(kbs):
                        P.op("pe", lambda e: e.matmul(kb.ps[bo][:, hh * 65:(hh + 1) * 65], lhsT=E[:, i, hh * 128:(hh + 1) * 128], rhs=vp[:, kbi, :],
                                                      start=(i == 0), stop=(i == len(kbs) - 1)),
                             reads=[b_E, b_kv], writes=[kb.pbuf[bo]], signal=(hh == 3 and i == len(kbs) - 1))
                yield
                o3 = kb.ps[bo][:, 0:260].rearrange("p (h d) -> p h d", d=65)
                P.op("dve", lambda e: e.tensor_tensor(out=den[:], in0=o3[:, :, 64], in1=esk[:, g * 4:(g + 1) * 4], op=ALU.add),
                     reads=[kb.pbuf[bo], b_esk], writes=[b_den])
                P.op("dve", lambda e: e.reciprocal(out=den[:], in_=den[:]), reads=[b_den], writes=[b_den])
                yield
                db = AP(den, 0, [[4, 128], [1, 4], [0, 64]])
                P.op("dve", lambda e: e.tensor_tensor(out=ot[:].rearrange("p (h d) -> p h d", d=64), in0=o3[:, :, 0:64], in1=db, op=ALU.mult),
                     reads=[kb.pbuf[bo], b_den], writes=[b_ot])
                kb.rel(bo)
                yield
                bt = kb.acq()
                for hf in range(2):
                    P.op("pe", lambda e: e.transpose(out=kb.psb[bt][:, hf * 128:(hf + 1) * 128], in_=ot[:, hf * 128:(hf + 1) * 128],
                                                     identity=kb.identb[:]), reads=[b_ot, kb.b_ident], writes=[kb.pbuf[bt]], signal=(hf == 1))
                yield
                P.op("act", lambda e: e.activation(out=oTt[:, :, qo:qo + 128], in_=kb.psb[bt][:, 0:256].rearrange("p (h t) -> p h t", t=128),
                                                   func=AF.Copy), reads=[kb.pbuf[bt]], writes=[b_oT])
                kb.rel(bt)
                if qb % 4 == 3:
                    t0 = (qb - 3) * 128
                    P.dma("pool", oT[g * 256:(g + 1) * 256, t0:t0 + 512].rearrange("(h p) t -> p h t", p=128), oTt[:], reads=[b_oT])
                yield

        interleave([grp_gen(g) for g in range(2)])
        P.flush()
        interleave([grp_gen(g) for g in range(2, 4)])
        P.flush()

    lt_residual(kb, oT[:, 0:L], 1024, L, W["swa_w_out"][0], Xsrc, Xdst, l, 0, s)


MIXERS[2] = mixer_swa


def lt_plain(kb, actT, K, L, Wap, N, dst):
    P = kb.P

    def setup(es):
        return dict(o=[(kb.sb(es, [128, 4, N], BF16), P.buf()) for _ in range(2)], n=0)

    def epi(kb, c, ti, a, nb, bk):
        o, b_o = c["o"][ti % 2]
        c["n"] += 1
        if c["n"] % 2:
            P.op("act", lambda e: e.activation(out=o[:, a, nb * 512:(nb + 1) * 512], in_=kb.ps[bk], func=AF.Copy),
                 reads=[kb.pbuf[bk]], writes=[b_o])
        else:
            P.op("dve", lambda e: e.tensor_copy(out=o[:, a, nb * 512:(nb + 1) * 512], in_=kb.ps[bk]),
                 reads=[kb.pbuf[bk]], writes=[b_o])

    def post(kb, c, ti):
        o, b_o = c["o"][ti % 2]
        P.dma("sp", dst[ti * 512:(ti + 1) * 512, :].rearrange("(a p) n -> p a n", p=128), o[:], reads=[b_o])

    phase_lt(kb, actT, K, L, Wap, N, epi, setup=setup, tile_post=post)


def mixer_ret(kb, W, hT, Xsrc, Xdst, L, l, s):
    P = kb.P
    Lm = max(kb.Ls)
    nblk = L // 128
    qT = kb.scratch("ret_qT", [1024, Lm], BF16)
    kT = kb.scratch("ret_kT", [1024, Lm], BF16)
    gsT = kb.scratch("ret_gsT", [2048, Lm], BF16)
    vD = kb.scratch("ret_v", [Lm, 2048], BF16)
    ogT = kb.scratch("ret_ogT", [2048, Lm], BF16)
    SbD = kb.scratch("ret_Sb", [4, Lm // 128, 128, 2, 512], BF16)
    win = W["ret_w_in"][0]

    groups = []
    for h in range(4):
        groups.append([(win, h * 256), (win, h * 256 + 128)])
    for h in range(4):
        groups.append([(win, 1024 + h * 256, 1.0 / 16), (win, 1024 + h * 256 + 128, 1.0 / 16)])
    for jj in range(16):
        groups.append([(win, 4096 + jj * 128)])

    def setup(es):
        c = dict()
        c["cs"] = kb.sb(es, [128, 2, 2048], F32)
        c["b_cs"] = P.buf()
        c["t"] = [(kb.sb(es, [128, 2048], F32), P.buf()) for _ in range(4)]
        c["o"] = [(kb.sb(es, [128, 2, 2048], BF16), P.buf()) for _ in range(2)]
        return c

    def epi(kb, c, gi, rows, t0, Ls):
        o, b_o = c["o"][gi % 2]
        if gi == 0:
            P.dma("sp", c["cs"][:, 0, 0:Ls], kb.C["ret_cos"][:, t0:t0 + Ls], writes=[c["b_cs"]])
            P.dma("sp", c["cs"][:, 1, 0:Ls], kb.C["ret_sin"][:, t0:t0 + Ls], writes=[c["b_cs"]])
        if gi < 8:
            (x1, b1), (x2, b2) = rows
            cos, sin = c["cs"][:, 0, 0:Ls], c["cs"][:, 1, 0:Ls]
            (t1, bt1), (t2, bt2), (t3, bt3), (t4, bt4) = c["t"]
            P.op("dve", lambda e: e.tensor_tensor(out=t1[:, 0:Ls], in0=x1[:, 0:Ls], in1=cos, op=ALU.mult), reads=[b1, c["b_cs"]], writes=[bt1])
            P.op("pool", lambda e: e.tensor_tensor(out=t2[:, 0:Ls], in0=x2[:, 0:Ls], in1=sin, op=ALU.mult), reads=[b2, c["b_cs"]], writes=[bt2])
            P.op("pool", lambda e: e.tensor_tensor(out=t3[:, 0:Ls], in0=x1[:, 0:Ls], in1=sin, op=ALU.mult), reads=[b1, c["b_cs"]], writes=[bt3])
            P.op("dve", lambda e: e.tensor_tensor(out=t4[:, 0:Ls], in0=x2[:, 0:Ls], in1=cos, op=ALU.mult), reads=[b2, c["b_cs"]], writes=[bt4])
            P.op("dve", lambda e: e.tensor_tensor(out=o[:, 0, 0:Ls], in0=t1[:, 0:Ls], in1=t2[:, 0:Ls], op=ALU.subtract), reads=[bt1, bt2], writes=[b_o])
            P.op("pool", lambda e: e.tensor_tensor(out=o[:, 1, 0:Ls], in0=t3[:, 0:Ls], in1=t4[:, 0:Ls], op=ALU.add), reads=[bt3, bt4], writes=[b_o])
            dstT = qT if gi < 4 else kT
            h = gi % 4
            P.dma("pool", dstT[h * 256:(h + 1) * 256, t0:t0 + Ls].rearrange("(c p) t -> p c t", p=128), o[:, :, 0:Ls], reads=[b_o])
        else:
            (g, bg), = rows
            jj = gi - 8
            P.op("act", lambda e: e.activation(out=o[:, 0, 0:Ls], in_=g[:, 0:Ls], func=AF.Silu), reads=[bg], writes=[b_o])
            P.dma("pool", gsT[jj * 128:(jj + 1) * 128, t0:t0 + Ls], o[:, 0, 0:Ls], reads=[b_o])

    phase_lf(kb, hT, 1024, L, groups, epi, setup=setup, depth=1, evac="alt", cast="pool")
    lt_plain(kb, hT, 1024, L, win[:, 2048:4096], 2048, vD)

    with ExitStack() as es:
        lg = kb.sb(es, [128, 8], F32)
        b_lg = P.buf()
        rd = W["ret_decay"][0:1]
        P.dma("sp", lg[:], AP(rd.tensor, rd.offset, [[0, 128], [1, 8]]), writes=[b_lg])
        P.op("act", lambda e: e.activation(out=lg[:], in_=lg[:], func=AF.Exp), reads=[b_lg], writes=[b_lg])
        P.op("dve", lambda e: e.tensor_scalar(out=lg[:], in0=lg[:], scalar1=-1.0, scalar2=None, op0=ALU.mult), reads=[b_lg], writes=[b_lg])
        hc = kb.sb(es, [128, 6, 128], F32)
        b_hc = P.buf()
        P.dma("sp", hc[:], kb.C["ret_tabs"].rearrange("j p t -> p j t"), writes=[b_hc])
        DT = kb.sb(es, [128, 4, 128], F32)
        qrow = kb.sb(es, [128, 4, 2, 128], F32)
        kcol = kb.sb(es, [128, 4, 2], F32)
        g128 = kb.sb(es, [128, 4, 2], F32)
        b_tab = P.buf()
        tmp = kb.sb(es, [128, 128], F32)
        b_tmp = P.buf()
        for h in range(4):
            lf, lb = lg[:, h:h + 1], lg[:, 4 + h:5 + h]
            P.op("dve", lambda e, lf=lf: e.tensor_scalar(out=tmp[:], in0=hc[:, 0, :], scalar1=lf, scalar2=None, op0=ALU.mult),
                 reads=[b_hc, b_lg], writes=[b_tmp])
            P.op("dve", lambda e, lb=lb: e.scalar_tensor_tensor(out=tmp[:], in0=hc[:, 1, :], scalar=lb, in1=tmp[:], op0=ALU.mult, op1=ALU.add),
                 reads=[b_hc, b_lg, b_tmp], writes=[b_tmp])
            P.op("act", lambda e: e.activation(out=tmp[:], in_=tmp[:], func=AF.Exp), reads=[b_tmp], writes=[b_tmp])
            P.op("dve", lambda e, h=h: e.tensor_tensor(out=DT[:, h, :], in0=tmp[:], in1=hc[:, 2, :], op=ALU.mult), reads=[b_tmp, b_hc], writes=[b_tab])
            P.op("act", lambda e, h=h, lf=lf: e.activation(out=qrow[:, h, 0, :], in_=hc[:, 3, :], func=AF.Exp, scale=lf), reads=[b_hc, b_lg], writes=[b_tab])
            P.op("act", lambda e, h=h, lb=lb: e.activation(out=qrow[:, h, 1, :], in_=hc[:, 4, :], func=AF.Exp, scale=lb), reads=[b_hc, b_lg], writes=[b_tab])
            P.op("act", lambda e, h=h, lf=lf: e.activation(out=kcol[:, h, 0:1], in_=hc[:, 5, 0:1], func=AF.Exp, scale=lf), reads=[b_hc, b_lg], writes=[b_tab])
            P.op("act", lambda e, h=h, lb=lb: e.activation(out=kcol[:, h, 1:2], in_=hc[:, 5, 1:2], func=AF.Exp, scale=lb), reads=[b_hc, b_lg], writes=[b_tab])
            P.op("act", lambda e, h=h, lf=lf: e.activation(out=g128[:, h, 0:1], in_=hc[:, 5, 2:3], func=AF.Exp, scale=lf), reads=[b_hc, b_lg], writes=[b_tab])
            P.op("act", lambda e, h=h, lb=lb: e.activation(out=g128[:, h, 1:2], in_=hc[:, 5, 2:3], func=AF.Exp, scale=lb), reads=[b_hc, b_lg], writes=[b_tab])

        nb4 = nblk // 4

        def mkbufs():
            B = dict()
            tok = P.buf()
            B["St"] = (kb.sb(es, [128, 2, 512], F32), P.buf())
            B["Stb"] = [(kb.sb(es, [128, 2, 512], BF16), P.buf()) for _ in range(2)]
            B["kt"] = (kb.sb(es, [128, 2, 512], BF16), tok)
            B["qt"] = (kb.sb(es, [128, 2, 512], BF16), tok)
            B["vt"] = (kb.sb(es, [128, 4, 512], BF16), tok)
            B["gt"] = (kb.sb(es, [128, 4, 512], BF16), tok)
            B["og"] = (kb.sb(es, [128, 4, 512], BF16), P.buf())
            B["sbs"] = [(kb.sb(es, [128, 2, 512], BF16), P.buf()) for _ in range(2)]
            B["kes"] = [(kb.sb(es, [128, 256], BF16), P.buf()) for _ in range(2)]
            B["pts"] = [(kb.sb(es, [128, 128], BF16), P.buf()) for _ in range(2)]
            B["qds"] = [(kb.sb(es, [128, 2, 2, 128], BF16), P.buf()) for _ in range(2)]
            B["ons"] = [(kb.sb(es, [128, 512], BF16), P.buf()) for _ in range(2)]
            B["sss"] = [(kb.sb(es, [128, 2], F32), P.buf()) for _ in range(2)]
            B["it"] = 0
            return B

        BUFS = [mkbufs() for _ in range(4)]

        def load_kv(B, h, b4, need_q):
            kt, b_kt = B["kt"]
            vt, b_vt = B["vt"]
            t0 = b4 * 512
            P.dma("sp", kt[:], kT[h * 256:(h + 1) * 256, t0:t0 + 512].rearrange("(c p) t -> p c t", p=128), writes=[b_kt])
            P.dma("sp", vt[:], vD[t0:t0 + 512, h * 512:(h + 1) * 512].rearrange("(a p) n -> p a n", p=128), writes=[b_vt])
            if need_q:
                qt, b_qt = B["qt"]
                gt_, b_gt = B["gt"]
                P.dma("sp", qt[:], qT[h * 256:(h + 1) * 256, t0:t0 + 512].rearrange("(c p) t -> p c t", p=128), writes=[b_qt])
                P.dma("sp", gt_[:], gsT[h * 512:(h + 1) * 512, t0:t0 + 512].rearrange("(c p) t -> p c t", p=128), writes=[b_gt])

        def state_update(B, h, d, a):
            St, b_St = B["St"]
            kt, b_kt = B["kt"]
            vt, b_vt = B["vt"]
            B["it"] += 1
            ke, b_ke = B["kes"][B["it"] % 2]
            bt = kb.acq()
            for cc in range(2):
                P.op("pe", lambda e: e.transpose(out=kb.psb[bt][:, cc * 128:(cc + 1) * 128], in_=kt[:, cc, a * 128:(a + 1) * 128],
                                                 identity=kb.identb[:]), reads=[b_kt, kb.b_ident], writes=[kb.pbuf[bt]], signal=(cc == 1))
            yield
            P.op("dve", lambda e: e.tensor_scalar(out=ke[:], in0=kb.psb[bt][:, 0:256], scalar1=kcol[:, h, d:d + 1], scalar2=None, op0=ALU.mult),
                 reads=[kb.pbuf[bt], b_tab], writes=[b_ke])
            kb.rel(bt)
            yield
            bks = []
            for cc in range(2):
                bk = kb.acq()
                bks.append(bk)
                P.op("pe", lambda e: e.matmul(kb.ps[bk], lhsT=ke[:, cc * 128:(cc + 1) * 128], rhs=vt[:, a, :], start=True, stop=True),
                     reads=[b_ke, b_vt], writes=[kb.pbuf[bk]])
            yield
            for cc in range(2):
                bk = bks[cc]
                P.op("dve", lambda e: e.scalar_tensor_tensor(out=St[:, cc, :], in0=St[:, cc, :], scalar=g128[:, h, d:d + 1],
                                                             in1=kb.ps[bk], op0=ALU.mult, op1=ALU.add),
                     reads=[kb.pbuf[bk], b_St, b_tab], writes=[b_St])
                kb.rel(bk)
            yield

        def bwd_gen(h, B):
            St, b_St = B["St"]
            P.op("dve", lambda e: e.memset(St[:], 0.0), writes=[b_St])
            for b4 in range(nb4 - 1, -1, -1):
                load_kv(B, h, b4, False)
                yield
                for a in range(3, -1, -1):
                    J = b4 * 4 + a
                    B["it"] += 1
                    sb_, b_sb = B["Stb"][B["it"] % 2]
                    P.op("act", lambda e: e.activation(out=sb_[:], in_=St[:], func=AF.Copy), reads=[b_St], writes=[b_sb])
                    P.dma("pool", SbD[h, J], sb_[:], reads=[b_sb])
                    yield
                    if J > 0:
                        yield from state_update(B, h, 1, a)

        def fwd_gen(h, B):
            St, b_St = B["St"]
            kt, b_kt = B["kt"]
            vt, b_vt = B["vt"]
            qt, b_qt = B["qt"]
            gt_, b_gt = B["gt"]
            og, b_og = B["og"]
            P.op("dve", lambda e: e.memset(St[:], 0.0), writes=[b_St])
            sfb, b_sfb = B["Stb"][0]
            P.op("act", lambda e: e.activation(out=sfb[:], in_=St[:], func=AF.Copy), reads=[b_St], writes=[b_sfb])
            for b4 in range(nb4):
                load_kv(B, h, b4, True)
                yield
                for a in range(4):
                    J = b4 * 4 + a
                    B["it"] += 1
                    it = B["it"]
                    sbt, b_sbt = B["sbs"][it % 2]
                    P.dma("sp", sbt[:], SbD[h, J], writes=[b_sbt])
                    pt, b_pt = B["pts"][it % 2]
                    qd, b_qd = B["qds"][it % 2]
                    on, b_on = B["ons"][it % 2]
                    ss, b_ss = B["sss"][it % 2]
                    ts = slice(a * 128, (a + 1) * 128)
                    b1 = kb.acq()
                    for cc in range(2):
                        P.op("pe", lambda e: e.matmul(kb.ps[b1][:, 0:128], lhsT=kt[:, cc, ts], rhs=qt[:, cc, ts],
                                                      start=(cc == 0), stop=(cc == 1)),
                             reads=[b_kt, b_qt], writes=[kb.pbuf[b1]], signal=(cc == 1))
                    for d in range(2):
                        qb_ = AP(qrow, (h * 2 + d) * 128, [[1024, 128], [0, 2], [1, 128]])
                        P.op("pool", lambda e: e.tensor_tensor(out=qd[:, d, :, :], in0=qt[:, :, ts], in1=qb_, op=ALU.mult),
                             reads=[b_qt, b_tab], writes=[b_qd])
                    yield
                    P.op("dve", lambda e: e.tensor_tensor(out=pt[:], in0=kb.ps[b1][:, 0:128], in1=DT[:, h, :], op=ALU.mult),
                         reads=[kb.pbuf[b1], b_tab], writes=[b_pt])
                    kb.rel(b1)
                    yield
                    bo = kb.acq()
                    P.op("pe", lambda e: e.matmul(kb.ps[bo], lhsT=pt[:], rhs=vt[:, a, :], start=True, stop=False),
                         reads=[b_pt, b_vt], writes=[kb.pbuf[bo]], signal=False)
                    for cc in range(2):
                        P.op("pe", lambda e: e.matmul(kb.ps[bo], lhsT=qd[:, 0, cc, :], rhs=sfb[:, cc, :], start=False, stop=False),
                             reads=[b_qd, b_sfb], writes=[kb.pbuf[bo]], signal=False)
                    for cc in range(2):
                        P.op("pe", lambda e: e.matmul(kb.ps[bo], lhsT=qd[:, 1, cc, :], rhs=sbt[:, cc, :], start=False, stop=(cc == 1)),
                             reads=[b_qd, b_sbt], writes=[kb.pbuf[bo]], signal=(cc == 1))
                    yield
                    P.op("act", lambda e: e.activation(out=on[:], in_=kb.ps[bo], func=AF.Square, accum_out=ss[:, 0:1]),
                         reads=[kb.pbuf[bo]], writes=[b_on, b_ss])
                    P.op("act", lambda e: e.activation(out=ss[:, 1:2], in_=ss[:, 0:1], func=AF.Sqrt, scale=1.0 / 512, bias=kb.epsc[:, 0:1]),
                         reads=[b_ss], writes=[b_ss])
                    yield
                    P.op("dve", lambda e: e.reciprocal(out=ss[:, 1:2], in_=ss[:, 1:2]), reads=[b_ss], writes=[b_ss])
                    P.op("dve", lambda e: e.tensor_scalar(out=on[:], in0=kb.ps[bo], scalar1=ss[:, 1:2], scalar2=None, op0=ALU.mult),
                         reads=[kb.pbuf[bo], b_ss], writes=[b_on])
                    kb.rel(bo)
                    yield
                    bt = kb.acq()
                    for cc in range(4):
                        P.op("pe", lambda e: e.transpose(out=kb.psb[bt][:, cc * 128:(cc + 1) * 128], in_=on[:, cc * 128:(cc + 1) * 128],
                                                         identity=kb.identb[:]), reads=[b_on, kb.b_ident], writes=[kb.pbuf[bt]], signal=(cc == 3))
                    yield
                    P.op("dve", lambda e: e.tensor_tensor(out=og[:, :, ts], in0=kb.psb[bt][:, 0:512].rearrange("p (c t) -> p c t", t=128),
                                                          in1=gt_[:, :, ts], op=ALU.mult), reads=[kb.pbuf[bt], b_gt], writes=[b_og])
                    kb.rel(bt)
                    yield
                    if J < nblk - 1:
                        yield from state_update(B, h, 0, a)
                        P.op("act", lambda e: e.activation(out=sfb[:], in_=St[:], func=AF.Copy), reads=[b_St], writes=[b_sfb])
                        yield
                P.dma("pool", ogT[h * 512:(h + 1) * 512, b4 * 512:(b4 + 1) * 512].rearrange("(c p) t -> p c t", p=128), og[:], reads=[b_og])

        interleave([bwd_gen(h, BUFS[h]) for h in range(4)])
        P.flush()
        interleave([fwd_gen(h, BUFS[h]) for h in range(4)])
        P.flush()

    lt_residual(kb, ogT[:, 0:L], 2048, L, W["ret_w_out"][0], Xsrc, Xdst, l, 0, s)


MIXERS[1] = mixer_ret


def mixer_hgrn(kb, W, hT, Xsrc, Xdst, L, l, s):
    P = kb.P
    Lm = max(kb.Ls)
    win = W["hg_w_in"][0]
    hq = kb.scratch("hg_q", [2, 1024, Lm], BF16)
    hki = kb.scratch("hg_ki", [2, 1024, Lm], BF16)
    hke = kb.scratch("hg_ke", [2, 1024, Lm], BF16)
    hdl = kb.scratch("hg_dl", [2, 1024, Lm // 64], F32)
    gsT = kb.scratch("hg_gsT", [1024, Lm], BF16)
    vD = kb.scratch("hg_v", [Lm, 1024], BF16)
    ogT = kb.scratch("hg_ogT", [1024, Lm], BF16)
    SbD = kb.scratch("hg_Sb", [8, Lm // 64, 128, 128], BF16)
    LS = min(1024, L)
    nchs = LS // 64

    groups = [[(win, h * 128), (win, 2048 + h * 128), (win, 3072 + h * 128)] for h in range(8)]
    groups += [[(win, 4096 + j * 128)] for j in range(8)]

    def setup(es):
        c = dict()
        lbr = kb.sb(es, [128, 2, 4, 8], F32)
        c["b_lb"] = P.buf()
        for d in range(2):
            for ll in range(4):
                load_col(P, lbr[:, d, ll, :], W["hg_lb"][d, ll], c["b_lb"])
        P.op("act", lambda e: e.activation(out=lbr[:], in_=lbr[:], func=AF.Exp), reads=[c["b_lb"]], writes=[c["b_lb"]])
        c["lb"] = kb.sb(es, [128, 2, 8], F32)
        c["oml"] = kb.sb(es, [128, 2, 8], F32)
        tot = kb.sb(es, [128, 2, 8], F32)
        lb, oml = c["lb"], c["oml"]
        P.op("dve", lambda e: e.memset(lb[:], 0.0), writes=[c["b_lb"]])
        P.op("dve", lambda e: e.memset(tot[:], 0.0), writes=[c["b_lb"]])
        for ll in range(4):
            if ll < l:
                P.op("dve", lambda e, ll=ll: e.tensor_tensor(out=lb[:], in0=lb[:], in1=lbr[:, :, ll, :], op=ALU.add),
                     reads=[c["b_lb"]], writes=[c["b_lb"]])
            P.op("dve", lambda e, ll=ll: e.tensor_tensor(out=tot[:], in0=tot[:], in1=lbr[:, :, ll, :], op=ALU.add),
                 reads=[c["b_lb"]], writes=[c["b_lb"]])
        P.op("dve", lambda e: e.reciprocal(out=tot[:], in_=tot[:]), reads=[c["b_lb"]], writes=[c["b_lb"]])
        P.op("dve", lambda e: e.tensor_tensor(out=lb[:], in0=lb[:], in1=tot[:], op=ALU.mult), reads=[c["b_lb"]], writes=[c["b_lb"]])
        P.op("dve", lambda e: e.tensor_scalar(out=oml[:], in0=lb[:], scalar1=-1.0, scalar2=1.0, op0=ALU.mult, op1=ALU.add),
             reads=[c["b_lb"]], writes=[c["b_lb"]])
        c["zero"] = kb.sb(es, [128, 1], F32)
        P.op("dve", lambda e: e.memset(c["zero"][:], 0.0), writes=[c["b_lb"]])
        c["mask"] = kb.sb(es, [128, LS], F32)
        c["b_mask"] = P.buf()
        P.op("dve", lambda e: e.memset(c["mask"][:], 1.0), writes=[c["b_mask"]])
        P.op("dve", lambda e: e.memset(c["mask"][:].rearrange("p (c t) -> p c t", t=64)[:, :, 0:1], 0.0), writes=[c["b_mask"]])
        c["qs"] = (kb.sb(es, [128, LS], F32), P.buf())
        for nm in ("sg", "sn", "g", "k", "cb", "bb", "ep", "em", "kf"):
            c[nm] = [(kb.sb(es, [128, LS], F32), P.buf()) for _ in range(2)]
        for nm in ("oq", "oki", "oke"):
            c[nm] = [(kb.sb(es, [128, LS], BF16), P.buf()) for _ in range(2)]
        c["dl"] = [(kb.sb(es, [128, nchs], F32), P.buf()) for _ in range(2)]
        c["n"] = 0
        return c

    def epi(kb, c, gi, rows, t0, Ls):
        if gi >= 8:
            (g, bg), = rows
            o, b_o = c["oq"][gi % 2]
            P.op("act", lambda e: e.activation(out=o[:, 0:Ls], in_=g[:, 0:Ls], func=AF.Silu), reads=[bg], writes=[b_o])
            P.dma("pool", gsT[(gi - 8) * 128:(gi - 7) * 128, t0:t0 + Ls], o[:, 0:Ls], reads=[b_o])
            return
        h = gi
        (q, bq), (ff, bff), (fb, bfb) = rows
        qs, b_qs = c["qs"]
        P.op("act", lambda e: e.activation(out=qs[:], in_=q[:, 0:Ls], func=AF.Silu), reads=[bq], writes=[b_qs])
        nch = Ls // 64
        def dir_gen(d, r, br):
            sg, b_sg = c["sg"][d]
            sn, b_sn = c["sn"][d]
            g, b_g = c["g"][d]
            k, b_k = c["k"][d]
            cb, b_cb = c["cb"][d]
            bb, b_bb = c["bb"][d]
            ep, b_ep = c["ep"][d]
            em, b_em = c["em"][d]
            kf, b_kf = c["kf"][d]
            oq, b_oq = c["oq"][d]
            oki, b_oki = c["oki"][d]
            oke, b_oke = c["oke"][d]
            dl, b_dl = c["dl"][d]
            lbc, omc = c["lb"][:, d, h:h + 1], c["oml"][:, d, h:h + 1]
            P.op("act", lambda e: e.activation(out=sg[:], in_=r[:, 0:Ls], func=AF.Sigmoid), reads=[br], writes=[b_sg])
            P.op("act", lambda e: e.activation(out=sn[:], in_=r[:, 0:Ls], func=AF.Sigmoid, scale=-1.0), reads=[br], writes=[b_sn])
            yield
            P.op("dve", lambda e: e.tensor_scalar(out=g[:], in0=sg[:], scalar1=omc, scalar2=lbc, op0=ALU.mult, op1=ALU.add),
                 reads=[b_sg, c["b_lb"]], writes=[b_g])
            P.op("act", lambda e: e.activation(out=k[:], in_=sn[:], func=AF.Identity, scale=omc, bias=c["zero"][:, 0:1]),
                 reads=[b_sn, c["b_lb"]], writes=[b_k])
            yield
            P.op("act", lambda e: e.activation(out=g[:], in_=g[:], func=AF.Ln), reads=[b_g], writes=[b_g])
            yield
            P.op("dve", lambda e: e.tensor_tensor_scan(out=cb[:], data0=c["mask"][:], data1=g[:], initial=0.0, op0=ALU.mult, op1=ALU.add),
                 reads=[c["b_mask"], b_g], writes=[b_cb])
            yield
            if d == 0:
                bsrc, b_bsrc = cb, b_cb
                dlv = AP(ep, 63, [[LS, 128], [64, nch], [0, 64]])
                dls = AP(ep, 63, [[LS, 128], [64, nch]])
            else:
                P.op("dve", lambda e: e.tensor_tensor(out=bb[:], in0=g[:], in1=cb[:], op=ALU.subtract), reads=[b_g, b_cb], writes=[b_bb])
                yield
                totb = AP(cb, 63, [[LS, 128], [64, nch], [0, 64]])
                P.op("dve", lambda e: e.tensor_tensor(out=bb[:].rearrange("p (c t) -> p c t", t=64),
                                                      in0=bb[:].rearrange("p (c t) -> p c t", t=64), in1=totb, op=ALU.add),
                     reads=[b_bb, b_cb], writes=[b_bb])
                yield
                bsrc, b_bsrc = bb, b_bb
                dlv = AP(ep, 0, [[LS, 128], [64, nch], [0, 64]])
                dls = AP(ep, 0, [[LS, 128], [64, nch]])
            P.op("act", lambda e: e.activation(out=ep[:], in_=bsrc[:], func=AF.Exp), reads=[b_bsrc], writes=[b_ep])
            P.op("act", lambda e: e.activation(out=em[:], in_=bsrc[:], func=AF.Exp, scale=-1.0), reads=[b_bsrc], writes=[b_em])
            yield
            P.op("pool", lambda e: e.tensor_tensor(out=oq[:], in0=qs[:], in1=ep[:], op=ALU.mult), reads=[b_qs, b_ep], writes=[b_oq])
            P.op("dve", lambda e: e.tensor_tensor(out=kf[:], in0=k[:], in1=em[:], op=ALU.mult), reads=[b_k, b_em], writes=[b_kf])
            yield
            P.op("act", lambda e: e.activation(out=oki[:], in_=kf[:], func=AF.Copy), reads=[b_kf], writes=[b_oki])
            P.op("pool", lambda e: e.tensor_tensor(out=oke[:].rearrange("p (c t) -> p c t", t=64),
                                                   in0=kf[:].rearrange("p (c t) -> p c t", t=64), in1=dlv, op=ALU.mult),
                 reads=[b_kf, b_ep], writes=[b_oke])
            P.op("dve", lambda e: e.tensor_copy(out=dl[:, 0:nch], in_=dls), reads=[b_ep], writes=[b_dl])
            yield
            rs = slice(h * 128, (h + 1) * 128)
            P.dma("pool", hq[d, rs, t0:t0 + Ls], oq[:, 0:Ls], reads=[b_oq])
            P.dma("pool", hki[d, rs, t0:t0 + Ls], oki[:, 0:Ls], reads=[b_oki])
            P.dma("pool", hke[d, rs, t0:t0 + Ls], oke[:, 0:Ls], reads=[b_oke])
            P.dma("pool", hdl[d, rs, t0 // 64:t0 // 64 + nch], dl[:, 0:nch], reads=[b_dl])

        interleave([dir_gen(0, ff, bff), dir_gen(1, fb, bfb)])

    phase_lf(kb, hT, 1024, L, groups, epi, Lseg=LS, nrow=3, setup=setup, depth=1, evac="alt", cast="act")
    lt_plain(kb, hT, 1024, L, win[:, 1024:2048], 1024, vD)

    nb4 = L // 512
    NWAY = 4
    kb.reserved = set(range(8 - NWAY, 8))
    with ExitStack() as es:
        mk = kb.sb(es, [128, 256], BF16)
        b_mk = P.buf()
        P.dma("pool", mk[:], kb.C["hg_mask"], writes=[b_mk])
        gcol = kb.sb(es, [128, 1], F32)
        b_gc = P.buf()
        load_col(P, gcol[:], W["hg_gain"][0], b_gc)

        def mkbufs():
            B = dict()
            tok = [P.buf(), P.buf()]
            B["St"] = (kb.sb(es, [128, 128], F32), P.buf())
            B["Sfb"] = [(kb.sb(es, [128, 128], BF16), P.buf()) for _ in range(2)]
            B["stg"] = [(kb.sb(es, [128, 8, 128], BF16), P.buf()) for _ in range(2)]
            B["kes"] = [(kb.sb(es, [128, 512], BF16), tok[i]) for i in range(2)]
            B["kis"] = [(kb.sb(es, [128, 2, 512], BF16), tok[i]) for i in range(2)]
            B["qds"] = [(kb.sb(es, [128, 2, 512], BF16), tok[i]) for i in range(2)]
            B["vts"] = [(kb.sb(es, [128, 4, 128], BF16), tok[i]) for i in range(2)]
            B["gts"] = [(kb.sb(es, [128, 512], BF16), tok[i]) for i in range(2)]
            B["ogs"] = [(kb.sb(es, [128, 512], BF16), P.buf()) for _ in range(2)]
            B["dls"] = [(kb.sb(es, [128, 8], F32), tok[i]) for i in range(2)]
            B["sbl"] = [(kb.sb(es, [128, 8, 128], BF16), tok[i]) for i in range(2)]
            B["ktm"] = [(kb.sb(es, [128, 128], BF16), P.buf()) for _ in range(2)]
            B["pts"] = [(kb.sb(es, [128, 256], BF16), P.buf()) for _ in range(2)]
            B["ons"] = [(kb.sb(es, [128, 128], BF16), P.buf()) for _ in range(2)]
            B["sss"] = [(kb.sb(es, [128, 2], F32), P.buf()) for _ in range(2)]
            B["cnt"] = 0
            return B

        BUFS = [mkbufs() for _ in range(NWAY)]

        def ktrans(B, ke, b_ke, a):
            B["cnt"] += 1
            kt, b_kt = B["ktm"][B["cnt"] % 2]
            bt = kb.bank()
            P.op("pe", lambda e: e.transpose(out=kb.psb[bt][:, 0:128], in_=ke[:, a * 128:(a + 1) * 128], identity=kb.identb[:]),
                 reads=[b_ke, kb.b_ident], writes=[kb.pbuf[bt]])
            P.op("act", lambda e: e.activation(out=kt[:], in_=kb.psb[bt][:, 0:128], func=AF.Copy), reads=[kb.pbuf[bt]], writes=[b_kt])
            return kt, b_kt

        def upd_chunk(B, kt, b_kt, vt, b_vt, dl, b_dl, a, hf):
            St, b_St = B["St"]
            bk = kb.bank()
            lo = hf * 64
            P.op("pe", lambda e: e.matmul(kb.ps[bk][:, 0:128], lhsT=kt[lo:lo + 64, :], rhs=vt[lo:lo + 64, a, :], start=True, stop=True),
                 reads=[b_kt, b_vt], writes=[kb.pbuf[bk]])
            ci = a * 2 + hf
            P.op("dve", lambda e: e.scalar_tensor_tensor(out=St[:], in0=St[:], scalar=dl[:, ci:ci + 1], in1=kb.ps[bk][:, 0:128],
                                                         op0=ALU.mult, op1=ALU.add), reads=[kb.pbuf[bk], b_St, b_dl], writes=[b_St])

        def bwd_gen(h, B):
            rs = slice(h * 128, (h + 1) * 128)
            St, b_St = B["St"]
            P.op("dve", lambda e: e.memset(St[:], 0.0), writes=[b_St])
            for b4 in range(nb4 - 1, -1, -1):
                t0 = b4 * 512
                ke, b_ke = B["kes"][b4 % 2]
                vt, b_vt = B["vts"][b4 % 2]
                dl, b_dl = B["dls"][b4 % 2]
                sg_, b_sg = B["stg"][b4 % 2]
                P.dma("sp", ke[:], hke[1, rs, t0:t0 + 512], writes=[b_ke])
                P.dma("sp", vt[:], vD[t0:t0 + 512, rs].rearrange("(a p) n -> p a n", p=128), writes=[b_vt])
                P.dma("sp", dl[:], hdl[1, rs, t0 // 64:t0 // 64 + 8], writes=[b_dl])
                yield
                for a in range(3, -1, -1):
                    kt, b_kt = ktrans(B, ke, b_ke, a)
                    yield
                    for hf in (1, 0):
                        ci = a * 2 + hf
                        P.op("act", lambda e: e.activation(out=sg_[:, ci, :], in_=St[:], func=AF.Copy), reads=[b_St], writes=[b_sg])
                        upd_chunk(B, kt, b_kt, vt, b_vt, dl, b_dl, a, hf)
                        yield
                P.dma("pool", SbD[h, t0 // 64:t0 // 64 + 8].rearrange("c p n -> p c n"), sg_[:], reads=[b_sg])

        def fwd_gen(h, B, bo):
            rs = slice(h * 128, (h + 1) * 128)
            St, b_St = B["St"]
            P.op("dve", lambda e: e.memset(St[:], 0.0), writes=[b_St])
            sfi = 0
            sf, b_sf = B["Sfb"][0]
            P.op("act", lambda e: e.activation(out=sf[:], in_=St[:], func=AF.Copy), reads=[b_St], writes=[b_sf])
            for b4 in range(nb4):
                t0 = b4 * 512
                ke, b_ke = B["kes"][b4 % 2]
                ki, b_ki = B["kis"][b4 % 2]
                qd, b_qd = B["qds"][b4 % 2]
                vt, b_vt = B["vts"][b4 % 2]
                gt_, b_gt = B["gts"][b4 % 2]
                og, b_og = B["ogs"][b4 % 2]
                dl, b_dl = B["dls"][b4 % 2]
                sb_, b_sb = B["sbl"][b4 % 2]
                P.dma("sp", ke[:], hke[0, rs, t0:t0 + 512], writes=[b_ke])
                for d in range(2):
                    P.dma("sp", ki[:, d, :], hki[d, rs, t0:t0 + 512], writes=[b_ki])
                    P.dma("sp", qd[:, d, :], hq[d, rs, t0:t0 + 512], writes=[b_qd])
                P.dma("sp", vt[:], vD[t0:t0 + 512, rs].rearrange("(a p) n -> p a n", p=128), writes=[b_vt])
                P.dma("sp", gt_[:], gsT[rs, t0:t0 + 512], writes=[b_gt])
                P.dma("sp", dl[:], hdl[0, rs, t0 // 64:t0 // 64 + 8], writes=[b_dl])
                P.dma("sp", sb_[:], SbD[h, t0 // 64:t0 // 64 + 8].rearrange("c p n -> p c n"), writes=[b_sb])
                yield
                for a in range(4):
                    B["cnt"] += 1
                    n = B["cnt"]
                    ts = slice(a * 128, (a + 1) * 128)
                    pt, b_pt = B["pts"][n % 2]
                    on, b_on = B["ons"][n % 2]
                    ss, b_ss = B["sss"][n % 2]
                    b1 = bo
                    for d in range(2):
                        P.op("pe", lambda e: e.matmul(kb.ps[b1][:, 128 + d * 128:128 + (d + 1) * 128], lhsT=ki[:, d, ts], rhs=qd[:, d, ts],
                                                      start=True, stop=True), reads=[b_ki, b_qd], writes=[kb.pbuf[b1]], signal=(d == 1))
                    kt, b_kt = ktrans(B, ke, b_ke, a)
                    yield
                    P.op("dve", lambda e: e.tensor_tensor(out=pt[:], in0=kb.ps[b1][:, 128:384], in1=mk[:], op=ALU.mult),
                         reads=[kb.pbuf[b1], b_mk], writes=[b_pt])
                    yield
                    P.op("pe", lambda e: e.matmul(kb.ps[bo][:, 0:128], lhsT=pt[:, 0:128], rhs=vt[:, a, :], start=True, stop=False),
                         reads=[b_pt, b_vt], writes=[kb.pbuf[bo]], signal=False)
                    P.op("pe", lambda e: e.matmul(kb.ps[bo][:, 0:128], lhsT=pt[:, 128:256], rhs=vt[:, a, :], start=False, stop=False),
                         reads=[b_pt, b_vt], writes=[kb.pbuf[bo]], signal=False)
                    for hf in range(2):
                        ci = a * 2 + hf
                        lo = hf * 64
                        P.op("pe", lambda e: e.matmul(kb.ps[bo][lo:lo + 64, 0:128], lhsT=qd[:, 1, a * 128 + lo:a * 128 + lo + 64],
                                                      rhs=sb_[:, ci, :], start=False, stop=False),
                             reads=[b_qd, b_sb], writes=[kb.pbuf[bo]], signal=False)
                    for hf in range(2):
                        ci = a * 2 + hf
                        lo = hf * 64
                        sf, b_sf = B["Sfb"][sfi % 2]
                        P.op("pe", lambda e: e.matmul(kb.ps[bo][lo:lo + 64, 0:128], lhsT=qd[:, 0, a * 128 + lo:a * 128 + lo + 64],
                                                      rhs=sf[:], start=False, stop=(hf == 1)),
                             reads=[b_qd, b_sf], writes=[kb.pbuf[bo]], signal=(hf == 1))
                        upd_chunk(B, kt, b_kt, vt, b_vt, dl, b_dl, a, hf)
                        sfi += 1
                        sf2, b_sf2 = B["Sfb"][sfi % 2]
                        P.op("act", lambda e: e.activation(out=sf2[:], in_=St[:], func=AF.Copy), reads=[b_St], writes=[b_sf2])
                        yield
                    P.op("act", lambda e: e.activation(out=on[:], in_=kb.ps[bo][:, 0:128], func=AF.Square, accum_out=ss[:, 0:1]),
                         reads=[kb.pbuf[bo]], writes=[b_on, b_ss])
                    P.op("act", lambda e: e.activation(out=ss[:, 1:2], in_=ss[:, 0:1], func=AF.Sqrt, scale=1.0 / 128, bias=kb.epsc[:, 0:1]),
                         reads=[b_ss], writes=[b_ss])
                    yield
                    P.op("dve", lambda e: e.reciprocal(out=ss[:, 1:2], in_=ss[:, 1:2]), reads=[b_ss], writes=[b_ss])
                    P.op("dve", lambda e: e.tensor_scalar(out=on[:], in0=kb.ps[bo][:, 0:128], scalar1=ss[:, 1:2], scalar2=None, op0=ALU.mult),
                         reads=[kb.pbuf[bo], b_ss], writes=[b_on])
                    yield
                    bt = bo
                    P.op("pe", lambda e: e.transpose(out=kb.psb[bt][:, 768:896], in_=on[:], identity=kb.identb[:]),
                         reads=[b_on, kb.b_ident], writes=[kb.pbuf[bt]])
                    yield
                    P.op("dve", lambda e: e.scalar_tensor_tensor(out=og[:, ts], in0=kb.psb[bt][:, 768:896], scalar=gcol[:, 0:1], in1=gt_[:, ts],
                                                                 op0=ALU.mult, op1=ALU.mult), reads=[kb.pbuf[bt], b_gc, b_gt], writes=[b_og])
                P.dma("pool", ogT[rs, t0:t0 + 512], og[:], reads=[b_og])

        for h0 in range(0, 8, NWAY):
            interleave([bwd_gen(h0 + i, BUFS[i]) for i in range(NWAY)])
            P.flush()
            interleave([fwd_gen(h0 + i, BUFS[i], 8 - NWAY + i) for i in range(NWAY)])
            P.flush()
        kb.reserved = set()

    lt_residual(kb, ogT[:, 0:L], 1024, L, W["hg_w_out"][0], Xsrc, Xdst, l, 0, s)


MIXERS[3] = mixer_hgrn


def fft_f1(kb, src, C, L, A):
    P = kb.P
    N1 = 2 * L // 128
    K = N1 // 2
    NB = 4096 // C
    F1t = kb.C[f"F1_{N1}"]
    with ExitStack() as es:
        f1 = kb.sb(es, [K, 2, N1], BF16)
        b_f1 = P.buf()
        P.dma("sp", f1[:], F1t, writes=[b_f1])
        xs = [(kb.sb(es, [K, NB * C], BF16), P.buf()) for _ in range(2)]
        As = [(kb.sb(es, [N1, 2, NB * C], BF16), P.buf()) for _ in range(2)]
        srcv = src.rearrange("(a b) c -> a b c", b=128)
        n = 0
        for i, n2c in enumerate(range(0, 128, NB)):
            x, b_x = xs[i % 2]
            At, b_A = As[i % 2]
            P.dma("sp", x[:].rearrange("p (b c) -> p b c", c=C), srcv[:, n2c:n2c + NB, :], writes=[b_x])
            for c0 in range(0, NB * C, 512):
                for ri in range(2):
                    bk = kb.bank()
                    P.op("pe", lambda e: e.matmul(kb.ps[bk][0:N1, :], lhsT=f1[:, ri, :], rhs=x[:, c0:c0 + 512], start=True, stop=True),
                         reads=[b_f1, b_x], writes=[kb.pbuf[bk]])
                    n += 1
                    if n % 2:
                        P.op("act", lambda e: e.activation(out=At[:, ri, c0:c0 + 512], in_=kb.ps[bk][0:N1, :], func=AF.Copy),
                             reads=[kb.pbuf[bk]], writes=[b_A])
                    else:
                        P.op("dve", lambda e: e.tensor_copy(out=At[:, ri, c0:c0 + 512], in_=kb.ps[bk][0:N1, :]),
                             reads=[kb.pbuf[bk]], writes=[b_A])
            for ri in range(2):
                P.dma("pool", A[ri, 0:N1, n2c:n2c + NB, 0:C], At[:, ri, :].rearrange("p (b c) -> p b c", c=C), reads=[b_A])
        P.flush()


def fft_f2(kb, A, C, L, mode, Hd, rn=None, Bd=None):
    P = kb.P
    N1 = 2 * L // 128
    Gt, CTt = kb.C[f"G_{N1}"], kb.C[f"CT_{N1}"]
    with ExitStack() as es:
        NB_ = 3
        Ats = [(kb.sb(es, [128, 2, C], BF16), P.buf()) for _ in range(NB_)]
        Gs = [(kb.sb(es, [128, 4, 128], BF16), P.buf()) for _ in range(NB_)]
        Hs = [(kb.sb(es, [128, 2, 1024], BF16), P.buf()) for _ in range(NB_)]
        if mode == "conv":
            Cs = [(kb.sb(es, [128, 3, 128], BF16), P.buf()) for _ in range(NB_)]
            Ts = [(kb.sb(es, [128, 4, 512], F32), P.buf()) for _ in range(2)]
            Ys = [(kb.sb(es, [128, 2, 512], BF16), P.buf()) for _ in range(2)]
            Bs = [(kb.sb(es, [128, 2, C], BF16), P.buf()) for _ in range(NB_)]
        ncb = (C if mode == "conv" else 1024) // 512
        items = [(k1, cb) for k1 in range(N1) for cb in range(ncb)]
        st = {}

        def loads(k1):
            At, b_A = Ats[k1 % NB_]
            G, b_G = Gs[k1 % NB_]
            for ri in range(2):
                P.dma("sp", At[:, ri, :], A[ri, k1, :, 0:C], writes=[b_A])
            P.dma("sp", G[:], Gt[k1], writes=[b_G])
            if mode == "conv":
                Ct, b_C = Cs[k1 % NB_]
                Ht, b_H = Hs[k1 % NB_]
                P.dma("sp", Ct[:], CTt[k1], writes=[b_C])
                for ri in range(2):
                    P.dma("sp", Ht[:, ri, :], Hd[ri, k1], writes=[b_H])

        def S1(n):
            k1, cb = items[n]
            c0 = cb * 512
            At, b_A = Ats[k1 % NB_]
            G, b_G = Gs[k1 % NB_]
            br, bi = kb.bank(), kb.bank()
            st[n] = (br, bi)
            if mode == "conv":
                seq_r = [(0, 0, c0), (2, 1, c0)]
                seq_i = [(1, 0, c0), (0, 1, c0)]
            else:
                seq_r = [(0, 0, c0), (2, 1, c0), (0, 0, 1024 + c0), (2, 1, 1024 + c0)]
                seq_i = [(1, 0, c0), (0, 1, c0), (2, 0, 1024 + c0), (3, 1, 1024 + c0)]
            for bk, seq in ((br, seq_r), (bi, seq_i)):
                for i, (gi_, ri, cc) in enumerate(seq):
                    last = (i == len(seq) - 1)
                    P.op("pe", lambda e: e.matmul(kb.ps[bk], lhsT=G[:, gi_, :], rhs=At[:, ri, cc:cc + 512], start=(i == 0), stop=last),
                         reads=[b_G, b_A], writes=[kb.pbuf[bk]], signal=last)

        def S2(n):
            k1, cb = items[n]
            c0 = cb * 512
            br, bi = st[n]
            Ht, b_H = Hs[k1 % NB_]
            if mode == "conv":
                T, b_T = Ts[n % 2]
                Y, b_Y = Ys[n % 2]
                hr, hi = Ht[:, 0, c0:c0 + 512], Ht[:, 1, c0:c0 + 512]
                P.op("dve", lambda e: e.tensor_tensor(out=T[:, 0, :], in0=kb.ps[br], in1=hr, op=ALU.mult), reads=[kb.pbuf[br], b_H], writes=[b_T])
                P.op("dve", lambda e: e.tensor_tensor(out=T[:, 1, :], in0=kb.ps[bi], in1=hi, op=ALU.mult), reads=[kb.pbuf[bi], b_H], writes=[b_T])
                P.op("dve", lambda e: e.tensor_tensor(out=T[:, 2, :], in0=kb.ps[br], in1=hi, op=ALU.mult), reads=[kb.pbuf[br], b_H], writes=[b_T])
                P.op("dve", lambda e: e.tensor_tensor(out=T[:, 3, :], in0=kb.ps[bi], in1=hr, op=ALU.mult), reads=[kb.pbuf[bi], b_H], writes=[b_T])
                P.op("pool", lambda e: e.tensor_tensor(out=Y[:, 0, :], in0=T[:, 0, :], in1=T[:, 1, :], op=ALU.subtract), reads=[b_T], writes=[b_Y])
                P.op("pool", lambda e: e.tensor_tensor(out=Y[:, 1, :], in0=T[:, 2, :], in1=T[:, 3, :], op=ALU.add), reads=[b_T], writes=[b_Y])
            else:
                for ri, bk in ((0, br), (1, bi)):
                    P.op("dve", lambda e: e.tensor_tensor(out=Ht[:, ri, c0:c0 + 512], in0=kb.ps[bk], in1=rn[0][:, c0:c0 + 512], op=ALU.mult),
                         reads=[kb.pbuf[bk], rn[1]], writes=[b_H])
                if cb == ncb - 1:
                    for ri in range(2):
                        P.dma("pool", Hd[ri, k1], Ht[:, ri, :], reads=[b_H])

        def S3(n):
            k1, cb = items[n]
            c0 = cb * 512
            Ct, b_C = Cs[k1 % NB_]
            Bt, b_B = Bs[k1 % NB_]
            Y, b_Y = Ys[n % 2]
            b2r, b2i = kb.bank(), kb.bank()
            P.op("pe", lambda e: e.matmul(kb.ps[b2r], lhsT=Ct[:, 0, :], rhs=Y[:, 0, :], start=True, stop=False),
                 reads=[b_C, b_Y], writes=[kb.pbuf[b2r]], signal=False)
            P.op("pe", lambda e: e.matmul(kb.ps[b2r], lhsT=Ct[:, 1, :], rhs=Y[:, 1, :], start=False, stop=True),
                 reads=[b_C, b_Y], writes=[kb.pbuf[b2r]])
            P.op("pe", lambda e: e.matmul(kb.ps[b2i], lhsT=Ct[:, 0, :], rhs=Y[:, 1, :], start=True, stop=False),
                 reads=[b_C, b_Y], writes=[kb.pbuf[b2i]], signal=False)
            P.op("pe", lambda e: e.matmul(kb.ps[b2i], lhsT=Ct[:, 2, :], rhs=Y[:, 0, :], start=False, stop=True),
                 reads=[b_C, b_Y], writes=[kb.pbuf[b2i]])
            P.op("act", lambda e: e.activation(out=Bt[:, 0, c0:c0 + 512], in_=kb.ps[b2r], func=AF.Copy), reads=[kb.pbuf[b2r]], writes=[b_B])
            P.op("act", lambda e: e.activation(out=Bt[:, 1, c0:c0 + 512], in_=kb.ps[b2i], func=AF.Copy), reads=[kb.pbuf[b2i]], writes=[b_B])
            if cb == ncb - 1:
                for ri in range(2):
                    P.dma("pool", Bd[ri, k1, :, 0:C], Bt[:, ri, :], reads=[b_B])

        loads(0)
        if N1 > 1:
            loads(1)
        S1(0)
        for n in range(len(items)):
            k1, cb = items[n]
            if cb == 0 and k1 + 2 < N1:
                loads(k1 + 2)
            if n + 1 < len(items):
                S1(n + 1)
            S2(n)
            if mode == "conv":
                S3(n)
            del st[n]
        P.flush()


def fft_f3(kb, Bd, C, L, ytm):
    P = kb.P
    N1 = 2 * L // 128
    M = N1 // 2
    NB = 4096 // C
    with ExitStack() as es:
        fi = kb.sb(es, [N1, 2, M], BF16)
        b_fi = P.buf()
        P.dma("sp", fi[:], kb.C[f"Finv_{N1}"], writes=[b_fi])
        Bs = [(kb.sb(es, [N1, 2, NB * C], BF16), P.buf()) for _ in range(2)]
        ys = [(kb.sb(es, [M, NB * C], F32), P.buf()) for _ in range(2)]
        yv = ytm.rearrange("(a b) c -> a b c", b=128)
        n = 0
        for i, n2c in enumerate(range(0, 128, NB)):
            Bt, b_B = Bs[i % 2]
            y, b_y = ys[i % 2]
            for ri in range(2):
                P.dma("sp", Bt[:, ri, :].rearrange("p (b c) -> p b c", c=C), Bd[ri, 0:N1, n2c:n2c + NB, 0:C], writes=[b_B])
            for c0 in range(0, NB * C, 512):
                bk = kb.bank()
                P.op("pe", lambda e: e.matmul(kb.ps[bk][0:M, :], lhsT=fi[:, 0, :], rhs=Bt[:, 0, c0:c0 + 512], start=True, stop=False),
                     reads=[b_fi, b_B], writes=[kb.pbuf[bk]], signal=False)
                P.op("pe", lambda e: e.matmul(kb.ps[bk][0:M, :], lhsT=fi[:, 1, :], rhs=Bt[:, 1, c0:c0 + 512], start=False, stop=True),
                     reads=[b_fi, b_B], writes=[kb.pbuf[bk]])
                n += 1
                if n % 2:
                    P.op("act", lambda e: e.activation(out=y[:, c0:c0 + 512], in_=kb.ps[bk][0:M, :], func=AF.Copy), reads=[kb.pbuf[bk]], writes=[b_y])
                else:
                    P.op("dve", lambda e: e.tensor_copy(out=y[:, c0:c0 + 512], in_=kb.ps[bk][0:M, :]), reads=[kb.pbuf[bk]], writes=[b_y])
            P.dma("pool", yv[:, n2c:n2c + NB, :], y[:].rearrange("p (b c) -> p b c", c=C), reads=[b_y])
        P.flush()


def hyena_taps(kb, W, L, s):
    P = kb.P
    Lm = max(kb.Ls)
    N1 = 2 * L // 128
    tD = kb.scratch("hy_taps", [Lm, 2048], BF16)
    A = kb.scratch("hy_A", [2, 2 * Lm // 128, 128, 2048], BF16)
    Hd = kb.scratch(f"hy_H{s}", [2, N1, 128, 1024], BF16)
    kb.Hd[s] = Hd
    ZT = kb.C[f"ZT_{L}"]
    negT = kb.C[f"negT_{L}"]
    rn = kb.sb(None, [128, 1024], F32, "hy_rn")
    b_rn = Buf()
    with ExitStack() as es:
        zt = kb.sb(es, [33, L], F32)
        b_zt = P.buf()
        P.dma("sp", zt[:], ZT, writes=[b_zt])
        w1 = kb.sb(es, [33, 64], F32)
        w2 = kb.sb(es, [64, 64], F32)
        w3 = kb.sb(es, [64, 2048], F32)
        cols = kb.sb(es, [64, 3], F32)
        b_w = P.buf()
        P.dma("sp", w1[:], W["hy_w1"][0], writes=[b_w])
        P.dma("sp", w2[:], W["hy_w2"][0], writes=[b_w])
        P.dma("sp", w3[:], W["hy_w3"][0], writes=[b_w])
        load_col(P, cols[:, 0:1], W["hy_b1"][0], b_w, 64)
        load_col(P, cols[:, 1:2], W["hy_b2"][0], b_w, 64)
        load_col(P, cols[:, 2:3], W["hy_freq"][0], b_w, 64)
        nt = kb.sb(es, [128, L // 128], F32)
        P.dma("sp", nt[:], negT, writes=[b_w])
        adec = kb.sb(es, [128, 2048], F32)
        dd = W["hy_decay"][0:1]
        P.dma("sp", adec[:], AP(dd.tensor, dd.offset, [[0, 128], [1, 2048]]), writes=[b_w])
        P.op("act", lambda e: e.activation(out=adec[:], in_=adec[:], func=AF.Abs), reads=[b_w], writes=[b_w])
        onesf = kb.sb(es, [128, 128], BF16)
        P.op("dve", lambda e: e.memset(onesf[:], 1.0), writes=[b_w])
        tabs_ = [(kb.sb(es, [128, 2048], BF16), P.buf()) for _ in range(2)]
        h1 = kb.sb(es, [64, L], F32)
        h2 = kb.sb(es, [64, L], F32)
        b_h1, b_h2 = P.buf(), P.buf()
        arg = [(kb.sb(es, [64, 512], F32), P.buf()) for _ in range(2)]
        ki = [(kb.sb(es, [64, 512], I32), P.buf()) for _ in range(2)]
        kf = [(kb.sb(es, [64, 512], F32), P.buf()) for _ in range(2)]
        TWO_PI = 2.0 * math.pi

        def sin_layer(wt, src, b_src, bcol, dst, b_dst, Kdim):
            for i, c0 in enumerate(range(0, L, 512)):
                a_, b_a = arg[i % 2]
                k_, b_k = ki[i % 2]
                f_, b_f = kf[i % 2]
                bk = i % 4
                P.op("pe", lambda e: e.matmul(kb.ps[bk][0:64, :], lhsT=wt[:], rhs=src[0:Kdim, c0:c0 + 512], start=True, stop=True),
                     reads=[b_w, b_src], writes=[kb.pbuf[bk]])
                P.op("dve", lambda e: e.tensor_scalar(out=a_[:], in0=kb.ps[bk][0:64, :], scalar1=cols[:, bcol:bcol + 1], scalar2=cols[:, 2:3],
                                                      op0=ALU.add, op1=ALU.mult), reads=[kb.pbuf[bk], b_w], writes=[b_a])
                P.op("dve", lambda e: e.tensor_scalar(out=k_[:], in0=a_[:], scalar1=1.0 / TWO_PI, scalar2=None, op0=ALU.mult), reads=[b_a], writes=[b_k])
                P.op("dve", lambda e: e.tensor_copy(out=f_[:], in_=k_[:]), reads=[b_k], writes=[b_f])
                P.op("dve", lambda e: e.scalar_tensor_tensor(out=a_[:], in0=f_[:], scalar=-TWO_PI, in1=a_[:], op0=ALU.mult, op1=ALU.add),
                     reads=[b_f, b_a], writes=[b_a])
                P.op("dve", lambda e: e.tensor_scalar(out=f_[:], in0=a_[:], scalar1=math.pi, scalar2=-TWO_PI, op0=ALU.is_gt, op1=ALU.mult), reads=[b_a], writes=[b_f])
                P.op("dve", lambda e: e.tensor_tensor(out=a_[:], in0=a_[:], in1=f_[:], op=ALU.add), reads=[b_a, b_f], writes=[b_a])
                P.op("dve", lambda e: e.tensor_scalar(out=f_[:], in0=a_[:], scalar1=-math.pi, scalar2=TWO_PI, op0=ALU.is_lt, op1=ALU.mult), reads=[b_a], writes=[b_f])
                P.op("dve", lambda e: e.tensor_tensor(out=a_[:], in0=a_[:], in1=f_[:], op=ALU.add), reads=[b_a, b_f], writes=[b_a])
                P.op("dve", lambda e: e.tensor_scalar(out=a_[:], in0=a_[:], scalar1=3.14159, scalar2=-3.14159, op0=ALU.min, op1=ALU.max), reads=[b_a], writes=[b_a])
                P.op("act", lambda e: e.activation(out=dst[:, c0:c0 + 512], in_=a_[:], func=AF.Sin), reads=[b_a], writes=[b_dst])

        sin_layer(w1, zt, b_zt, 0, h1, b_h1, 33)
        sin_layer(w2, h1, b_h1, 1, h2, b_h2, 64)
        wins = [(kb.sb(es, [128, 2048], F32), P.buf()) for _ in range(2)]
        tps = [(kb.sb(es, [128, 2048], F32), P.buf()) for _ in range(2)]
        tbs = [(kb.sb(es, [128, 2048], BF16), P.buf()) for _ in range(2)]
        nti = L // 128
        for ti in range(nti):
            wn, b_wn = wins[ti % 2]
            tp, b_tp = tps[ti % 2]
            tb, b_tb = tbs[ti % 2]
            P.op("act", lambda e: e.activation(out=wn[:], in_=adec[:], func=AF.Exp, scale=nt[:, ti:ti + 1]), reads=[b_w], writes=[b_wn])
            for nb in range(4):
                bk = nb
                P.op("pe", lambda e: e.matmul(kb.ps[bk], lhsT=h2[:, ti * 128:(ti + 1) * 128], rhs=w3[:, nb * 512:(nb + 1) * 512], start=True, stop=True),
                     reads=[b_h2, b_w], writes=[kb.pbuf[bk]])
                P.op("dve", lambda e: e.tensor_tensor(out=tp[:, nb * 512:(nb + 1) * 512], in0=kb.ps[bk], in1=wn[:, nb * 512:(nb + 1) * 512], op=ALU.mult),
                     reads=[kb.pbuf[bk], b_wn], writes=[b_tp])
            if ti == 0:
                P.op("dve", lambda e: e.memset(tp[0:1, 1024:2048], 0.0), writes=[b_tp])
            P.op("act", lambda e: e.activation(out=tb[:], in_=tp[:], func=AF.Copy), reads=[b_tp], writes=[b_tb])
            P.dma("sp", tD[ti * 128:(ti + 1) * 128, :], tb[:], reads=[b_tb])
            ta, b_ta = tabs_[ti % 2]
            P.op("act", lambda e: e.activation(out=ta[:], in_=tp[:], func=AF.Abs), reads=[b_tp], writes=[b_ta])
            for nb in range(4):
                bk = 4 + nb
                P.op("pe", lambda e: e.matmul(kb.ps[bk], lhsT=onesf[:], rhs=ta[:, nb * 512:(nb + 1) * 512], start=(ti == 0), stop=(ti == nti - 1)),
                     reads=[b_w, b_ta], writes=[kb.pbuf[bk]], signal=(ti == nti - 1 or nb == 3))
        for hf in range(2):
            P.op("act", lambda e: e.activation(out=rn[:, hf * 512:(hf + 1) * 512], in_=kb.ps[6 + hf], func=AF.Copy), reads=[kb.pbuf[6 + hf]], writes=[b_rn])
            P.op("dve", lambda e: e.tensor_tensor(out=rn[:, hf * 512:(hf + 1) * 512], in0=kb.ps[4 + hf], in1=rn[:, hf * 512:(hf + 1) * 512], op=ALU.add),
                 reads=[kb.pbuf[4 + hf], b_rn], writes=[b_rn])
        P.op("dve", lambda e: e.reciprocal(out=rn[:], in_=rn[:]), reads=[b_rn], writes=[b_rn])
        P.flush()
    fft_f1(kb, tD[0:L, :], 2048, L, A)
    fft_f2(kb, A, 2048, L, "H", Hd, rn=(rn, b_rn))


def mixer_hyena(kb, W, hT, Xsrc, Xdst, L, l, s):
    P = kb.P
    Lm = max(kb.Ls)
    win = W["hy_w_in"][0]
    if not hasattr(kb, "Hd"):
        kb.Hd = {}
    if s not in kb.Hd:
        hyena_taps(kb, W, L, s)
    x0T = kb.scratch("hy_x0T", [1024, Lm], F32)
    uT = kb.scratch("hy_uT", [1024, Lm], F32)
    utm = kb.scratch("hy_utm", [Lm, 1024], BF16)
    A = kb.scratch("hy_A", [2, 2 * Lm // 128, 128, 2048], BF16)
    Bd = kb.scratch("hy_B", [2, 2 * Lm // 128, 128, 1024], BF16)
    ytm = kb.scratch("hy_ytm", [Lm, 1024], F32)
    mT = kb.scratch("hy_mT", [1024, Lm], BF16)

    groups = [[(win, j * 128)] for j in range(8)]
    groups += [[(win, 1024 + j * 128), (win, 2048 + j * 128)] for j in range(8)]

    def setup(es):
        c = dict()
        c["cw"] = kb.sb(es, [128, 4, 24], F32)
        c["b_cw"] = P.buf()
        for jj in range(3):
            load_col(P, c["cw"][:, jj, :], W["hy_conv_w"][0, jj], c["b_cw"])
        load_col(P, c["cw"][:, 3, :], W["hy_conv_b"][0], c["b_cw"])
        c["r"] = [(kb.sb(es, [128, 2048], F32), P.buf()) for _ in range(5)]
        c["ub"] = [(kb.sb(es, [128, 2048], BF16), P.buf()) for _ in range(2)]
        c["ut"] = [(kb.sb(es, [128, 16, 128], BF16), P.buf()) for _ in range(2)]
        c["n"] = 0
        return c

    def conv(c, g, b_g, ch, r, b_r, Ls):
        cw = c["cw"]
        P.op("act", lambda e: e.activation(out=r[:, 0:Ls], in_=g[:, 1:Ls + 1], func=AF.Identity,
                                           scale=cw[:, 1, ch:ch + 1], bias=cw[:, 3, ch:ch + 1]), reads=[b_g, c["b_cw"]], writes=[b_r])
        P.op("dve", lambda e: e.scalar_tensor_tensor(out=r[:, 0:Ls], in0=g[:, 0:Ls], scalar=cw[:, 0, ch:ch + 1], in1=r[:, 0:Ls],
                                                     op0=ALU.mult, op1=ALU.add), reads=[b_g, c["b_cw"], b_r], writes=[b_r])
        P.op("dve", lambda e: e.scalar_tensor_tensor(out=r[:, 0:Ls], in0=g[:, 2:Ls + 2], scalar=cw[:, 2, ch:ch + 1], in1=r[:, 0:Ls],
                                                     op0=ALU.mult, op1=ALU.add), reads=[b_g, c["b_cw"], b_r], writes=[b_r])

    def epi(kb, c, gi, rows, t0, Ls):
        c["n"] += 1
        n = c["n"]
        if gi < 8:
            (g, b_g), = rows
            r, b_r = c["r"][0]
            conv(c, g, b_g, gi, r, b_r, Ls)
            P.dma("pool", x0T[gi * 128:(gi + 1) * 128, t0:t0 + Ls], r[:, 0:Ls], reads=[b_r])
            return
        j = gi - 8
        (g1, b_g1), (g2, b_g2) = rows
        r1, b_r1 = c["r"][1 + 2 * (gi % 2)]
        r2, b_r2 = c["r"][2 + 2 * (gi % 2)]
        conv(c, g1, b_g1, 8 + j, r1, b_r1, Ls)
        conv(c, g2, b_g2, 16 + j, r2, b_r2, Ls)
        ub, b_ub = c["ub"][n % 2]
        ut, b_ut = c["ut"][n % 2]
        P.op("pool", lambda e: e.tensor_tensor(out=r1[:, 0:Ls], in0=r1[:, 0:Ls], in1=r2[:, 0:Ls], op=ALU.mult), reads=[b_r1, b_r2], writes=[b_r1])
        P.dma("pool", uT[j * 128:(j + 1) * 128, t0:t0 + Ls], r1[:, 0:Ls], reads=[b_r1])
        P.op("act", lambda e: e.activation(out=ub[:, 0:Ls], in_=r1[:, 0:Ls], func=AF.Copy), reads=[b_r1], writes=[b_ub])
        nbk = Ls // 128
        for q0 in range(0, nbk, 8):
            bt = kb.bank()
            for q in range(q0, min(q0 + 8, nbk)):
                P.op("pe", lambda e: e.transpose(out=kb.psb[bt][:, (q - q0) * 128:(q - q0 + 1) * 128], in_=ub[:, q * 128:(q + 1) * 128],
                                                 identity=kb.identb[:]), reads=[b_ub, kb.b_ident], writes=[kb.pbuf[bt]], signal=(q == min(q0 + 8, nbk) - 1))
            nq = min(q0 + 8, nbk) - q0
            P.op("act", lambda e: e.activation(out=ut[:, q0:q0 + nq, :], in_=kb.psb[bt][:, 0:nq * 128].rearrange("p (q c) -> p q c", c=128),
                                               func=AF.Copy), reads=[kb.pbuf[bt]], writes=[b_ut])
        P.dma("sp", utm[t0:t0 + Ls, j * 128:(j + 1) * 128].rearrange("(q p) c -> p q c", p=128), ut[:, 0:nbk, :], reads=[b_ut])

    phase_lf(kb, hT, 1024, L, groups, epi, halo=1, setup=setup)
    fft_f1(kb, utm[0:L, :], 1024, L, A)
    fft_f2(kb, A, 1024, L, "conv", kb.Hd[s], Bd=Bd)
    fft_f3(kb, Bd, 1024, L, ytm[0:L, :])

    with ExitStack() as es:
        skc = kb.sb(es, [128, 8], F32)
        b_sk = P.buf()
        load_col(P, skc[:], W["hy_skip"][0], b_sk)
        ys = [(kb.sb(es, [128, 4, 1024], F32), P.buf()) for _ in range(2)]
        us = [(kb.sb(es, [128, 8, 512], F32), P.buf()) for _ in range(2)]
        xs = [(kb.sb(es, [128, 8, 512], F32), P.buf()) for _ in range(2)]
        ms = [(kb.sb(es, [128, 8, 512], BF16), P.buf()) for _ in range(2)]
        tmps = [(kb.sb(es, [128, 512], F32), P.buf()) for _ in range(2)]
        for ti in range(L // 512):
            t0 = ti * 512
            y, b_y = ys[ti % 2]
            u, b_u = us[ti % 2]
            x, b_x = xs[ti % 2]
            m, b_m = ms[ti % 2]
            P.dma("sp", y[:], ytm[t0:t0 + 512, :].rearrange("(a p) c -> p a c", p=128), writes=[b_y])
            P.dma("sp", u[:], uT[:, t0:t0 + 512].rearrange("(k p) t -> p k t", p=128), writes=[b_u])
            P.dma("sp", x[:], x0T[:, t0:t0 + 512].rearrange("(k p) t -> p k t", p=128), writes=[b_x])
            for k in range(8):
                bk = kb.bank()
                for a in range(4):
                    P.op("pe", lambda e: e.transpose(out=kb.ps[bk][:, a * 128:(a + 1) * 128], in_=y[:, a, k * 128:(k + 1) * 128], identity=kb.identf[:]),
                         reads=[b_y, kb.b_ident], writes=[kb.pbuf[bk]], signal=(a == 3))
                tmp, b_tmp = tmps[k % 2]
                P.op("dve", lambda e: e.scalar_tensor_tensor(out=tmp[:], in0=u[:, k, :], scalar=skc[:, k:k + 1], in1=kb.ps[bk], op0=ALU.mult, op1=ALU.add),
                     reads=[b_u, b_sk, kb.pbuf[bk]], writes=[b_tmp])
                P.op("pool", lambda e: e.tensor_tensor(out=m[:, k, :], in0=tmp[:], in1=x[:, k, :], op=ALU.mult), reads=[b_tmp, b_x], writes=[b_m])
            P.dma("sp", mT[:, t0:t0 + 512].rearrange("(k p) t -> p k t", p=128), m[:], reads=[b_m])
        P.flush()

    lt_residual(kb, mT[:, 0:L], 1024, L, W["hy_w_out"][0], Xsrc, Xdst, l, 0, s)


MIXERS[0] = mixer_hyena


def kernel(**inputs):
    Ls = (2048, 8192)
    inputs = {k: np.asarray(v) for k, v in inputs.items()}
    kb = build(Ls)
    res = run(kb, inputs)
    y0 = np.stack([np.asarray(res[i]["y0"], dtype=np.float32) for i in range(8)])
    y1 = np.stack([np.asarray(res[i]["y1"], dtype=np.float32) for i in range(8)])
    return (y0, y1)
```
